# Optimizing a Trainium2 kernel written in Bass

```python
import jax, jax.numpy as jnp
from jax import lax
import numpy as np

D_MODEL = 2048
BATCH = 8
SEQ = 2048
DEPTH = 1
DEC_BATCH = 2
DEC_SEQ = 16384
PAST_LEN = 128

D_CONV = D_MODEL // 2
D_SHORT = D_MODEL - D_CONV
CONV_WIDTH = 31
SHORT_WIDTH = 3
D_IN = 2 * D_CONV + 3 * D_SHORT
N_KEYS = 128
N_EXPERTS = N_KEYS * N_KEYS
PEER_HEADS = 8
D_QUERY = 256
D_HALF = D_QUERY // 2
TOPK = 16
PEER_CHUNK = 128
RMS_EPS = 1e-6
LN_EPS = 1e-5

kernel_name = "hybrid_conformer_shortconv_peer_adaln_encoder"


def rmsnorm(x, g):
    x32 = x.astype(jnp.float32)
    y = x32 * lax.rsqrt(jnp.mean(x32 * x32, axis=-1, keepdims=True) + RMS_EPS)
    return (y * g.astype(jnp.float32)).astype(x.dtype)


def layernorm(x, g, b):
    x32 = x.astype(jnp.float32)
    mu = jnp.mean(x32, axis=-1, keepdims=True)
    xc = x32 - mu
    var = jnp.mean(xc * xc, axis=-1, keepdims=True)
    y = xc * lax.rsqrt(var + LN_EPS) * g.astype(jnp.float32) + b.astype(jnp.float32)
    return y.astype(x.dtype)


def depthwise_conv(x, w):
    c = x.shape[-1]
    pad = (w.shape[0] - 1) // 2
    return lax.conv_general_dilated(
        x, w[:, None, :].astype(x.dtype), window_strides=(1,), padding=[(pad, pad)],
        dimension_numbers=('NWC', 'WIO', 'NWC'), feature_group_count=c)


def token_mixer(h, w_in, conv_w, conv_b, ln_g, ln_b, short_w, w_out):
    z = h @ w_in
    a, a_gate, b_gate, c_gate, xs = jnp.split(
        z, [D_CONV, 2 * D_CONV, 2 * D_CONV + D_SHORT, 2 * D_CONV + 2 * D_SHORT], axis=-1)
    u = a * jax.nn.sigmoid(a_gate)
    u = depthwise_conv(u, conv_w) + conv_b
    u = jax.nn.silu(layernorm(u, ln_g, ln_b))
    s = b_gate * depthwise_conv(c_gate * xs, short_w)
    return jnp.concatenate([u, s], axis=-1) @ w_out


def peer(h, w_query, sub_keys, expert_u, expert_v):
    bsz, seq, d = h.shape
    blocks = h.reshape(-1, PEER_CHUNK, d)

    def block_fn(xc):
        t = xc.shape[0]
        q = (xc @ w_query).reshape(t, PEER_HEADS, 2, D_HALF)
        s = jnp.einsum('thpd,hpkd->thpk', q, sub_keys).astype(jnp.float32)
        sv, si = lax.top_k(s, TOPK)
        cand = (sv[:, :, 0, :, None] + sv[:, :, 1, None, :]).reshape(t, PEER_HEADS, TOPK * TOPK)
        cidx = (si[:, :, 0, :, None] * N_KEYS + si[:, :, 1, None, :]).reshape(t, PEER_HEADS, TOPK * TOPK)
        top_s, pos = lax.top_k(cand, TOPK)
        eidx = jnp.take_along_axis(cidx, pos, axis=-1).reshape(t, PEER_HEADS * TOPK)
        gate = jax.nn.softmax(top_s, axis=-1).reshape(t, PEER_HEADS * TOPK)
        u = expert_u[eidx]
        act = jax.nn.gelu(jnp.einsum('ted,td->te', u, xc).astype(jnp.float32), approximate=False)
        coef = (gate * act).astype(xc.dtype)
        return jnp.einsum('te,ted->td', coef, expert_v[eidx])

    return lax.map(block_fn, blocks).reshape(bsz, seq, d)


def trunk(x, c, w_ada, b_ada, g_norm1, w_in, conv_w, conv_b, ln_g, ln_b, short_w, w_out,
          g_norm2, w_query, sub_keys, expert_u, expert_v, g_final):
    for l in range(DEPTH):
        mod = (jax.nn.silu(c) @ w_ada[l] + b_ada[l])[:, None, :]
        sh1, sc1, gt1, sh2, sc2, gt2 = jnp.split(mod, 6, axis=-1)
        h = rmsnorm(x, g_norm1[l]) * (1 + sc1) + sh1
        x = x + gt1 * token_mixer(h, w_in[l], conv_w[l], conv_b[l], ln_g[l], ln_b[l], short_w[l], w_out[l])
        h = rmsnorm(x, g_norm2[l]) * (1 + sc2) + sh2
        x = x + gt2 * peer(h, w_query[l], sub_keys[l], expert_u[l], expert_v[l])
    return rmsnorm(x, g_final)


def setup_inputs(seed: int = 0) -> dict:
    key = jax.random.key(seed)
    ks = jax.random.split(key, 24)
    f = jnp.float32
    n = lambda k, shape, s: jax.random.normal(k, shape, f) * s
    d = D_MODEL
    return {
        "x_prompt": n(ks[0], (BATCH, SEQ, d), 1.0),
        "x_sample": n(ks[1], (DEC_BATCH, DEC_SEQ, d), 1.0),
        "c_prompt": n(ks[2], (BATCH, d), 1.0),
        "c_sample": n(ks[3], (DEC_BATCH, d), 1.0),
        "w_ada": n(ks[4], (DEPTH, d, 6 * d), 0.5 * d ** -0.5),
        "b_ada": n(ks[5], (DEPTH, 6 * d), 0.01),
        "g_norm1": 1.0 + n(ks[6], (DEPTH, d), 0.02),
        "w_in": n(ks[7], (DEPTH, d, D_IN), d ** -0.5),
        "conv_w": n(ks[8], (DEPTH, CONV_WIDTH, D_CONV), CONV_WIDTH ** -0.5),
        "conv_b": n(ks[9], (DEPTH, D_CONV), 0.02),
        "ln_g": 1.0 + n(ks[10], (DEPTH, D_CONV), 0.02),
        "ln_b": n(ks[11], (DEPTH, D_CONV), 0.02),
        "short_w": n(ks[12], (DEPTH, SHORT_WIDTH, D_SHORT), SHORT_WIDTH ** -0.5),
        "w_out": n(ks[13], (DEPTH, d, d), d ** -0.5),
        "g_norm2": 1.0 + n(ks[14], (DEPTH, d), 0.02),
        "w_query": n(ks[15], (DEPTH, d, PEER_HEADS * D_QUERY), d ** -0.5),
        "sub_keys": n(ks[16], (DEPTH, PEER_HEADS, 2, N_KEYS, D_HALF), D_HALF ** -0.5),
        "expert_u": n(ks[17], (DEPTH, N_EXPERTS, d), d ** -0.5),
        "expert_v": n(ks[18], (DEPTH, N_EXPERTS, d), PEER_HEADS ** -0.5),
        "g_final": 1.0 + n(ks[19], (d,), 0.02),
    }


def reference(x_prompt, x_sample, c_prompt, c_sample, w_ada, b_ada, g_norm1, w_in, conv_w, conv_b,
              ln_g, ln_b, short_w, w_out, g_norm2, w_query, sub_keys, expert_u, expert_v, g_final):
    y_prompt = trunk(x_prompt, c_prompt, w_ada, b_ada, g_norm1, w_in, conv_w, conv_b, ln_g, ln_b,
                     short_w, w_out, g_norm2, w_query, sub_keys, expert_u, expert_v, g_final)
    y_sample = trunk(x_sample, c_sample, w_ada, b_ada, g_norm1, w_in, conv_w, conv_b, ln_g, ln_b,
                     short_w, w_out, g_norm2, w_query, sub_keys, expert_u, expert_v, g_final)
    return (y_prompt, y_sample)
```

```python
import contextlib
import numpy as np
import concourse.bass as bass
import concourse.mybir as mybir
from concourse.bass_utils import run_bass_kernel_spmd

F32 = mybir.dt.float32
BF16 = mybir.dt.bfloat16
I32 = mybir.dt.int32
U32 = mybir.dt.uint32
ALU = mybir.AluOpType
AF = mybir.ActivationFunctionType
AX = mybir.AxisListType

D = 2048
KC = 16
DIN = 5120
NEXP = 16384
T = 256
NS = T // 128
HALO = 15
N = T + 2 * HALO
RMS_EPS = 1e-6
LN_EPS = 1e-5
NBLK = 72


class Buf:
    __slots__ = ("name", "writer", "readers")

    def __init__(self, name):
        self.name = name
        self.writer = None
        self.readers = {}


class DSem:
    __slots__ = ("sem", "count")

    def __init__(self, sem):
        self.sem = sem
        self.count = 0


class Op:
    __slots__ = ("eng", "fn", "deps", "needed", "sem", "val", "inc", "dsem")

    def __init__(self, eng, fn):
        self.eng = eng
        self.fn = fn
        self.deps = []
        self.needed = False
        self.sem = None
        self.val = 0
        self.inc = 1
        self.dsem = None


def _flat(x):
    out = []
    for b in x:
        if isinstance(b, (list, tuple)):
            out.extend(_flat(b))
        elif b is not None:
            out.append(b)
    return out


def _writers(b):
    w = b.writer
    if w is None:
        return []
    return w if isinstance(w, list) else [w]


class Prog:
    ENGS = ("sp", "act", "dve", "pe", "pool")

    def __init__(self):
        self.ops = {e: [] for e in self.ENGS}
        self.defer = None

    COST = {"dve": 0.6, "pe": 0.27, "act": 0.45, "sp": 0.0, "pool": 0.0}

    def drain(self, fifo, budget):
        used = {e: 0.0 for e in self.ENGS}
        while fifo:
            eng0 = fifo[0][1]
            if used[eng0] > 0 and used[eng0] + self.COST[eng0] > budget[eng0]:
                break
            kind, eng, fn, dsem, r, w = fifo.pop(0)
            used[eng] += self.COST[eng]
            if kind == "op":
                self._add(Op(eng, fn), r, w)
            else:
                o = Op(eng, fn)
                o.dsem = dsem
                o.needed = True
                o.inc = 16
                self._add(o, r, w)

    def _add(self, op, reads, writes):
        reads = _flat(reads)
        writes = _flat(writes)
        deps = {}
        for b in reads:
            for wr in _writers(b):
                deps[id(wr)] = wr
        for b in writes:
            for wr in _writers(b):
                deps[id(wr)] = wr
            for r in b.readers.values():
                deps[id(r)] = r
        for d in deps.values():
            if d is op:
                continue
            if op.eng == "pe" and d.eng == "pe" and d.dsem is None:
                continue
            op.deps.append(d)
            d.needed = True
        for b in reads:
            key = ("d", id(op.dsem)) if op.dsem is not None else op.eng
            b.readers[key] = op
        for b in writes:
            b.writer = op
            b.readers = {}
        self.ops[op.eng].append(op)
        return op

    def op(self, eng, fn, r=(), w=()):
        if self.defer is not None:
            self.defer.append(("op", eng, fn, None, r, w))
            return None
        return self._add(Op(eng, fn), r, w)

    def dma(self, eng, fn, dsem, r=(), w=()):
        if self.defer is not None:
            self.defer.append(("dma", eng, fn, dsem, r, w))
            return None
        o = Op(eng, fn)
        o.dsem = dsem
        o.needed = True
        o.inc = 16
        return self._add(o, r, w)

    def resolve(self, engsem):
        for e in self.ENGS:
            cnt = 0
            for o in self.ops[e]:
                if o.dsem is not None:
                    o.dsem.count += 16
                    o.sem = o.dsem.sem
                    o.val = o.dsem.count
                elif o.needed:
                    cnt += 1
                    o.sem = engsem[e]
                    o.val = cnt

    def emit(self, eng_name, eng):
        waited = {}
        for o in self.ops[eng_name]:
            need = {}
            for d in o.deps:
                k = id(d.sem)
                if k not in need or need[k][1] < d.val:
                    need[k] = (d.sem, d.val)
            for k, (sem, val) in need.items():
                if waited.get(k, 0) >= val:
                    continue
                eng.wait_ge(sem, val)
                waited[k] = val
            ins = o.fn(eng)
            if o.needed:
                ins.then_inc(o.sem, o.inc)

    def clear(self):
        self.ops = {e: [] for e in self.ENGS}


def build_program(L0, L1, debug=False):
    assert L0 % T == 0 and L1 % T == 0
    nc = bass.Bass("TRN2", target_bir_lowering=False)
    RIN = L0 + L1 + 4 * HALO
    LT = L0 + L1

    def din(name, shape, dt=F32):
        return nc.dram_tensor(name, list(shape), dt, kind="ExternalInput").ap()

    xin = din("xin", [RIN, D])
    cT = din("cT", [128, KC, 2])
    wada_blocks = din("wada_blocks", [96, 128, KC, 128])
    badaT_d = din("badaT", [128, 96])
    g1T_d = din("g1T", [128, KC])
    g2T_d = din("g2T", [128, KC])
    gfin_b = din("gfin_b", [128, D])
    w_blocks = din("w_blocks", [NBLK, 128, KC, 128])
    convw_d = din("convw", [128, 8, 31])
    convb_d = din("convb", [128, 8])
    lng_d = din("lng", [128, 8])
    lnb_d = din("lnb", [128, 8])
    shortw_d = din("shortw", [128, 8, 3])
    keysT_d = din("keysT", [128, 16, 128])
    eu = din("expert_u", [NEXP, D])
    ev = din("expert_v", [NEXP, D])
    ident_d = din("ident", [128, 128])
    iota16_d = din("iota16", [128, 16])
    em_d = din("emask", [128, 4])
    yout = nc.dram_tensor("yout", [LT, D], F32, kind="ExternalOutput").ap()
    wbf = nc.dram_tensor("wbf_scratch", [NBLK, 128, KC * 128], BF16).ap()
    uvbf = nc.dram_tensor("uvbf_scratch", [NEXP, 2 * D], BF16).ap()
    dbg = {}
    if debug:
        for nm, shp, dt in (("dbg_h2", [128, D], F32), ("dbg_eidx", [128, 128], I32),
                            ("dbg_gate", [128, 128], F32), ("dbg_act", [128, 128], F32),
                            ("dbg_x1", [128, D], F32), ("dbg_S", [128, 2048], F32),
                            ("dbg_us", [128, 16 * T], F32), ("dbg_peer", [128, D], F32)):
            dbg[nm] = nc.dram_tensor(nm, shp, dt, kind="ExternalOutput").ap()

    es = contextlib.ExitStack()
    with es:
        def sb(name, shape, dt=F32):
            return es.enter_context(nc.sbuf_tensor(name + "_sb", list(shape), dt))

        def ps(name, shape, dt=F32):
            return es.enter_context(nc.psum_tensor(name + "_ps", list(shape), dt))

        def newsem(name):
            return es.enter_context(nc.semaphore(name))

        engsem = {e: newsem("s_" + e) for e in Prog.ENGS}
        _dsn = [0]

        def dsem():
            _dsn[0] += 1
            return DSem(newsem("d%d" % _dsn[0]))

        P = Prog()

        ident_f = sb("ident_f", [128, 128]);
        ident_b = sb("ident_b", [128, 128], BF16)
        ones_f = sb("ones_f", [128, 128])
        iota16 = sb("iota16", [128, 16])
        em = sb("em", [128, 4])
        featv = sb("featv", [128, KC, 12])
        featmod = sb("featmod", [128, 96, 2])
        badaT = sb("badaT_s", [128, 96])
        g1T = sb("g1T_s", [128, KC])
        g2T = sb("g2T_s", [128, KC])
        cTs = sb("cTs", [128, KC, 2])
        sT = sb("sT", [128, KC, 2])
        g2b = sb("g2b", [128, D]);
        b2b = sb("b2b", [128, D]);
        gt2b = sb("gt2b", [128, D]);
        gfb = sb("gfb", [128, D])
        convw = sb("convw_s", [128, 8, 31]);
        convb = sb("convb_s", [128, 8])
        lng = sb("lng_s", [128, 8]);
        lnb = sb("lnb_s", [128, 8]);
        shortw = sb("shortw_s", [128, 8, 3])
        keysT = sb("keysT_s", [128, 16, 128])

        B_const = Buf("const")
        B_featv = Buf("featv")

        psA = [ps("psA%d" % i, [128, 512]) for i in range(4)]
        B_psA = [Buf("psA%d" % i) for i in range(4)]
        psZ = [ps("psZ%d" % i, [128, 512]) for i in range(2)]
        B_psZ = [Buf("psZ%d" % i) for i in range(2)]
        psZ_, B_psZ_ = psZ, B_psZ
        psX = ps("psX", [128, 512])
        B_psX = Buf("psX")
        psS = ps("psS", [128, 512])
        B_psS = Buf("psS")
        psXr = [psX, psS]
        B_psXr = [B_psX, B_psS]
        psTvr = [psXr[i][:, 0:256].bitcast(BF16).rearrange("p (j t) -> p j t", j=4) for i in range(2)]
        xrc = [0]

        xt = sb("xt", [128, NS, D])
        B_xt = [Buf("xt%d" % s) for s in range(NS)]
        xfin = sb("xfin", [128, D])
        B_xfin = Buf("xfin")
        h2b = sb("h2b", [128, 2, NS, D], BF16)
        B_h2b = [[Buf("h2b_%d_%d" % (a, s)) for s in range(NS)] for a in range(2)]
        sqa = sb("sqa", [128, D], BF16)
        B_sqa = Buf("sqa")
        stt = sb("stt", [128, 16])
        B_stt = Buf("stt")
        B_stt2 = Buf("stt2")
        hT = sb("hT", [128, KC, N], BF16)
        B_hT = Buf("hT")
        h2T = hT
        B_h2T = B_hT
        wst = sb("wst", [128, 2, KC * 128])
        B_wst = [Buf("wst0"), Buf("wst1")]
        d_wst = [dsem(), dsem()]
        NWB = 3
        wb = sb("wb", [128, NWB, KC, 128], BF16)
        B_wb = [Buf("wb%d" % i) for i in range(NWB)]
        d_wb = [dsem() for _ in range(NWB)]
        B_wbf = Buf("wbf")
        B_uvbf = Buf("uvbf")
        sg = sb("sg", [128, N]);
        B_sg = Buf("sg")
        u2 = sb("u", [128, 2, N], BF16);
        B_u2 = [Buf("u0"), Buf("u1")]
        NDC = 4
        dgc = sb("dgc", [128, NDC, 128], BF16)
        B_dgc = [Buf("dgc%d" % i) for i in range(NDC)]
        dgn = [0]
        xs_sb = sb("xs_sb", [128, N]);
        B_xs = Buf("xs")
        vv = sb("vv", [128, N]);
        B_vv = Buf("vv")
        acc = sb("acc", [128, T]);
        B_acc = Buf("acc")
        poolA = sb("poolA", [128, 4096])
        B_pA = [Buf("pA%d" % i) for i in range(16)]
        ucall = poolA[:, 0:2048].rearrange("p (c t) -> p c t", c=8)
        cat = poolA[:, 2048:2560].rearrange("p (a t) -> p a t", a=2)
        B_cat = [B_pA[8], B_pA[9]]
        lnm = poolA[:, 2560:2816];
        B_lnm = B_pA[10]
        lnr = poolA[:, 2816:3072];
        B_lnr = B_pA[11]
        tmp1 = poolA[:, 3072:3328];
        B_tmp1 = B_pA[12]
        tmp2 = poolA[:, 3328:3584];
        B_tmp2 = B_pA[13]
        Sall = poolA[:].rearrange("p (s q k) -> p s q k", s=2, q=16)

        def B_S(s, q):
            return B_pA[(s * 16 + q) // 2]

        usT = sb("usT", [128, KC, T], BF16)
        B_usT = [Buf("usT%d" % i) for i in range(KC)]
        mTs = sb("mTs", [128, 2, T]);
        B_mTs = [Buf("mTs0"), Buf("mTs1")]
        work = sb("work", [128, 256]);
        B_work = Buf("work")
        tv = sb("tv", [128, 16, 16]);
        B_tv = Buf("tv")
        ti_ = sb("ti", [128, 16, 16], U32);
        B_ti = Buf("ti")
        tif = sb("tif", [128, 16, 16]);
        B_tif = Buf("tif")
        xh = wst[:, 0, :]
        B_xh = B_wst[0]
        cand = wst[:, 0, :].rearrange("p (h c) -> p h c", h=8)
        B_cand = B_wst[0]
        oh = wst[:, 1, :].rearrange("p (h a b) -> p h a b", h=8, a=16)
        B_oh = B_wst[1]
        h2f = wst[:, 1, :]
        B_h2f = B_wst[1]
        cv = sb("cv", [128, 8, 16]);
        B_cv = Buf("cv")
        cpos = sb("cpos", [128, 8, 16], U32);
        B_cpos = Buf("cpos")
        cab = sb("cab", [128, 2, 128], U32);
        B_cab = Buf("cab")
        cabf = sb("cabf", [128, 2, 128]);
        B_cabf = Buf("cabf")
        isel = sb("isel", [128, 2, 128]);
        B_isel = Buf("isel")
        eidxf = sb("eidxf", [128, 128]);
        B_eidxf = Buf("eidxf")
        RJ = 4
        eidx = sb("eidx", [128, RJ, 128], I32);
        B_eidx = [Buf("eidx%d" % i) for i in range(RJ)]
        gate = sb("gate", [128, RJ, 128]);
        B_gate = [Buf("gate%d" % i) for i in range(RJ)]
        gsum = sb("gsum", [128, 8]);
        B_gsum = Buf("gsum")
        actv = sb("actv", [128, 128]);
        gel = sb("gel", [128, 128]);
        cgt = sb("cgt", [128, 128]);
        NCT = 4
        B_actc = [Buf("actc%d" % i) for i in range(NCT)]
        B_gelc = [Buf("gelc%d" % i) for i in range(NCT)]
        B_cgc = [Buf("cgc%d" % i) for i in range(NCT)]
        NG = 4
        UV = [sb("UV%d" % i, [128, 2 * D], BF16) for i in range(NG)]
        B_UVu = [Buf("UVu%d" % i) for i in range(NG)]
        B_UVv = [Buf("UVv%d" % i) for i in range(NG)]
        d_UV = [dsem() for _ in range(NG)]
        d_stg = [dsem() for _ in range(2 * NG)]
        diag = sb("diag", [128, 2, 128], BF16)
        B_diag = [Buf("diag0"), Buf("diag1")]
        tmpo = sb("tmpo", [128, 512]);
        B_tmpo = Buf("tmpo")
        d_x = [dsem() for _ in range(NS)]
        d_xh = dsem()
        d_st = [dsem() for _ in range(NS)]
        d_rl = dsem()
        d_y = dsem()
        B_ydram = [Buf("ydram%d" % i) for i in range(RJ)]

        def fv(seg, kk, kc):
            return featv[:, kc, seg * 6 + kk:seg * 6 + kk + 1]

        def rstd_from_ss(col, Bst, nparts=128):
            P.op("act", lambda e: e.activation(out=stt[0:nparts, col:col + 1], in_=stt[0:nparts, col:col + 1],
                                               func=AF.Sqrt, bias=RMS_EPS, scale=1.0 / D),
                 r=[Bst], w=[Bst])
            P.op("dve", lambda e: e.reciprocal(out=stt[0:nparts, col:col + 1], in_=stt[0:nparts, col:col + 1]),
                 r=[Bst], w=[Bst])

        def bcast_tiles(seg, which, banks=None, Bbanks=None):
            psZ, B_psZ = (banks, Bbanks) if banks is not None else (psZ_, B_psZ_)
            for (dst, kk) in which:
                for kc in range(KC):
                    mi = kc % 2
                    P.op("dve", lambda e, mi=mi, kc=kc, kk=kk, seg=seg: e.tensor_scalar(
                        out=mTs[:, mi, 0:128], in0=ident_f[:], scalar1=fv(seg, kk, kc), scalar2=None,
                        op0=ALU.mult), r=[B_const, B_featv], w=[B_mTs[mi]])
                    n = (kc // 4) % 2
                    P.op("pe", lambda e, mi=mi, kc=kc, n=n: e.matmul(
                        psZ[n][:, (kc % 4) * 128:(kc % 4 + 1) * 128], lhsT=ones_f[:], rhs=mTs[:, mi, 0:128],
                        start=True, stop=True), r=[B_mTs[mi], B_const], w=[B_psZ[n]])
                    if kc % 4 == 3:
                        c4 = kc // 4
                        P.op("dve", lambda e, dst=dst, n=n, c4=c4: e.tensor_copy(
                            out=dst[:, c4 * 512:(c4 + 1) * 512], in_=psZ[n][:, :]), r=[B_psZ[n]], w=[B_bc[id(dst)]])

        B_bc = {id(b2b): Buf("b2b"), id(g2b): Buf("g2b"), id(gt2b): Buf("gt2b")}

        d_misc = dsem()
        B_misc = Buf("misc")
        loads = [(ident_f[:], ident_d), (iota16[:], iota16_d), (em[:], em_d),
                 (gfb[:], gfin_b), (convw[:], convw_d), (convb[:], convb_d), (lng[:], lng_d),
                 (lnb[:], lnb_d), (shortw[:], shortw_d), (keysT[:], keysT_d), (cTs[:], cT),
                 (badaT[:], badaT_d), (g1T[:], g1T_d), (g2T[:], g2T_d)]
        for (dst, src) in loads:
            P.dma("sp", lambda e, dst=dst, src=src: e.dma_start(out=dst, in_=src), d_misc, w=[B_misc])
        P.op("dve", lambda e: e.tensor_copy(out=ident_b[:], in_=ident_f[:]), r=[B_misc], w=[B_const])
        P.op("dve", lambda e: e.memset(ones_f[:], 1.0), w=[B_const])
        B_sT = Buf("sT")
        P.op("act", lambda e: e.activation(out=sT[:], in_=cTs[:], func=AF.Silu), r=[B_misc], w=[B_sT])
        for oc in range(96):
            i = oc % 2
            P.dma("sp", lambda e, i=i, oc=oc: e.dma_start(
                out=wst[:, i, :], in_=wada_blocks[oc].rearrange("p k c -> p (k c)")), d_wst[i], w=[B_wst[i]])
            for kc in range(KC):
                P.op("pe", lambda e, i=i, oc=oc, kc=kc: e.matmul(
                    psS[:, oc * 2:(oc + 1) * 2], lhsT=wst[:, i, kc * 128:(kc + 1) * 128], rhs=sT[:, kc, :],
                    start=(kc == 0), stop=(kc == KC - 1)), r=[B_wst[i], B_sT], w=[B_psS])
        B_fm = Buf("featmod")
        P.op("dve", lambda e: e.tensor_tensor(
            out=featmod[:], in0=psS[:, 0:192].rearrange("p (o r) -> p o r", r=2),
            in1=badaT[:].unsqueeze(2).to_broadcast([128, 96, 2]), op=ALU.add), r=[B_psS, B_misc], w=[B_fm])
        for seg_ in range(2):
            for kk in range(6):
                src = featmod[:, kk * 16:(kk + 1) * 16, seg_]
                dst = featv[:, :, seg_ * 6 + kk]
                if kk in (1, 4):
                    gT = g1T if kk == 1 else g2T
                    P.op("dve", lambda e, src=src, dst=dst, gT=gT: e.scalar_tensor_tensor(
                        out=dst, in0=src, scalar=1.0, in1=gT[:], op0=ALU.add, op1=ALU.mult),
                         r=[B_fm, B_misc], w=[B_featv])
                else:
                    P.op("dve", lambda e, src=src, dst=dst: e.tensor_copy(out=dst, in_=src), r=[B_fm], w=[B_featv])

        stage_bf = ([(UV[i][:, 0:D], B_UVu[i], d_stg[i]) for i in range(NG)]
                    + [(UV[i][:, D:2 * D], B_UVv[i], d_stg[NG + i]) for i in range(NG)])
        cnt_ = [0]
        last_store = {}

        def convert(src_ap, dst_ap, Bdst):
            k = cnt_[0]
            cnt_[0] += 1
            i = k % 2
            bt, Bbt, dbt = stage_bf[k % len(stage_bf)]
            P.dma("sp", lambda e, i=i, src_ap=src_ap: e.dma_start(out=wst[:, i, :], in_=src_ap), d_wst[i],
                  w=[B_wst[i]])
            if k % 2 == 0:
                P.op("act", lambda e, i=i, bt=bt: e.activation(out=bt, in_=wst[:, i, :], func=AF.Copy),
                     r=[B_wst[i]], w=[Bbt])
            else:
                P.op("dve", lambda e, i=i, bt=bt: e.tensor_copy(out=bt, in_=wst[:, i, :]), r=[B_wst[i]], w=[Bbt])
            o = P.dma("pool", lambda e, bt=bt, dst_ap=dst_ap: e.dma_start(out=dst_ap, in_=bt), dbt,
                      r=[Bbt])
            last_store.setdefault(id(Bdst), {})[id(dbt)] = o

        for blk in range(NBLK):
            convert(w_blocks[blk].rearrange("p k c -> p (k c)"), wbf[blk], B_wbf)
        for r_ in range(NEXP // 128):
            convert(eu[r_ * 128:(r_ + 1) * 128, :], uvbf[r_ * 128:(r_ + 1) * 128, 0:D], B_uvbf)
            convert(ev[r_ * 128:(r_ + 1) * 128, :], uvbf[r_ * 128:(r_ + 1) * 128, D:2 * D], B_uvbf)
        for Bd in (B_wbf, B_uvbf):
            Bd.writer = list(last_store[id(Bd)].values())
        print("[kernel] sbuf bytes remaining per partition:", nc.sbuf_bytes_remaining)

        tiles = []
        for (seg, xbase, ybase, ntiles) in ((0, HALO, 0, L0 // T), (1, L0 + 3 * HALO, L0, L1 // T)):
            for it in range(ntiles):
                tiles.append((seg, it, ntiles, xbase + it * T, ybase + it * T))
        NT = len(tiles)
        NJ = NT * NS

        wload = [0]

        def prefetch_to(g):
            lim = min(g, NT * NBLK - 1)
            while wload[0] <= lim:
                k = wload[0]
                wload[0] += 1
                i = k % NWB
                blk = k % NBLK
                P.dma("sp", lambda e, i=i, blk=blk: e.dma_start(
                    out=wb[:, i].rearrange("p k c -> p (k c)"), in_=wbf[blk]), d_wb[i], r=[B_wbf], w=[B_wb[i]])

        zr = [0]

        def zmm(g, ncols, rhsT, Brhs):
            prefetch_to(g + 2)
            i = g % NWB
            wv, Bw = wb[:, i], B_wb[i]
            bi = zr[0] % 2
            zr[0] += 1
            for kc in range(KC):
                P.op("pe", lambda e, bi=bi, kc=kc, wv=wv, ncols=ncols, rhsT=rhsT: e.matmul(
                    psZ[bi][:, 0:ncols], lhsT=wv[:, kc, :], rhs=rhsT[:, kc, 0:ncols],
                    start=(kc == 0), stop=(kc == KC - 1)), r=[Bw, Brhs], w=[B_psZ[bi]])
            return psZ[bi], B_psZ[bi]

        def transposes_bf(src, npart, dsts, Bsrc, Bdst, scale_bias=None):
            for g4 in range(4):
                xi = xrc[0] % 2
                xrc[0] += 1
                psTv = psTvr[xi]
                B_psX = B_psXr[xi]
                for j in range(4):
                    kc = g4 * 4 + j
                    P.op("pe", lambda e, kc=kc, j=j, psTv=psTv: e.transpose(
                        psTv[:, j, 0:npart], src[0:npart, kc * 128:(kc + 1) * 128], ident_b[0:npart, 0:npart]),
                         r=[Bsrc, B_const], w=[B_psX])
                for j in range(4):
                    kc = g4 * 4 + j
                    for (dfn, p0, w_) in dsts:
                        dst = dfn(kc)
                        if scale_bias is not None:
                            sc, bi_ = scale_bias(kc)
                            if j % 2 == 0:
                                P.op("dve", lambda e, dst=dst, j=j, p0=p0, w_=w_, sc=sc, bi_=bi_, psTv=psTv: e.tensor_scalar(
                                    out=dst, in0=psTv[:, j, p0:p0 + w_], scalar1=sc, scalar2=bi_,
                                    op0=ALU.mult, op1=ALU.add), r=[B_psX, B_featv], w=[Bdst])
                            else:
                                P.op("act", lambda e, dst=dst, j=j, p0=p0, w_=w_, sc=sc, bi_=bi_, psTv=psTv: e.activation(
                                    out=dst, in_=psTv[:, j, p0:p0 + w_], func=AF.Identity, scale=sc, bias=bi_),
                                     r=[B_psX, B_featv], w=[Bdst])
                        else:
                            if j % 2 == 0:
                                P.op("dve", lambda e, dst=dst, j=j, p0=p0, w_=w_, psTv=psTv: e.tensor_copy(
                                    out=dst, in_=psTv[:, j, p0:p0 + w_]), r=[B_psX], w=[Bdst])
                            else:
                                P.op("act", lambda e, dst=dst, j=j, p0=p0, w_=w_, psTv=psTv: e.activation(
                                    out=dst, in_=psTv[:, j, p0:p0 + w_], func=AF.Copy), r=[B_psX], w=[Bdst])

        def M_units(ti):
            seg, it, ntiles, r0, y0 = tiles[ti]
            edgeL = (it == 0)
            edgeR = (it == ntiles - 1)
            hp_ = ti % 2
            g0 = ti * NBLK
            units = []

            def add(cost, fn):
                units.append((cost, fn))

            if it == 0:
                add(6, lambda: bcast_tiles(seg, ((b2b, 3), (g2b, 4))))

            def st_1a():
                for s in range(NS):
                    P.dma("sp", lambda e, s=s: e.dma_start(
                        out=xt[:, s, :], in_=xin[r0 + 128 * s:r0 + 128 * (s + 1), :]), d_x[s], w=[B_xt[s]])
                P.dma("sp", lambda e: e.dma_start(out=xh[0:HALO, :], in_=xin[r0 - HALO:r0, :]), d_xh, w=[B_xh])
                P.dma("sp", lambda e: e.dma_start(out=xh[HALO:2 * HALO, :], in_=xin[r0 + T:r0 + T + HALO, :]),
                      d_xh, w=[B_xh])
                prefetch_to(g0 + 1)

            add(1, st_1a)

            def st_1b(pt):
                if pt < NS:
                    src, Bs, npart = xt[:, pt, :], B_xt[pt], 128
                else:
                    src, Bs, npart = xh, B_xh, 2 * HALO
                xn, B_xn = h2b[:, hp_, pt % 2, :], B_h2b[hp_][pt % 2]
                P.op("act", lambda e: e.activation(
                    out=sqa[0:npart, :], in_=src[0:npart, :], func=AF.Square,
                    accum_out=stt[0:npart, pt:pt + 1]), r=[Bs], w=[B_sqa, B_stt])
                rstd_from_ss(pt, B_stt, npart)
                P.op("act", lambda e: e.activation(
                    out=xn[0:npart, :], in_=src[0:npart, :], func=AF.Copy,
                    scale=stt[0:npart, pt:pt + 1]), r=[Bs, B_stt], w=[B_xn])
                if pt < NS:
                    dsts = [(lambda kc: hT[:, kc, HALO + 128 * pt:HALO + 128 * (pt + 1)], 0, 128)]
                else:
                    dsts = [(lambda kc: hT[:, kc, 0:HALO], 0, HALO),
                            (lambda kc: hT[:, kc, HALO + T:N], HALO, HALO)]
                transposes_bf(xn, npart, dsts, B_xn, B_hT, scale_bias=lambda kc: (fv(seg, 1, kc), fv(seg, 0, kc)))

            for pt in range(NS + 1):
                add(8, lambda pt=pt: st_1b(pt))

            def conf_chunk(c):
                za, Bza = zmm(g0 + 2 * c, N, hT, B_hT)
                zg, Bzg = zmm(g0 + 2 * c + 1, N, hT, B_hT)
                ui = c % 2
                u, B_u = u2[:, ui, :], B_u2[ui]
                P.op("act", lambda e: e.activation(out=sg[:], in_=zg[:, 0:N], func=AF.Sigmoid), r=[Bzg], w=[B_sg])
                P.op("dve", lambda e: e.tensor_tensor(out=u, in0=za[:, 0:N], in1=sg[:], op=ALU.mult),
                     r=[Bza, B_sg], w=[B_u])
                if edgeL:
                    P.op("dve", lambda e: e.tensor_scalar(
                        out=u[:, 0:HALO], in0=u[:, 0:HALO], scalar1=em[:, 2 * seg:2 * seg + 1], scalar2=None,
                        op0=ALU.mult), r=[B_u, B_const], w=[B_u])
                if edgeR:
                    P.op("dve", lambda e: e.tensor_scalar(
                        out=u[:, HALO + T:N], in0=u[:, HALO + T:N], scalar1=em[:, 2 * seg + 1:2 * seg + 2],
                        scalar2=None, op0=ALU.mult), r=[B_u, B_const], w=[B_u])
                cb = zr[0] % 2
                zr[0] += 1
                for k in range(31):
                    di = dgn[0] % NDC
                    dgn[0] += 1
                    P.op("act", lambda e, k=k, di=di: e.activation(
                        out=dgc[:, di, :], in_=ident_f[:], func=AF.Copy, scale=convw[:, c, k:k + 1]),
                         r=[B_const], w=[B_dgc[di]])
                    P.op("pe", lambda e, k=k, di=di: e.matmul(
                        psZ[cb][:, 0:T], lhsT=dgc[:, di, :], rhs=u[:, k:k + T], start=(k == 0), stop=(k == 30)),
                         r=[B_dgc[di], B_u], w=[B_psZ[cb]])
                P.op("act", lambda e: e.activation(out=cat[:, 0, :], in_=psZ[cb][:, 0:T], func=AF.Identity,
                                                   bias=convb[:, c:c + 1], scale=1.0),
                     r=[B_psZ[cb], B_const], w=[B_cat[0]])
                P.op("act", lambda e: e.activation(out=cat[:, 1, :], in_=psZ[cb][:, 0:T], func=AF.Square,
                                                   bias=convb[:, c:c + 1], scale=1.0),
                     r=[B_psZ[cb], B_const], w=[B_cat[1]])
                P.op("dve", lambda e: e.tensor_copy(out=ucall[:, c, :], in_=cat[:, 0, :]),
                     r=[B_cat[0]], w=[B_pA[c]])
                P.op("pe", lambda e: e.matmul(
                    psS[:, :], lhsT=ones_f[:], rhs=cat[:].rearrange("p a t -> p (a t)"),
                    start=(c == 0), stop=(c == 7)), r=[B_cat, B_const], w=[B_psS])

            for c in range(8):
                add(25, lambda c=c: conf_chunk(c))

            def ln_stats():
                P.op("dve", lambda e: e.tensor_scalar(out=lnm, in0=psS[:, 0:T], scalar1=1.0 / 1024, scalar2=None,
                                                      op0=ALU.mult), r=[B_psS], w=[B_lnm])
                P.op("dve", lambda e: e.tensor_tensor(out=tmp1, in0=lnm, in1=lnm, op=ALU.mult),
                     r=[B_lnm], w=[B_tmp1])
                P.op("dve", lambda e: e.scalar_tensor_tensor(out=tmp2, in0=psS[:, T:2 * T], scalar=1.0 / 1024,
                                                             in1=tmp1, op0=ALU.mult, op1=ALU.subtract),
                     r=[B_psS, B_tmp1], w=[B_tmp2])
                P.op("act", lambda e: e.activation(out=tmp2, in_=tmp2, func=AF.Sqrt, bias=LN_EPS, scale=1.0),
                     r=[B_tmp2], w=[B_tmp2])
                P.op("dve", lambda e: e.reciprocal(out=lnr, in_=tmp2), r=[B_tmp2], w=[B_lnr])

            add(4, ln_stats)

            def ln_apply(c):
                P.op("dve", lambda e: e.tensor_tensor(out=tmp1, in0=ucall[:, c, :], in1=lnm, op=ALU.subtract),
                     r=[B_pA[c], B_lnm], w=[B_tmp1])
                P.op("dve", lambda e: e.tensor_tensor(out=tmp1, in0=tmp1, in1=lnr, op=ALU.mult),
                     r=[B_tmp1, B_lnr], w=[B_tmp1])
                P.op("act", lambda e: e.activation(out=usT[:, c, :], in_=tmp1, func=AF.Silu,
                                                   scale=lng[:, c:c + 1], bias=lnb[:, c:c + 1]),
                     r=[B_tmp1, B_const], w=[B_usT[c]])

            for c in range(8):
                add(2, lambda c=c: ln_apply(c))

            def short_chunk(c):
                gb = g0 + 16 + 3 * c
                zx, Bzx = zmm(gb, N, hT, B_hT)
                zc, Bzc = zmm(gb + 1, N, hT, B_hT)
                P.op("act", lambda e: e.activation(out=xs_sb[:], in_=zx[:, 0:N], func=AF.Copy), r=[Bzx], w=[B_xs])
                P.op("dve", lambda e: e.tensor_tensor(out=vv[:], in0=zc[:, 0:N], in1=xs_sb[:], op=ALU.mult),
                     r=[Bzc, B_xs], w=[B_vv])
                zb, Bzb = zmm(gb + 2, N, hT, B_hT)
                if edgeL:
                    P.op("dve", lambda e: e.tensor_scalar(
                        out=vv[:, 0:HALO], in0=vv[:, 0:HALO], scalar1=em[:, 2 * seg:2 * seg + 1], scalar2=None,
                        op0=ALU.mult), r=[B_vv, B_const], w=[B_vv])
                if edgeR:
                    P.op("dve", lambda e: e.tensor_scalar(
                        out=vv[:, HALO + T:N], in0=vv[:, HALO + T:N], scalar1=em[:, 2 * seg + 1:2 * seg + 2],
                        scalar2=None, op0=ALU.mult), r=[B_vv, B_const], w=[B_vv])
                P.op("dve", lambda e: e.tensor_scalar(
                    out=acc[:], in0=vv[:, HALO - 1:HALO - 1 + T], scalar1=shortw[:, c, 0:1], scalar2=None,
                    op0=ALU.mult), r=[B_vv, B_const], w=[B_acc])
                for k in (1, 2):
                    P.op("dve", lambda e, k=k: e.scalar_tensor_tensor(
                        out=acc[:], in0=vv[:, HALO - 1 + k:HALO - 1 + k + T], scalar=shortw[:, c, k:k + 1],
                        in1=acc[:], op0=ALU.mult, op1=ALU.add), r=[B_vv, B_acc, B_const], w=[B_acc])
                P.op("dve", lambda e: e.tensor_tensor(
                    out=usT[:, 8 + c, :], in0=zb[:, HALO:HALO + T], in1=acc[:], op=ALU.mult),
                     r=[Bzb, B_acc], w=[B_usT[8 + c]])

            for c in range(8):
                add(12, lambda c=c: short_chunk(c))

            def wout_chunk(dc):
                mp, Bmp = zmm(g0 + 40 + dc, T, usT, B_usT)
                mi = dc % 2
                psX, B_psX = psXr[dc % 2], B_psXr[dc % 2]
                P.op("act", lambda e: e.activation(
                    out=mTs[:, mi, :], in_=mp[:, 0:T], func=AF.Copy, scale=fv(seg, 2, dc)),
                     r=[Bmp, B_featv], w=[B_mTs[mi]])
                for s in range(NS):
                    P.op("pe", lambda e, s=s: e.transpose(
                        psX[:, s * 128:(s + 1) * 128], mTs[:, mi, s * 128:(s + 1) * 128], ident_f[:]),
                         r=[B_mTs[mi], B_const], w=[B_psX])
                for s in range(NS):
                    P.op("dve", lambda e, s=s: e.tensor_tensor(
                        out=xt[:, s, dc * 128:(dc + 1) * 128], in0=xt[:, s, dc * 128:(dc + 1) * 128],
                        in1=psX[:, s * 128:(s + 1) * 128], op=ALU.add),
                         r=[B_xt[s], B_psX], w=[B_xt[s]])

            for dc in range(KC):
                add(4, lambda dc=dc: wout_chunk(dc))

            def st_3(s):
                j = ti * NS + s
                P.dma("sp", lambda e: e.dma_start(out=yout[y0 + 128 * s:y0 + 128 * (s + 1), :], in_=xt[:, s, :]),
                      d_st[s], r=[B_xt[s]], w=[B_ydram[j % RJ]])
                P.op("act", lambda e: e.activation(
                    out=sqa[:], in_=xt[:, s, :], func=AF.Square, accum_out=stt[:, 4 + s:5 + s]),
                     r=[B_xt[s]], w=[B_sqa, B_stt])
                rstd_from_ss(4 + s, B_stt)
                P.op("dve", lambda e: e.scalar_tensor_tensor(
                    out=h2f, in0=xt[:, s, :], scalar=stt[:, 4 + s:5 + s], in1=g2b[:],
                    op0=ALU.mult, op1=ALU.mult), r=[B_xt[s], B_stt, B_bc[id(g2b)]], w=[B_h2f])
                P.op("dve", lambda e: e.tensor_tensor(
                    out=h2b[:, hp_, s, :], in0=h2f, in1=b2b[:], op=ALU.add),
                     r=[B_h2f, B_bc[id(b2b)]], w=[B_h2b[hp_][s]])
                dsts = [(lambda kc: h2T[:, kc, 128 * s:128 * (s + 1)], 0, 128)]
                transposes_bf(h2b[:, hp_, s, :], 128, dsts, B_h2b[hp_][s], B_h2T)

            for s in range(NS):
                add(12, lambda s=s: st_3(s))

            def q_chunk(qc):
                qp, Bqp = zmm(g0 + 56 + qc, T, h2T, B_h2T)
                mi = qc % 2
                psX, B_psX = psXr[qc % 2], B_psXr[qc % 2]
                P.op("act", lambda e: e.activation(out=mTs[:, mi, :], in_=qp[:, 0:T], func=AF.Copy),
                     r=[Bqp], w=[B_mTs[mi]])
                for s in range(NS):
                    P.op("pe", lambda e, s=s: e.matmul(
                        psX[:, s * 128:(s + 1) * 128], lhsT=mTs[:, mi, s * 128:(s + 1) * 128],
                        rhs=keysT[:, qc, :], start=True, stop=True),
                         r=[B_mTs[mi], B_const], w=[B_psX])
                if qc % 2 == 0:
                    P.op("dve", lambda e: e.tensor_copy(
                        out=Sall[:, :, qc, :], in_=psX[:, 0:256].rearrange("p (s k) -> p s k", s=2)),
                         r=[B_psX], w=[B_S(0, qc), B_S(1, qc)])
                else:
                    P.op("act", lambda e: e.activation(
                        out=Sall[:, :, qc, :], in_=psX[:, 0:256].rearrange("p (s k) -> p s k", s=2), func=AF.Copy),
                         r=[B_psX], w=[B_S(0, qc), B_S(1, qc)])

            for qc in range(16):
                add(4, lambda qc=qc: q_chunk(qc))

            def retr_topk(s, hps):
                B_Ss = [B_S(s, q) for q in range(16)]
                for hp in hps:
                    Sv = Sall[:, s, hp, :]
                    P.op("dve", lambda e, hp=hp, Sv=Sv: e.max(out=tv[:, hp, 0:8], in_=Sv), r=[B_Ss], w=[B_tv])
                    P.op("dve", lambda e, hp=hp, Sv=Sv: e.match_replace(
                        out=work[:, 0:128], in_to_replace=tv[:, hp, 0:8], in_values=Sv, imm_value=-1e30),
                         r=[B_Ss, B_tv], w=[B_work])
                    P.op("dve", lambda e, hp=hp: e.max(out=tv[:, hp, 8:16], in_=work[:, 0:128]),
                         r=[B_work], w=[B_tv])
                    P.op("dve", lambda e, hp=hp, Sv=Sv: e.max_index(
                        out=ti_[:, hp, 0:8], in_max=tv[:, hp, 0:8], in_values=Sv), r=[B_Ss, B_tv], w=[B_ti])
                    P.op("dve", lambda e, hp=hp, Sv=Sv: e.max_index(
                        out=ti_[:, hp, 8:16], in_max=tv[:, hp, 8:16], in_values=Sv), r=[B_Ss, B_tv], w=[B_ti])

            def retr_cand(s):
                P.op("dve", lambda e: e.tensor_copy(out=tif[:], in_=ti_[:]), r=[B_ti], w=[B_tif])
                tv4 = tv[:].rearrange("p (h t) k -> p h t k", t=2)
                P.op("dve", lambda e: e.tensor_tensor(
                    out=cand.rearrange("p h (a b) -> p h a b", a=16),
                    in0=tv4[:, :, 0, :].unsqueeze(3).to_broadcast([128, 8, 16, 16]),
                    in1=tv4[:, :, 1, :].unsqueeze(2).to_broadcast([128, 8, 16, 16]), op=ALU.add),
                     r=[B_tv], w=[B_cand])
                for h in range(8):
                    P.op("dve", lambda e, h=h: e.max(out=cv[:, h, 0:8], in_=cand[:, h, :]), r=[B_cand], w=[B_cv])
                    P.op("dve", lambda e, h=h: e.match_replace(
                        out=work[:, :], in_to_replace=cv[:, h, 0:8], in_values=cand[:, h, :], imm_value=-1e30),
                         r=[B_cand, B_cv], w=[B_work])
                    P.op("dve", lambda e, h=h: e.max(out=cv[:, h, 8:16], in_=work[:, :]), r=[B_work], w=[B_cv])
                    P.op("dve", lambda e, h=h: e.max_index(
                        out=cpos[:, h, 0:8], in_max=cv[:, h, 0:8], in_values=cand[:, h, :]),
                         r=[B_cand, B_cv], w=[B_cpos])
                    P.op("dve", lambda e, h=h: e.max_index(
                        out=cpos[:, h, 8:16], in_max=cv[:, h, 8:16], in_values=cand[:, h, :]),
                         r=[B_cand, B_cv], w=[B_cpos])

            def retr_idx(s):
                j = ti * NS + s
                jr = j % RJ
                cposf = cpos[:].rearrange("p h k -> p (h k)")
                P.op("dve", lambda e: e.tensor_single_scalar(out=cab[:, 0, :], in_=cposf, scalar=4,
                                                             op=ALU.logical_shift_right), r=[B_cpos], w=[B_cab])
                P.op("dve", lambda e: e.tensor_single_scalar(out=cab[:, 1, :], in_=cposf, scalar=15,
                                                             op=ALU.bitwise_and), r=[B_cpos], w=[B_cab])
                P.op("dve", lambda e: e.tensor_copy(out=cabf[:], in_=cab[:]), r=[B_cab], w=[B_cabf])
                tif4 = tif[:].rearrange("p (h t) k -> p h t k", t=2)
                for t_ in range(2):
                    P.op("dve", lambda e, t_=t_: e.tensor_tensor(
                        out=oh,
                        in0=cabf[:, t_, :].rearrange("p (h k) -> p h k", h=8).unsqueeze(3).to_broadcast(
                            [128, 8, 16, 16]),
                        in1=iota16[:].unsqueeze(1).unsqueeze(1).to_broadcast([128, 8, 16, 16]),
                        op=ALU.is_equal), r=[B_cabf, B_const], w=[B_oh])
                    P.op("dve", lambda e, t_=t_: e.tensor_tensor(
                        out=oh, in0=oh, in1=tif4[:, :, t_, :].unsqueeze(2).to_broadcast([128, 8, 16, 16]),
                        op=ALU.mult), r=[B_oh, B_tif], w=[B_oh])
                    P.op("dve", lambda e, t_=t_: e.tensor_reduce(
                        out=isel[:, t_, :].rearrange("p (h k) -> p h k", h=8), in_=oh, axis=AX.X, op=ALU.add),
                         r=[B_oh], w=[B_isel])
                P.op("dve", lambda e: e.scalar_tensor_tensor(
                    out=eidxf[:], in0=isel[:, 0, :], scalar=128.0, in1=isel[:, 1, :], op0=ALU.mult, op1=ALU.add),
                     r=[B_isel], w=[B_eidxf])
                P.op("dve", lambda e: e.tensor_copy(out=eidx[:, jr, :], in_=eidxf[:]), r=[B_eidxf], w=[B_eidx[jr]])
                gv_ = gate[:, jr, :].rearrange("p (h k) -> p h k", h=8)
                P.op("dve", lambda e: e.tensor_tensor(
                    out=gv_, in0=cv[:], in1=cv[:, :, 0:1].to_broadcast([128, 8, 16]), op=ALU.subtract),
                     r=[B_cv], w=[B_gate[jr]])
                P.op("act", lambda e: e.activation(out=gv_, in_=gv_, func=AF.Exp), r=[B_gate[jr]], w=[B_gate[jr]])
                P.op("dve", lambda e: e.tensor_reduce(out=gsum[:], in_=gv_, axis=AX.X, op=ALU.add),
                     r=[B_gate[jr]], w=[B_gsum])
                P.op("dve", lambda e: e.reciprocal(out=gsum[:], in_=gsum[:]), r=[B_gsum], w=[B_gsum])
                P.op("dve", lambda e: e.tensor_tensor(
                    out=gv_, in0=gv_, in1=gsum[:].unsqueeze(2).to_broadcast([128, 8, 16]), op=ALU.mult),
                     r=[B_gate[jr], B_gsum], w=[B_gate[jr]])

            for s in range(NS):
                add(12, lambda s=s: retr_topk(s, range(0, 8)))
                add(12, lambda s=s: retr_topk(s, range(8, 16)))
                add(20, lambda s=s: retr_cand(s))
                add(20, lambda s=s: retr_idx(s))
            return units

        gcnt = {"g": 0, "d": 0}

        def E_gather(j, slot):
            jr = j % RJ
            hp_, s = (j // NS) % 2, j % NS
            gi = gcnt["g"] % NG
            gcnt["g"] += 1
            ci = slot % NCT
            P.dma("pool", lambda e: e.indirect_dma_start(
                out=UV[gi][:, :], out_offset=None, in_=uvbf,
                in_offset=bass.IndirectOffsetOnAxis(ap=eidx[:, jr, slot:slot + 1], axis=0)),
                  d_UV[gi], r=[B_eidx[jr], B_uvbf], w=[B_UVu[gi], B_UVv[gi]])
            P.op("dve", lambda e: e.scalar_tensor_tensor(
                out=UV[gi][:, 0:D], in0=UV[gi][:, 0:D], scalar=1.0, in1=h2b[:, hp_, s, :], op0=ALU.mult,
                op1=ALU.mult, accum_out=actv[:, slot:slot + 1]),
                 r=[B_UVu[gi], B_h2b[hp_][s]], w=[B_UVu[gi], B_actc[ci]])
            P.op("act", lambda e: e.activation(out=gel[:, slot:slot + 1], in_=actv[:, slot:slot + 1], func=AF.Gelu),
                 r=[B_actc[ci]], w=[B_gelc[ci]])
            P.op("act", lambda e: e.activation(out=cgt[:, slot:slot + 1], in_=gel[:, slot:slot + 1], func=AF.Copy,
                                               scale=gate[:, jr, slot:slot + 1]),
                 r=[B_gelc[ci], B_gate[jr]], w=[B_cgc[ci]])
            return (gi, ci)

        def E_apply(j, slot, st):
            gi, ci = st
            di = gcnt["d"] % 2
            gcnt["d"] += 1
            P.op("act", lambda e: e.activation(
                out=diag[:, di, :], in_=ident_f[:], func=AF.Copy, scale=cgt[:, slot:slot + 1]),
                 r=[B_cgc[ci], B_const], w=[B_diag[di]])
            for n in range(4):
                P.op("pe", lambda e, n=n: e.matmul(
                    psA[n][:, :], lhsT=diag[:, di, :], rhs=UV[gi][:, D + n * 512:D + (n + 1) * 512],
                    start=(slot == 0), stop=(slot == 127)),
                     r=[B_diag[di], B_UVv[gi]], w=[B_psA[n]])

        def V_start(j):
            ti = j // NS
            s = j % NS
            y0 = tiles[ti][4]
            P.dma("act", lambda e: e.dma_start(out=xfin[:], in_=yout[y0 + 128 * s:y0 + 128 * (s + 1), :]),
                  d_rl, r=[B_ydram[j % RJ]], w=[B_xfin])

        def V_finish(j):
            ti = j // NS
            s = j % NS
            seg, it, ntiles, r0, y0 = tiles[ti]
            for n in range(4):
                P.op("dve", lambda e, n=n: e.tensor_tensor(
                    out=tmpo[:], in0=psA[n][:, :], in1=gt2b[:, n * 512:(n + 1) * 512], op=ALU.mult),
                     r=[B_psA[n], B_bc[id(gt2b)]], w=[B_tmpo])
                P.op("dve", lambda e, n=n: e.tensor_tensor(
                    out=xfin[:, n * 512:(n + 1) * 512], in0=xfin[:, n * 512:(n + 1) * 512], in1=tmpo[:],
                    op=ALU.add), r=[B_tmpo, B_xfin], w=[B_xfin])
            P.op("act", lambda e: e.activation(
                out=sqa[:], in_=xfin[:], func=AF.Square, accum_out=stt[:, 8:9]),
                 r=[B_xfin], w=[B_sqa, B_stt2])
            rstd_from_ss(8, B_stt2)
            P.op("dve", lambda e: e.scalar_tensor_tensor(
                out=xfin[:], in0=xfin[:], scalar=stt[:, 8:9], in1=gfb[:],
                op0=ALU.mult, op1=ALU.mult), r=[B_xfin, B_stt2, B_misc], w=[B_xfin])
            P.dma("act", lambda e: e.dma_start(out=yout[y0 + 128 * s:y0 + 128 * (s + 1), :], in_=xfin[:]),
                  d_y, r=[B_xfin], w=[B_ydram[j % RJ]])
            if j + 1 < NJ and (j + 1) % NS == 0 and tiles[(j + 1) // NS][1] == 0:
                bcast_tiles(tiles[(j + 1) // NS][0], ((gt2b, 5),), psA[0:2], B_psA[0:2])

        for (c_, fn) in M_units(0):
            fn()

        class MSched:
            def __init__(self, fifo, nhalf):
                self.items = fifo
                n = len(fifo)
                self.deps = [None] * n
                self.when = [None] * n
                self.first = 0
                writer, readers = {}, {}
                for i, (kind, eng, fn, dsem, r, w) in enumerate(fifo):
                    d = set()
                    rr, ww = _flat(r), _flat(w)
                    for b_ in rr:
                        if id(b_) in writer:
                            d.add(writer[id(b_)])
                    for b_ in ww:
                        if id(b_) in writer:
                            d.add(writer[id(b_)])
                        d.update(readers.get(id(b_), ()))
                    d.discard(i)
                    self.deps[i] = d
                    for b_ in rr:
                        readers.setdefault(id(b_), set()).add(i)
                    for b_ in ww:
                        writer[id(b_)] = i
                        readers[id(b_)] = set()
                tot = {e: 0.0 for e in Prog.ENGS}
                for it_ in fifo:
                    tot[it_[1]] += Prog.COST[it_[1]]
                tgt = max(nhalf * 0.72, 1.0)
                self.budget = {e: 1.25 * tot[e] / tgt + Prog.COST[e] for e in Prog.ENGS}
                self.t = 0

            def empty(self):
                return self.first >= len(self.items)

            def step(self, flush=False):
                used = {e: 0.0 for e in Prog.ENGS}
                n = len(self.items)
                i = self.first
                scanned = 0
                while i < n and scanned < 700:
                    if self.when[i] is None:
                        scanned += 1
                        kind, eng, fn, dsem, r, w = self.items[i]
                        ok = flush or used[eng] + Prog.COST[eng] <= self.budget[eng]
                        if ok:
                            for d in self.deps[i]:
                                wd = self.when[d]
                                if wd is None:
                                    ok = False
                                    break
                                if not flush and self.items[d][1] != eng and wd >= self.t:
                                    ok = False
                                    break
                        if ok:
                            self.when[i] = self.t
                            used[eng] += Prog.COST[eng]
                            if kind == "op":
                                P._add(Op(eng, fn), r, w)
                            else:
                                o = Op(eng, fn)
                                o.dsem = dsem
                                o.needed = True
                                o.inc = 16
                                P._add(o, r, w)
                    i += 1
                while self.first < n and self.when[self.first] is not None:
                    self.first += 1
                self.t += 1

        bcast_tiles(tiles[0][0], ((gt2b, 5),), psA[0:2], B_psA[0:2])
        sched = None
        for j in range(NJ):
            if j % NS == 0:
                tnext = j // NS + 1
                assert sched is None or sched.empty()
                sched = None
                if tnext < NT:
                    fifo = []
                    P.defer = fifo
                    for (c_, fn) in M_units(tnext):
                        fn()
                    P.defer = None
                    sched = MSched(fifo, NS * 128 * 2)
            V_start(j)
            prev = None
            for slot in range(128):
                st = E_gather(j, slot)
                if sched is not None:
                    sched.step()
                if prev is not None:
                    E_apply(j, slot - 1, prev)
                prev = st
                if sched is not None:
                    sched.step()
            E_apply(j, 127, prev)
            if sched is not None and j % NS == NS - 1:
                while not sched.empty():
                    sched.step(flush=True)
            V_finish(j)

        lasts = {}
        for en in ("sp", "act"):
            for o in P.ops[en]:
                if o.dsem is not None and (o.dsem is d_y or o.dsem in d_st):
                    lasts[id(o.dsem)] = o
        fin = Op("sp", lambda e: e.nop())
        fin.deps = list(lasts.values())
        P.ops["sp"].append(fin)

        with nc.Block() as block2:
            P.resolve(engsem)

            @block2.sync
            def _(e):
                P.emit("sp", e)

            @block2.scalar
            def _(e):
                P.emit("act", e)

            @block2.vector
            def _(e):
                P.emit("dve", e)

            @block2.tensor
            def _(e):
                P.emit("pe", e)

            @block2.gpsimd
            def _(e):
                P.emit("pool", e)
    return nc


def _block_order():
    order = []
    for c in range(8):
        order += [c, 8 + c]
    for c in range(8):
        order += [32 + c, 24 + c, 16 + c]
    return order


def prep_shared(w_ada, b_ada, g_norm1, w_in, conv_w, conv_b, ln_g, ln_b, short_w, w_out, g_norm2, w_query,
                sub_keys, expert_u, expert_v, g_final):
    f = np.float32
    sh = {}
    wa = np.asarray(w_ada[0], dtype=f)
    sh["wada_blocks"] = np.ascontiguousarray(wa.reshape(KC, 128, 96, 128).transpose(2, 1, 0, 3))
    sh["badaT"] = np.ascontiguousarray(np.asarray(b_ada[0], dtype=f).reshape(96, 128).T)
    sh["g1T"] = np.ascontiguousarray(np.asarray(g_norm1[0], dtype=f).reshape(KC, 128).T)
    sh["g2T"] = np.ascontiguousarray(np.asarray(g_norm2[0], dtype=f).reshape(KC, 128).T)
    sh["gfin_b"] = np.ascontiguousarray(np.broadcast_to(np.asarray(g_final)[None, :], (128, D)), dtype=f)
    wi = np.asarray(w_in[0], dtype=f)
    wi_blocks = wi.reshape(KC, 128, 40, 128).transpose(2, 1, 0, 3)
    wi_blocks = wi_blocks[_block_order()]
    wo_blocks = np.asarray(w_out[0], dtype=f).reshape(KC, 128, 16, 128).transpose(2, 1, 0, 3)
    wq_blocks = np.asarray(w_query[0], dtype=f).reshape(KC, 128, 16, 128).transpose(2, 1, 0, 3)
    sh["w_blocks"] = np.ascontiguousarray(np.concatenate([wi_blocks, wo_blocks, wq_blocks], axis=0))
    sh["convw"] = np.ascontiguousarray(np.asarray(conv_w[0], dtype=f).reshape(31, 8, 128).transpose(2, 1, 0))
    sh["convb"] = np.ascontiguousarray(np.asarray(conv_b[0], dtype=f).reshape(8, 128).T)
    sh["lng"] = np.ascontiguousarray(np.asarray(ln_g[0], dtype=f).reshape(8, 128).T)
    sh["lnb"] = np.ascontiguousarray(np.asarray(ln_b[0], dtype=f).reshape(8, 128).T)
    sh["shortw"] = np.ascontiguousarray(np.asarray(short_w[0], dtype=f).reshape(3, 8, 128).transpose(2, 1, 0))
    sk = np.asarray(sub_keys[0], dtype=f)
    sh["keysT"] = np.ascontiguousarray(sk.reshape(16, 128, 128).transpose(2, 0, 1))
    sh["expert_u"] = np.ascontiguousarray(expert_u[0], dtype=f)
    sh["expert_v"] = np.ascontiguousarray(expert_v[0], dtype=f)
    sh["ident"] = np.eye(128, dtype=f)
    sh["iota16"] = np.ascontiguousarray(np.broadcast_to(np.arange(16, dtype=f)[None, :], (128, 16)))
    return sh


def prep_core(xp_seq, xs_seq, s0, L1, cp, cs):
    f = np.float32
    L0 = xp_seq.shape[0]
    S = xs_seq.shape[0]
    xin = np.zeros((L0 + L1 + 4 * HALO, D), dtype=f)
    xin[HALO:HALO + L0] = xp_seq
    base = L0 + 2 * HALO
    lo = max(0, s0 - HALO)
    hi = min(S, s0 + L1 + HALO)
    xin[base + (lo - (s0 - HALO)):base + (hi - (s0 - HALO))] = xs_seq[lo:hi]
    emask = np.zeros((128, 4), dtype=f)
    emask[:, 2] = 1.0 if s0 > 0 else 0.0
    emask[:, 3] = 1.0 if s0 + L1 < S else 0.0
    cc = np.stack([np.asarray(cp, dtype=f), np.asarray(cs, dtype=f)], axis=-1)
    cT = np.ascontiguousarray(cc.reshape(KC, 128, 2).transpose(1, 0, 2))
    return {"xin": xin, "emask": emask, "cT": cT}


_NC_CACHE = {}


def kernel(x_prompt, x_sample, c_prompt, c_sample, w_ada, b_ada, g_norm1, w_in, conv_w, conv_b,
           ln_g, ln_b, short_w, w_out, g_norm2, w_query, sub_keys, expert_u, expert_v, g_final):
    x_prompt = np.asarray(x_prompt)
    x_sample = np.asarray(x_sample)
    c_prompt = np.asarray(c_prompt)
    c_sample = np.asarray(c_sample)
    ncores = 8
    B0, L0, _ = x_prompt.shape
    B1, S1, _ = x_sample.shape
    per = ncores // B1
    L1 = S1 // per
    sh = prep_shared(*(np.asarray(a) for a in (w_ada, b_ada, g_norm1, w_in, conv_w, conv_b, ln_g, ln_b, short_w,
                                               w_out, g_norm2, w_query, sub_keys, expert_u, expert_v, g_final)))
    in_maps = []
    for c in range(ncores):
        b1 = c // per
        q = c % per
        m = dict(sh)
        m.update(prep_core(x_prompt[c], x_sample[b1], q * L1, L1, c_prompt[c], c_sample[b1]))
        in_maps.append(m)
    key = (L0, L1)
    if key not in _NC_CACHE:
        _NC_CACHE[key] = build_program(L0, L1)
    nc = _NC_CACHE[key]
    res = run_bass_kernel_spmd(nc, in_maps, core_ids=list(range(ncores)))
    y_prompt = np.empty((B0, L0, D), dtype=np.float32)
    y_sample = np.empty((B1, S1, D), dtype=np.float32)
    for c in range(ncores):
        y = np.asarray(res.results[c]["yout"])
        y_prompt[c] = y[:L0]
        y_sample[c // per, (c % per) * L1:(c % per + 1) * L1] = y[L0:]
    return (y_prompt, y_sample)
```

```python
import contextlib
import numpy as np
import concourse.bass as bass
import concourse.mybir as mybir
from concourse.bass_utils import run_bass_kernel_spmd

F32 = mybir.dt.float32
BF16 = mybir.dt.bfloat16
I32 = mybir.dt.int32
U32 = mybir.dt.uint32
ALU = mybir.AluOpType
AF = mybir.ActivationFunctionType
AX = mybir.AxisListType

D = 2048
KC = 16
DIN = 5120
NEXP = 16384
T = 256
NS = T // 128
HALO = 15
N = T + 2 * HALO
RMS_EPS = 1e-6
LN_EPS = 1e-5
NBLK = 72


class Buf:
    __slots__ = ("name", "writer", "readers")

    def __init__(self, name):
        self.name = name
        self.writer = None
        self.readers = {}


class DSem:
    __slots__ = ("sem", "count")

    def __init__(self, sem):
        self.sem = sem
        self.count = 0


class Op:
    __slots__ = ("eng", "fn", "deps", "needed", "sem", "val", "inc", "dsem")

    def __init__(self, eng, fn):
        self.eng = eng
        self.fn = fn
        self.deps = []
        self.needed = False
        self.sem = None
        self.val = 0
        self.inc = 1
        self.dsem = None


def _flat(x):
    out = []
    for b in x:
        if isinstance(b, (list, tuple)):
            out.extend(_flat(b))
        elif b is not None:
            out.append(b)
    return out


def _writers(b):
    w = b.writer
    if w is None:
        return []
    return w if isinstance(w, list) else [w]


class Prog:
    ENGS = ("sp", "act", "dve", "pe", "pool")

    def __init__(self):
        self.ops = {e: [] for e in self.ENGS}
        self.defer = None

    COST = {"dve": 0.6, "pe": 0.27, "act": 0.45, "sp": 0.0, "pool": 0.0}

    def drain(self, fifo, budget):
        used = {e: 0.0 for e in self.ENGS}
        while fifo:
            eng0 = fifo[0][1]
            if used[eng0] > 0 and used[eng0] + self.COST[eng0] > budget[eng0]:
                break
            kind, eng, fn, dsem, r, w = fifo.pop(0)
            used[eng] += self.COST[eng]
            if kind == "op":
                self._add(Op(eng, fn), r, w)
            else:
                o = Op(eng, fn)
                o.dsem = dsem
                o.needed = True
                o.inc = 16
                self._add(o, r, w)

    def _add(self, op, reads, writes):
        reads = _flat(reads)
        writes = _flat(writes)
        deps = {}
        for b in reads:
            for wr in _writers(b):
                deps[id(wr)] = wr
        for b in writes:
            for wr in _writers(b):
                deps[id(wr)] = wr
            for r in b.readers.values():
                deps[id(r)] = r
        for d in deps.values():
            if d is op:
                continue
            if op.eng == "pe" and d.eng == "pe" and d.dsem is None:
                continue
            op.deps.append(d)
            d.needed = True
        for b in reads:
            key = ("d", id(op.dsem)) if op.dsem is not None else op.eng
            b.readers[key] = op
        for b in writes:
            b.writer = op
            b.readers = {}
        self.ops[op.eng].append(op)
        return op

    def op(self, eng, fn, r=(), w=()):
        if self.defer is not None:
            self.defer.append(("op", eng, fn, None, r, w))
            return None
        return self._add(Op(eng, fn), r, w)

    def dma(self, eng, fn, dsem, r=(), w=()):
        if self.defer is not None:
            self.defer.append(("dma", eng, fn, dsem, r, w))
            return None
        o = Op(eng, fn)
        o.dsem = dsem
        o.needed = True
        o.inc = 16
        return self._add(o, r, w)

    def resolve(self, engsem):
        for e in self.ENGS:
            cnt = 0
            for o in self.ops[e]:
                if o.dsem is not None:
                    o.dsem.count += 16
                    o.sem = o.dsem.sem
                    o.val = o.dsem.count
                elif o.needed:
                    cnt += 1
                    o.sem = engsem[e]
                    o.val = cnt

    def emit(self, eng_name, eng):
        waited = {}
        for o in self.ops[eng_name]:
            need = {}
            for d in o.deps:
                k = id(d.sem)
                if k not in need or need[k][1] < d.val:
                    need[k] = (d.sem, d.val)
            for k, (sem, val) in need.items():
                if waited.get(k, 0) >= val:
                    continue
                eng.wait_ge(sem, val)
                waited[k] = val
            ins = o.fn(eng)
            if o.needed:
                ins.then_inc(o.sem, o.inc)

    def clear(self):
        self.ops = {e: [] for e in self.ENGS}


def build_program(L0, L1, debug=False):
    assert L0 % T == 0 and L1 % T == 0
    nc = bass.Bass("TRN2", target_bir_lowering=False)
    RIN = L0 + L1 + 4 * HALO
    LT = L0 + L1

    def din(name, shape, dt=F32):
        return nc.dram_tensor(name, list(shape), dt, kind="ExternalInput").ap()

    xin = din("xin", [RIN, D])
    cT = din("cT", [128, KC, 2])
    wada_blocks = din("wada_blocks", [96, 128, KC, 128])
    badaT_d = din("badaT", [128, 96])
    g1T_d = din("g1T", [128, KC])
    g2T_d = din("g2T", [128, KC])
    gfin_b = din("gfin_b", [128, D])
    w_blocks = din("w_blocks", [NBLK, 128, KC, 128])
    convw_d = din("convw", [128, 8, 31])
    convb_d = din("convb", [128, 8])
    lng_d = din("lng", [128, 8])
    lnb_d = din("lnb", [128, 8])
    shortw_d = din("shortw", [128, 8, 3])
    keysT_d = din("keysT", [128, 16, 128])
    eu = din("expert_u", [NEXP, D])
    ev = din("expert_v", [NEXP, D])
    ident_d = din("ident", [128, 128])
    iota16_d = din("iota16", [128, 16])
    em_d = din("emask", [128, 4])
    yout = nc.dram_tensor("yout", [LT, D], F32, kind="ExternalOutput").ap()
    wbf = nc.dram_tensor("wbf_scratch", [NBLK, 128, KC * 128], BF16).ap()
    ubf = nc.dram_tensor("ubf_scratch", [NEXP, D], BF16).ap()
    vbf = nc.dram_tensor("vbf_scratch", [NEXP, D], BF16).ap()
    dbg = {}
    if debug:
        for nm, shp, dt in (("dbg_h2", [128, D], F32), ("dbg_eidx", [128, 128], I32),
                            ("dbg_gate", [128, 128], F32), ("dbg_act", [128, 128], F32),
                            ("dbg_x1", [128, D], F32), ("dbg_S", [128, 2048], F32),
                            ("dbg_us", [128, 16 * T], F32), ("dbg_peer", [128, D], F32)):
            dbg[nm] = nc.dram_tensor(nm, shp, dt, kind="ExternalOutput").ap()

    es = contextlib.ExitStack()
    with es:
        def sb(name, shape, dt=F32):
            return es.enter_context(nc.sbuf_tensor(name + "_sb", list(shape), dt))

        def ps(name, shape, dt=F32):
            return es.enter_context(nc.psum_tensor(name + "_ps", list(shape), dt))

        def newsem(name):
            return es.enter_context(nc.semaphore(name))

        engsem = {e: newsem("s_" + e) for e in Prog.ENGS}
        _dsn = [0]

        def dsem():
            _dsn[0] += 1
            return DSem(newsem("d%d" % _dsn[0]))

        P = Prog()

        ident_f = sb("ident_f", [128, 128]);
        ident_b = sb("ident_b", [128, 128], BF16)
        ones_f = sb("ones_f", [128, 128])
        iota16 = sb("iota16", [128, 16])
        em = sb("em", [128, 4])
        featv = sb("featv", [128, KC, 12])
        featmod = sb("featmod", [128, 96, 2])
        badaT = sb("badaT_s", [128, 96])
        g1T = sb("g1T_s", [128, KC])
        g2T = sb("g2T_s", [128, KC])
        cTs = sb("cTs", [128, KC, 2])
        sT = sb("sT", [128, KC, 2])
        g2b = sb("g2b", [128, D]);
        b2b = sb("b2b", [128, D]);
        gt2b = sb("gt2b", [128, D]);
        gfb = sb("gfb", [128, D])
        convw = sb("convw_s", [128, 8, 31]);
        convb = sb("convb_s", [128, 8])
        lng = sb("lng_s", [128, 8]);
        lnb = sb("lnb_s", [128, 8]);
        shortw = sb("shortw_s", [128, 8, 3])
        keysT = sb("keysT_s", [128, 16, 128])

        B_const = Buf("const")
        B_featv = Buf("featv")

        psA = [ps("psA%d" % i, [128, 512]) for i in range(4)]
        B_psA = [Buf("psA%d" % i) for i in range(4)]
        psZ = [ps("psZ%d" % i, [128, 512]) for i in range(2)]
        B_psZ = [Buf("psZ%d" % i) for i in range(2)]
        psZ_, B_psZ_ = psZ, B_psZ
        psX = ps("psX", [128, 512])
        B_psX = Buf("psX")
        psS = ps("psS", [128, 512])
        B_psS = Buf("psS")
        psXr = [psX, psS]
        B_psXr = [B_psX, B_psS]
        psTvr = [psXr[i][:, 0:256].bitcast(BF16).rearrange("p (j t) -> p j t", j=4) for i in range(2)]
        xrc = [0]

        xt = sb("xt", [128, NS, D])
        B_xt = [Buf("xt%d" % s) for s in range(NS)]
        xfin = sb("xfin", [128, D])
        B_xfin = Buf("xfin")
        h2b = sb("h2b", [128, 2, NS, D], BF16)
        B_h2b = [[Buf("h2b_%d_%d" % (a, s)) for s in range(NS)] for a in range(2)]
        sqa = sb("sqa", [128, D], BF16)
        B_sqa = Buf("sqa")
        sqd = sb("sqd", [128, D], BF16)
        B_sqd = Buf("sqd")
        stt = sb("stt", [128, 16])
        B_stt = Buf("stt")
        B_stt2 = Buf("stt2")
        hT = sb("hT", [128, KC, N], BF16)
        B_hT = Buf("hT")
        h2T = hT
        B_h2T = B_hT
        wst = sb("wst", [128, 2, KC * 128])
        B_wst = [Buf("wst0"), Buf("wst1")]
        d_wst = [dsem(), dsem()]
        NWB = 3
        wb = sb("wb", [128, NWB, KC, 128], BF16)
        B_wb = [Buf("wb%d" % i) for i in range(NWB)]
        d_wb = [dsem() for _ in range(NWB)]
        B_wbf = Buf("wbf")
        B_ubf = Buf("ubf")
        B_vbf = Buf("vbf")
        sg = sb("sg", [128, N]);
        B_sg = Buf("sg")
        u2 = sb("u", [128, 2, N], BF16);
        B_u2 = [Buf("u0"), Buf("u1")]
        NDC = 4
        dgc = sb("dgc", [128, NDC, 128], BF16)
        B_dgc = [Buf("dgc%d" % i) for i in range(NDC)]
        dgn = [0]
        xs_sb = sb("xs_sb", [128, N]);
        B_xs = Buf("xs")
        vv = sb("vv", [128, N]);
        B_vv = Buf("vv")
        acc = sb("acc", [128, T]);
        B_acc = Buf("acc")
        poolA = sb("poolA", [128, 4096])
        B_pA = [Buf("pA%d" % i) for i in range(16)]
        ucall = poolA[:, 0:2048].rearrange("p (c t) -> p c t", c=8)
        cat = poolA[:, 2048:2560].rearrange("p (a t) -> p a t", a=2)
        B_cat = [B_pA[8], B_pA[9]]
        lnm = poolA[:, 2560:2816];
        B_lnm = B_pA[10]
        lnr = poolA[:, 2816:3072];
        B_lnr = B_pA[11]
        tmp1 = poolA[:, 3072:3328];
        B_tmp1 = B_pA[12]
        tmp2 = poolA[:, 3328:3584];
        B_tmp2 = B_pA[13]
        Sall = poolA[:].rearrange("p (s q k) -> p s q k", s=2, q=16)

        def B_S(s, q):
            return B_pA[(s * 16 + q) // 2]

        usT = sb("usT", [128, KC, T], BF16)
        B_usT = [Buf("usT%d" % i) for i in range(KC)]
        mTs = sb("mTs", [128, 2, T]);
        B_mTs = [Buf("mTs0"), Buf("mTs1")]
        work = sb("work", [128, 256]);
        B_work = Buf("work")
        tv = sb("tv", [128, 16, 16]);
        B_tv = Buf("tv")
        ti_ = sb("ti", [128, 16, 16], U32);
        B_ti = Buf("ti")
        tif = sb("tif", [128, 16, 16]);
        B_tif = Buf("tif")
        xh = wst[:, 0, :]
        B_xh = B_wst[0]
        cand = wst[:, 0, :].rearrange("p (h c) -> p h c", h=8)
        B_cand = B_wst[0]
        oh = wst[:, 1, :].rearrange("p (h a b) -> p h a b", h=8, a=16)
        B_oh = B_wst[1]
        h2f = wst[:, 1, :]
        B_h2f = B_wst[1]
        cv = sb("cv", [128, 8, 16]);
        B_cv = Buf("cv")
        cpos = sb("cpos", [128, 8, 16], U32);
        B_cpos = Buf("cpos")
        cab = sb("cab", [128, 2, 128], U32);
        B_cab = Buf("cab")
        cabf = sb("cabf", [128, 2, 128]);
        B_cabf = Buf("cabf")
        isel = sb("isel", [128, 2, 128]);
        B_isel = Buf("isel")
        eidxf = sb("eidxf", [128, 128]);
        B_eidxf = Buf("eidxf")
        RJ = 4
        eidx = sb("eidx", [128, RJ, 128], I32);
        B_eidx = [Buf("eidx%d" % i) for i in range(RJ)]
        gate = sb("gate", [128, RJ, 128]);
        B_gate = [Buf("gate%d" % i) for i in range(RJ)]
        gsum = sb("gsum", [128, 8]);
        B_gsum = Buf("gsum")
        actv = sb("actv", [128, RJ, 128]);
        B_actv = [Buf("actv%d" % i) for i in range(RJ)]
        coef = sb("coef", [128, RJ, 128]);
        B_coef = [Buf("coef%d" % i) for i in range(RJ)]
        NG = 3
        GU = [sb("GU%d" % i, [128, D], BF16) for i in range(NG)]
        B_GU = [Buf("GU%d" % i) for i in range(NG)]
        d_GU = [dsem() for _ in range(NG)]
        GV = [sb("GV%d" % i, [128, D], BF16) for i in range(NG)]
        B_GV = [Buf("GV%d" % i) for i in range(NG)]
        d_GV = [dsem() for _ in range(NG)]
        diag = sb("diag", [128, 2, 128], BF16)
        B_diag = [Buf("diag0"), Buf("diag1")]
        tmpo = sb("tmpo", [128, 512]);
        B_tmpo = Buf("tmpo")
        d_x = [dsem() for _ in range(NS)]
        d_xh = dsem()
        d_st = [dsem() for _ in range(NS)]
        d_rl = dsem()
        d_y = dsem()
        B_ydram = [Buf("ydram%d" % i) for i in range(RJ)]

        def fv(seg, kk, kc):
            return featv[:, kc, seg * 6 + kk:seg * 6 + kk + 1]

        def rstd_from_ss(col, Bst, nparts=128):
            P.op("act", lambda e: e.activation(out=stt[0:nparts, col:col + 1], in_=stt[0:nparts, col:col + 1],
                                               func=AF.Sqrt, bias=RMS_EPS, scale=1.0 / D),
                 r=[Bst], w=[Bst])
            P.op("dve", lambda e: e.reciprocal(out=stt[0:nparts, col:col + 1], in_=stt[0:nparts, col:col + 1]),
                 r=[Bst], w=[Bst])

        def bcast_tiles(seg, which, banks=None, Bbanks=None):
            psZ, B_psZ = (banks, Bbanks) if banks is not None else (psZ_, B_psZ_)
            for (dst, kk) in which:
                for kc in range(KC):
                    mi = kc % 2
                    P.op("dve", lambda e, mi=mi, kc=kc, kk=kk, seg=seg: e.tensor_scalar(
                        out=mTs[:, mi, 0:128], in0=ident_f[:], scalar1=fv(seg, kk, kc), scalar2=None,
                        op0=ALU.mult), r=[B_const, B_featv], w=[B_mTs[mi]])
                    n = (kc // 4) % 2
                    P.op("pe", lambda e, mi=mi, kc=kc, n=n: e.matmul(
                        psZ[n][:, (kc % 4) * 128:(kc % 4 + 1) * 128], lhsT=ones_f[:], rhs=mTs[:, mi, 0:128],
                        start=True, stop=True), r=[B_mTs[mi], B_const], w=[B_psZ[n]])
                    if kc % 4 == 3:
                        c4 = kc // 4
                        P.op("dve", lambda e, dst=dst, n=n, c4=c4: e.tensor_copy(
                            out=dst[:, c4 * 512:(c4 + 1) * 512], in_=psZ[n][:, :]), r=[B_psZ[n]], w=[B_bc[id(dst)]])

        B_bc = {id(b2b): Buf("b2b"), id(g2b): Buf("g2b"), id(gt2b): Buf("gt2b")}

        d_misc = dsem()
        B_misc = Buf("misc")
        loads = [(ident_f[:], ident_d), (iota16[:], iota16_d), (em[:], em_d),
                 (gfb[:], gfin_b), (convw[:], convw_d), (convb[:], convb_d), (lng[:], lng_d),
                 (lnb[:], lnb_d), (shortw[:], shortw_d), (keysT[:], keysT_d), (cTs[:], cT),
                 (badaT[:], badaT_d), (g1T[:], g1T_d), (g2T[:], g2T_d)]
        for (dst, src) in loads:
            P.dma("sp", lambda e, dst=dst, src=src: e.dma_start(out=dst, in_=src), d_misc, w=[B_misc])
        P.op("dve", lambda e: e.tensor_copy(out=ident_b[:], in_=ident_f[:]), r=[B_misc], w=[B_const])
        P.op("dve", lambda e: e.memset(ones_f[:], 1.0), w=[B_const])
        B_sT = Buf("sT")
        P.op("act", lambda e: e.activation(out=sT[:], in_=cTs[:], func=AF.Silu), r=[B_misc], w=[B_sT])
        for oc in range(96):
            i = oc % 2
            P.dma("sp", lambda e, i=i, oc=oc: e.dma_start(
                out=wst[:, i, :], in_=wada_blocks[oc].rearrange("p k c -> p (k c)")), d_wst[i], w=[B_wst[i]])
            for kc in range(KC):
                P.op("pe", lambda e, i=i, oc=oc, kc=kc: e.matmul(
                    psS[:, oc * 2:(oc + 1) * 2], lhsT=wst[:, i, kc * 128:(kc + 1) * 128], rhs=sT[:, kc, :],
                    start=(kc == 0), stop=(kc == KC - 1)), r=[B_wst[i], B_sT], w=[B_psS])
        B_fm = Buf("featmod")
        P.op("dve", lambda e: e.tensor_tensor(
            out=featmod[:], in0=psS[:, 0:192].rearrange("p (o r) -> p o r", r=2),
            in1=badaT[:].unsqueeze(2).to_broadcast([128, 96, 2]), op=ALU.add), r=[B_psS, B_misc], w=[B_fm])
        for seg_ in range(2):
            for kk in range(6):
                src = featmod[:, kk * 16:(kk + 1) * 16, seg_]
                dst = featv[:, :, seg_ * 6 + kk]
                if kk in (1, 4):
                    gT = g1T if kk == 1 else g2T
                    P.op("dve", lambda e, src=src, dst=dst, gT=gT: e.scalar_tensor_tensor(
                        out=dst, in0=src, scalar=1.0, in1=gT[:], op0=ALU.add, op1=ALU.mult),
                         r=[B_fm, B_misc], w=[B_featv])
                else:
                    P.op("dve", lambda e, src=src, dst=dst: e.tensor_copy(out=dst, in_=src), r=[B_fm], w=[B_featv])

        stage_bf = [(GU[i], B_GU[i], d_GU[i]) for i in range(NG)] + [(GV[i], B_GV[i], d_GV[i]) for i in range(NG)]
        cnt_ = [0]
        last_store = {}

        def convert(src_ap, dst_ap, Bdst):
            k = cnt_[0]
            cnt_[0] += 1
            i = k % 2
            bt, Bbt, dbt = stage_bf[k % len(stage_bf)]
            P.dma("sp", lambda e, i=i, src_ap=src_ap: e.dma_start(out=wst[:, i, :], in_=src_ap), d_wst[i],
                  w=[B_wst[i]])
            if k % 2 == 0:
                P.op("act", lambda e, i=i, bt=bt: e.activation(out=bt[:], in_=wst[:, i, :], func=AF.Copy),
                     r=[B_wst[i]], w=[Bbt])
            else:
                P.op("dve", lambda e, i=i, bt=bt: e.tensor_copy(out=bt[:], in_=wst[:, i, :]), r=[B_wst[i]], w=[Bbt])
            o = P.dma("pool", lambda e, bt=bt, dst_ap=dst_ap: e.dma_start(out=dst_ap, in_=bt[:]), dbt,
                      r=[Bbt])
            last_store.setdefault(id(Bdst), {})[id(dbt)] = o

        for blk in range(NBLK):
            convert(w_blocks[blk].rearrange("p k c -> p (k c)"), wbf[blk], B_wbf)
        for r_ in range(NEXP // 128):
            convert(eu[r_ * 128:(r_ + 1) * 128, :], ubf[r_ * 128:(r_ + 1) * 128, :], B_ubf)
            convert(ev[r_ * 128:(r_ + 1) * 128, :], vbf[r_ * 128:(r_ + 1) * 128, :], B_vbf)
        for Bd in (B_wbf, B_ubf, B_vbf):
            Bd.writer = list(last_store[id(Bd)].values())
        print("[kernel] sbuf bytes remaining per partition:", nc.sbuf_bytes_remaining)

        tiles = []
        for (seg, xbase, ybase, ntiles) in ((0, HALO, 0, L0 // T), (1, L0 + 3 * HALO, L0, L1 // T)):
            for it in range(ntiles):
                tiles.append((seg, it, ntiles, xbase + it * T, ybase + it * T))
        NT = len(tiles)
        NJ = NT * NS

        wload = [0]

        def prefetch_to(g):
            lim = min(g, NT * NBLK - 1)
            while wload[0] <= lim:
                k = wload[0]
                wload[0] += 1
                i = k % NWB
                blk = k % NBLK
                P.dma("sp", lambda e, i=i, blk=blk: e.dma_start(
                    out=wb[:, i].rearrange("p k c -> p (k c)"), in_=wbf[blk]), d_wb[i], r=[B_wbf], w=[B_wb[i]])

        zr = [0]

        def zmm(g, ncols, rhsT, Brhs):
            prefetch_to(g + 2)
            i = g % NWB
            wv, Bw = wb[:, i], B_wb[i]
            bi = zr[0] % 2
            zr[0] += 1
            for kc in range(KC):
                P.op("pe", lambda e, bi=bi, kc=kc, wv=wv, ncols=ncols, rhsT=rhsT: e.matmul(
                    psZ[bi][:, 0:ncols], lhsT=wv[:, kc, :], rhs=rhsT[:, kc, 0:ncols],
                    start=(kc == 0), stop=(kc == KC - 1)), r=[Bw, Brhs], w=[B_psZ[bi]])
            return psZ[bi], B_psZ[bi]

        def transposes_bf(src, npart, dsts, Bsrc, Bdst, scale_bias=None):
            for g4 in range(4):
                xi = xrc[0] % 2
                xrc[0] += 1
                psTv = psTvr[xi]
                B_psX = B_psXr[xi]
                for j in range(4):
                    kc = g4 * 4 + j
                    P.op("pe", lambda e, kc=kc, j=j, psTv=psTv: e.transpose(
                        psTv[:, j, 0:npart], src[0:npart, kc * 128:(kc + 1) * 128], ident_b[0:npart, 0:npart]),
                         r=[Bsrc, B_const], w=[B_psX])
                for j in range(4):
                    kc = g4 * 4 + j
                    for (dfn, p0, w_) in dsts:
                        dst = dfn(kc)
                        if scale_bias is not None:
                            sc, bi_ = scale_bias(kc)
                            if j % 2 == 0:
                                P.op("dve", lambda e, dst=dst, j=j, p0=p0, w_=w_, sc=sc, bi_=bi_, psTv=psTv: e.tensor_scalar(
                                    out=dst, in0=psTv[:, j, p0:p0 + w_], scalar1=sc, scalar2=bi_,
                                    op0=ALU.mult, op1=ALU.add), r=[B_psX, B_featv], w=[Bdst])
                            else:
                                P.op("act", lambda e, dst=dst, j=j, p0=p0, w_=w_, sc=sc, bi_=bi_, psTv=psTv: e.activation(
                                    out=dst, in_=psTv[:, j, p0:p0 + w_], func=AF.Identity, scale=sc, bias=bi_),
                                     r=[B_psX, B_featv], w=[Bdst])
                        else:
                            if j % 2 == 0:
                                P.op("dve", lambda e, dst=dst, j=j, p0=p0, w_=w_, psTv=psTv: e.tensor_copy(
                                    out=dst, in_=psTv[:, j, p0:p0 + w_]), r=[B_psX], w=[Bdst])
                            else:
                                P.op("act", lambda e, dst=dst, j=j, p0=p0, w_=w_, psTv=psTv: e.activation(
                                    out=dst, in_=psTv[:, j, p0:p0 + w_], func=AF.Copy), r=[B_psX], w=[Bdst])

        def M_units(ti):
            seg, it, ntiles, r0, y0 = tiles[ti]
            edgeL = (it == 0)
            edgeR = (it == ntiles - 1)
            hp_ = ti % 2
            g0 = ti * NBLK
            units = []

            def add(cost, fn):
                units.append((cost, fn))

            if it == 0:
                add(6, lambda: bcast_tiles(seg, ((b2b, 3), (g2b, 4))))

            def st_1a():
                for s in range(NS):
                    P.dma("sp", lambda e, s=s: e.dma_start(
                        out=xt[:, s, :], in_=xin[r0 + 128 * s:r0 + 128 * (s + 1), :]), d_x[s], w=[B_xt[s]])
                P.dma("sp", lambda e: e.dma_start(out=xh[0:HALO, :], in_=xin[r0 - HALO:r0, :]), d_xh, w=[B_xh])
                P.dma("sp", lambda e: e.dma_start(out=xh[HALO:2 * HALO, :], in_=xin[r0 + T:r0 + T + HALO, :]),
                      d_xh, w=[B_xh])
                prefetch_to(g0 + 1)

            add(1, st_1a)

            def st_1b(pt):
                if pt < NS:
                    src, Bs, npart = xt[:, pt, :], B_xt[pt], 128
                else:
                    src, Bs, npart = xh, B_xh, 2 * HALO
                xn, B_xn = h2b[:, hp_, pt % 2, :], B_h2b[hp_][pt % 2]
                P.op("act", lambda e: e.activation(
                    out=sqa[0:npart, :], in_=src[0:npart, :], func=AF.Square,
                    accum_out=stt[0:npart, pt:pt + 1]), r=[Bs], w=[B_sqa, B_stt])
                rstd_from_ss(pt, B_stt, npart)
                P.op("act", lambda e: e.activation(
                    out=xn[0:npart, :], in_=src[0:npart, :], func=AF.Copy,
                    scale=stt[0:npart, pt:pt + 1]), r=[Bs, B_stt], w=[B_xn])
                if pt < NS:
                    dsts = [(lambda kc: hT[:, kc, HALO + 128 * pt:HALO + 128 * (pt + 1)], 0, 128)]
                else:
                    dsts = [(lambda kc: hT[:, kc, 0:HALO], 0, HALO),
                            (lambda kc: hT[:, kc, HALO + T:N], HALO, HALO)]
                transposes_bf(xn, npart, dsts, B_xn, B_hT, scale_bias=lambda kc: (fv(seg, 1, kc), fv(seg, 0, kc)))

            for pt in range(NS + 1):
                add(8, lambda pt=pt: st_1b(pt))

            def conf_chunk(c):
                za, Bza = zmm(g0 + 2 * c, N, hT, B_hT)
                zg, Bzg = zmm(g0 + 2 * c + 1, N, hT, B_hT)
                ui = c % 2
                u, B_u = u2[:, ui, :], B_u2[ui]
                P.op("act", lambda e: e.activation(out=sg[:], in_=zg[:, 0:N], func=AF.Sigmoid), r=[Bzg], w=[B_sg])
                P.op("dve", lambda e: e.tensor_tensor(out=u, in0=za[:, 0:N], in1=sg[:], op=ALU.mult),
                     r=[Bza, B_sg], w=[B_u])
                if edgeL:
                    P.op("dve", lambda e: e.tensor_scalar(
                        out=u[:, 0:HALO], in0=u[:, 0:HALO], scalar1=em[:, 2 * seg:2 * seg + 1], scalar2=None,
                        op0=ALU.mult), r=[B_u, B_const], w=[B_u])
                if edgeR:
                    P.op("dve", lambda e: e.tensor_scalar(
                        out=u[:, HALO + T:N], in0=u[:, HALO + T:N], scalar1=em[:, 2 * seg + 1:2 * seg + 2],
                        scalar2=None, op0=ALU.mult), r=[B_u, B_const], w=[B_u])
                cb = zr[0] % 2
                zr[0] += 1
                for k in range(31):
                    di = dgn[0] % NDC
                    dgn[0] += 1
                    P.op("act", lambda e, k=k, di=di: e.activation(
                        out=dgc[:, di, :], in_=ident_f[:], func=AF.Copy, scale=convw[:, c, k:k + 1]),
                         r=[B_const], w=[B_dgc[di]])
                    P.op("pe", lambda e, k=k, di=di: e.matmul(
                        psZ[cb][:, 0:T], lhsT=dgc[:, di, :], rhs=u[:, k:k + T], start=(k == 0), stop=(k == 30)),
                         r=[B_dgc[di], B_u], w=[B_psZ[cb]])
                P.op("act", lambda e: e.activation(out=cat[:, 0, :], in_=psZ[cb][:, 0:T], func=AF.Identity,
                                                   bias=convb[:, c:c + 1], scale=1.0),
                     r=[B_psZ[cb], B_const], w=[B_cat[0]])
                P.op("act", lambda e: e.activation(out=cat[:, 1, :], in_=psZ[cb][:, 0:T], func=AF.Square,
                                                   bias=convb[:, c:c + 1], scale=1.0),
                     r=[B_psZ[cb], B_const], w=[B_cat[1]])
                P.op("dve", lambda e: e.tensor_copy(out=ucall[:, c, :], in_=cat[:, 0, :]),
                     r=[B_cat[0]], w=[B_pA[c]])
                P.op("pe", lambda e: e.matmul(
                    psS[:, :], lhsT=ones_f[:], rhs=cat[:].rearrange("p a t -> p (a t)"),
                    start=(c == 0), stop=(c == 7)), r=[B_cat, B_const], w=[B_psS])

            for c in range(8):
                add(25, lambda c=c: conf_chunk(c))

            def ln_stats():
                P.op("dve", lambda e: e.tensor_scalar(out=lnm, in0=psS[:, 0:T], scalar1=1.0 / 1024, scalar2=None,
                                                      op0=ALU.mult), r=[B_psS], w=[B_lnm])
                P.op("dve", lambda e: e.tensor_tensor(out=tmp1, in0=lnm, in1=lnm, op=ALU.mult),
                     r=[B_lnm], w=[B_tmp1])
                P.op("dve", lambda e: e.scalar_tensor_tensor(out=tmp2, in0=psS[:, T:2 * T], scalar=1.0 / 1024,
                                                             in1=tmp1, op0=ALU.mult, op1=ALU.subtract),
                     r=[B_psS, B_tmp1], w=[B_tmp2])
                P.op("act", lambda e: e.activation(out=tmp2, in_=tmp2, func=AF.Sqrt, bias=LN_EPS, scale=1.0),
                     r=[B_tmp2], w=[B_tmp2])
                P.op("dve", lambda e: e.reciprocal(out=lnr, in_=tmp2), r=[B_tmp2], w=[B_lnr])

            add(4, ln_stats)

            def ln_apply(c):
                P.op("dve", lambda e: e.tensor_tensor(out=tmp1, in0=ucall[:, c, :], in1=lnm, op=ALU.subtract),
                     r=[B_pA[c], B_lnm], w=[B_tmp1])
                P.op("dve", lambda e: e.tensor_tensor(out=tmp1, in0=tmp1, in1=lnr, op=ALU.mult),
                     r=[B_tmp1, B_lnr], w=[B_tmp1])
                P.op("act", lambda e: e.activation(out=usT[:, c, :], in_=tmp1, func=AF.Silu,
                                                   scale=lng[:, c:c + 1], bias=lnb[:, c:c + 1]),
                     r=[B_tmp1, B_const], w=[B_usT[c]])

            for c in range(8):
                add(2, lambda c=c: ln_apply(c))

            def short_chunk(c):
                gb = g0 + 16 + 3 * c
                zx, Bzx = zmm(gb, N, hT, B_hT)
                zc, Bzc = zmm(gb + 1, N, hT, B_hT)
                P.op("act", lambda e: e.activation(out=xs_sb[:], in_=zx[:, 0:N], func=AF.Copy), r=[Bzx], w=[B_xs])
                P.op("dve", lambda e: e.tensor_tensor(out=vv[:], in0=zc[:, 0:N], in1=xs_sb[:], op=ALU.mult),
                     r=[Bzc, B_xs], w=[B_vv])
                zb, Bzb = zmm(gb + 2, N, hT, B_hT)
                if edgeL:
                    P.op("dve", lambda e: e.tensor_scalar(
                        out=vv[:, 0:HALO], in0=vv[:, 0:HALO], scalar1=em[:, 2 * seg:2 * seg + 1], scalar2=None,
                        op0=ALU.mult), r=[B_vv, B_const], w=[B_vv])
                if edgeR:
                    P.op("dve", lambda e: e.tensor_scalar(
                        out=vv[:, HALO + T:N], in0=vv[:, HALO + T:N], scalar1=em[:, 2 * seg + 1:2 * seg + 2],
                        scalar2=None, op0=ALU.mult), r=[B_vv, B_const], w=[B_vv])
                P.op("dve", lambda e: e.tensor_scalar(
                    out=acc[:], in0=vv[:, HALO - 1:HALO - 1 + T], scalar1=shortw[:, c, 0:1], scalar2=None,
                    op0=ALU.mult), r=[B_vv, B_const], w=[B_acc])
                for k in (1, 2):
                    P.op("dve", lambda e, k=k: e.scalar_tensor_tensor(
                        out=acc[:], in0=vv[:, HALO - 1 + k:HALO - 1 + k + T], scalar=shortw[:, c, k:k + 1],
                        in1=acc[:], op0=ALU.mult, op1=ALU.add), r=[B_vv, B_acc, B_const], w=[B_acc])
                P.op("dve", lambda e: e.tensor_tensor(
                    out=usT[:, 8 + c, :], in0=zb[:, HALO:HALO + T], in1=acc[:], op=ALU.mult),
                     r=[Bzb, B_acc], w=[B_usT[8 + c]])

            for c in range(8):
                add(12, lambda c=c: short_chunk(c))

            def wout_chunk(dc):
                mp, Bmp = zmm(g0 + 40 + dc, T, usT, B_usT)
                mi = dc % 2
                psX, B_psX = psXr[dc % 2], B_psXr[dc % 2]
                P.op("act", lambda e: e.activation(
                    out=mTs[:, mi, :], in_=mp[:, 0:T], func=AF.Copy, scale=fv(seg, 2, dc)),
                     r=[Bmp, B_featv], w=[B_mTs[mi]])
                for s in range(NS):
                    P.op("pe", lambda e, s=s: e.transpose(
                        psX[:, s * 128:(s + 1) * 128], mTs[:, mi, s * 128:(s + 1) * 128], ident_f[:]),
                         r=[B_mTs[mi], B_const], w=[B_psX])
                for s in range(NS):
                    P.op("dve", lambda e, s=s: e.tensor_tensor(
                        out=xt[:, s, dc * 128:(dc + 1) * 128], in0=xt[:, s, dc * 128:(dc + 1) * 128],
                        in1=psX[:, s * 128:(s + 1) * 128], op=ALU.add),
                         r=[B_xt[s], B_psX], w=[B_xt[s]])

            for dc in range(KC):
                add(4, lambda dc=dc: wout_chunk(dc))

            def st_3(s):
                j = ti * NS + s
                P.dma("sp", lambda e: e.dma_start(out=yout[y0 + 128 * s:y0 + 128 * (s + 1), :], in_=xt[:, s, :]),
                      d_st[s], r=[B_xt[s]], w=[B_ydram[j % RJ]])
                P.op("act", lambda e: e.activation(
                    out=sqa[:], in_=xt[:, s, :], func=AF.Square, accum_out=stt[:, 4 + s:5 + s]),
                     r=[B_xt[s]], w=[B_sqa, B_stt])
                rstd_from_ss(4 + s, B_stt)
                P.op("dve", lambda e: e.scalar_tensor_tensor(
                    out=h2f, in0=xt[:, s, :], scalar=stt[:, 4 + s:5 + s], in1=g2b[:],
                    op0=ALU.mult, op1=ALU.mult), r=[B_xt[s], B_stt, B_bc[id(g2b)]], w=[B_h2f])
                P.op("dve", lambda e: e.tensor_tensor(
                    out=h2b[:, hp_, s, :], in0=h2f, in1=b2b[:], op=ALU.add),
                     r=[B_h2f, B_bc[id(b2b)]], w=[B_h2b[hp_][s]])
                dsts = [(lambda kc: h2T[:, kc, 128 * s:128 * (s + 1)], 0, 128)]
                transposes_bf(h2b[:, hp_, s, :], 128, dsts, B_h2b[hp_][s], B_h2T)

            for s in range(NS):
                add(12, lambda s=s: st_3(s))

            def q_chunk(qc):
                qp, Bqp = zmm(g0 + 56 + qc, T, h2T, B_h2T)
                mi = qc % 2
                psX, B_psX = psXr[qc % 2], B_psXr[qc % 2]
                P.op("act", lambda e: e.activation(out=mTs[:, mi, :], in_=qp[:, 0:T], func=AF.Copy),
                     r=[Bqp], w=[B_mTs[mi]])
                for s in range(NS):
                    P.op("pe", lambda e, s=s: e.matmul(
                        psX[:, s * 128:(s + 1) * 128], lhsT=mTs[:, mi, s * 128:(s + 1) * 128],
                        rhs=keysT[:, qc, :], start=True, stop=True),
                         r=[B_mTs[mi], B_const], w=[B_psX])
                if qc % 2 == 0:
                    P.op("dve", lambda e: e.tensor_copy(
                        out=Sall[:, :, qc, :], in_=psX[:, 0:256].rearrange("p (s k) -> p s k", s=2)),
                         r=[B_psX], w=[B_S(0, qc), B_S(1, qc)])
                else:
                    P.op("act", lambda e: e.activation(
                        out=Sall[:, :, qc, :], in_=psX[:, 0:256].rearrange("p (s k) -> p s k", s=2), func=AF.Copy),
                         r=[B_psX], w=[B_S(0, qc), B_S(1, qc)])

            for qc in range(16):
                add(4, lambda qc=qc: q_chunk(qc))

            def retr_topk(s, hps):
                B_Ss = [B_S(s, q) for q in range(16)]
                for hp in hps:
                    Sv = Sall[:, s, hp, :]
                    P.op("dve", lambda e, hp=hp, Sv=Sv: e.max(out=tv[:, hp, 0:8], in_=Sv), r=[B_Ss], w=[B_tv])
                    P.op("dve", lambda e, hp=hp, Sv=Sv: e.match_replace(
                        out=work[:, 0:128], in_to_replace=tv[:, hp, 0:8], in_values=Sv, imm_value=-1e30),
                         r=[B_Ss, B_tv], w=[B_work])
                    P.op("dve", lambda e, hp=hp: e.max(out=tv[:, hp, 8:16], in_=work[:, 0:128]),
                         r=[B_work], w=[B_tv])
                    P.op("dve", lambda e, hp=hp, Sv=Sv: e.max_index(
                        out=ti_[:, hp, 0:8], in_max=tv[:, hp, 0:8], in_values=Sv), r=[B_Ss, B_tv], w=[B_ti])
                    P.op("dve", lambda e, hp=hp, Sv=Sv: e.max_index(
                        out=ti_[:, hp, 8:16], in_max=tv[:, hp, 8:16], in_values=Sv), r=[B_Ss, B_tv], w=[B_ti])

            def retr_cand(s):
                P.op("dve", lambda e: e.tensor_copy(out=tif[:], in_=ti_[:]), r=[B_ti], w=[B_tif])
                tv4 = tv[:].rearrange("p (h t) k -> p h t k", t=2)
                P.op("dve", lambda e: e.tensor_tensor(
                    out=cand.rearrange("p h (a b) -> p h a b", a=16),
                    in0=tv4[:, :, 0, :].unsqueeze(3).to_broadcast([128, 8, 16, 16]),
                    in1=tv4[:, :, 1, :].unsqueeze(2).to_broadcast([128, 8, 16, 16]), op=ALU.add),
                     r=[B_tv], w=[B_cand])
                for h in range(8):
                    P.op("dve", lambda e, h=h: e.max(out=cv[:, h, 0:8], in_=cand[:, h, :]), r=[B_cand], w=[B_cv])
                    P.op("dve", lambda e, h=h: e.match_replace(
                        out=work[:, :], in_to_replace=cv[:, h, 0:8], in_values=cand[:, h, :], imm_value=-1e30),
                         r=[B_cand, B_cv], w=[B_work])
                    P.op("dve", lambda e, h=h: e.max(out=cv[:, h, 8:16], in_=work[:, :]), r=[B_work], w=[B_cv])
                    P.op("dve", lambda e, h=h: e.max_index(
                        out=cpos[:, h, 0:8], in_max=cv[:, h, 0:8], in_values=cand[:, h, :]),
                         r=[B_cand, B_cv], w=[B_cpos])
                    P.op("dve", lambda e, h=h: e.max_index(
                        out=cpos[:, h, 8:16], in_max=cv[:, h, 8:16], in_values=cand[:, h, :]),
                         r=[B_cand, B_cv], w=[B_cpos])

            def retr_idx(s):
                j = ti * NS + s
                jr = j % RJ
                cposf = cpos[:].rearrange("p h k -> p (h k)")
                P.op("dve", lambda e: e.tensor_single_scalar(out=cab[:, 0, :], in_=cposf, scalar=4,
                                                             op=ALU.logical_shift_right), r=[B_cpos], w=[B_cab])
                P.op("dve", lambda e: e.tensor_single_scalar(out=cab[:, 1, :], in_=cposf, scalar=15,
                                                             op=ALU.bitwise_and), r=[B_cpos], w=[B_cab])
                P.op("dve", lambda e: e.tensor_copy(out=cabf[:], in_=cab[:]), r=[B_cab], w=[B_cabf])
                tif4 = tif[:].rearrange("p (h t) k -> p h t k", t=2)
                for t_ in range(2):
                    P.op("dve", lambda e, t_=t_: e.tensor_tensor(
                        out=oh,
                        in0=cabf[:, t_, :].rearrange("p (h k) -> p h k", h=8).unsqueeze(3).to_broadcast(
                            [128, 8, 16, 16]),
                        in1=iota16[:].unsqueeze(1).unsqueeze(1).to_broadcast([128, 8, 16, 16]),
                        op=ALU.is_equal), r=[B_cabf, B_const], w=[B_oh])
                    P.op("dve", lambda e, t_=t_: e.tensor_tensor(
                        out=oh, in0=oh, in1=tif4[:, :, t_, :].unsqueeze(2).to_broadcast([128, 8, 16, 16]),
                        op=ALU.mult), r=[B_oh, B_tif], w=[B_oh])
                    P.op("dve", lambda e, t_=t_: e.tensor_reduce(
                        out=isel[:, t_, :].rearrange("p (h k) -> p h k", h=8), in_=oh, axis=AX.X, op=ALU.add),
                         r=[B_oh], w=[B_isel])
                P.op("dve", lambda e: e.scalar_tensor_tensor(
                    out=eidxf[:], in0=isel[:, 0, :], scalar=128.0, in1=isel[:, 1, :], op0=ALU.mult, op1=ALU.add),
                     r=[B_isel], w=[B_eidxf])
                P.op("dve", lambda e: e.tensor_copy(out=eidx[:, jr, :], in_=eidxf[:]), r=[B_eidxf], w=[B_eidx[jr]])
                gv_ = gate[:, jr, :].rearrange("p (h k) -> p h k", h=8)
                P.op("dve", lambda e: e.tensor_tensor(
                    out=gv_, in0=cv[:], in1=cv[:, :, 0:1].to_broadcast([128, 8, 16]), op=ALU.subtract),
                     r=[B_cv], w=[B_gate[jr]])
                P.op("act", lambda e: e.activation(out=gv_, in_=gv_, func=AF.Exp), r=[B_gate[jr]], w=[B_gate[jr]])
                P.op("dve", lambda e: e.tensor_reduce(out=gsum[:], in_=gv_, axis=AX.X, op=ALU.add),
                     r=[B_gate[jr]], w=[B_gsum])
                P.op("dve", lambda e: e.reciprocal(out=gsum[:], in_=gsum[:]), r=[B_gsum], w=[B_gsum])
                P.op("dve", lambda e: e.tensor_tensor(
                    out=gv_, in0=gv_, in1=gsum[:].unsqueeze(2).to_broadcast([128, 8, 16]), op=ALU.mult),
                     r=[B_gate[jr], B_gsum], w=[B_gate[jr]])

            for s in range(NS):
                add(12, lambda s=s: retr_topk(s, range(0, 8)))
                add(12, lambda s=s: retr_topk(s, range(8, 16)))
                add(20, lambda s=s: retr_cand(s))
                add(20, lambda s=s: retr_idx(s))
            return units

        gcnt = {"u": 0, "v": 0, "d": 0}

        def U_slot(j, slot):
            jr = j % RJ
            hp_, s = (j // NS) % 2, j % NS
            gi = gcnt["u"] % NG
            gcnt["u"] += 1
            P.dma("pool", lambda e: e.indirect_dma_start(
                out=GU[gi][:, :], out_offset=None, in_=ubf,
                in_offset=bass.IndirectOffsetOnAxis(ap=eidx[:, jr, slot:slot + 1], axis=0)),
                  d_GU[gi], r=[B_eidx[jr], B_ubf], w=[B_GU[gi]])
            P.op("dve", lambda e: e.scalar_tensor_tensor(
                out=sqd[:], in0=GU[gi][:], scalar=1.0, in1=h2b[:, hp_, s, :], op0=ALU.mult, op1=ALU.mult,
                accum_out=actv[:, jr, slot:slot + 1]), r=[B_GU[gi], B_h2b[hp_][s]], w=[B_sqd, B_actv[jr]])

        def U_finish(j):
            jr = j % RJ
            P.op("act", lambda e: e.activation(out=coef[:, jr, :], in_=actv[:, jr, :], func=AF.Gelu),
                 r=[B_actv[jr]], w=[B_coef[jr]])
            P.op("dve", lambda e: e.tensor_tensor(out=coef[:, jr, :], in0=coef[:, jr, :], in1=gate[:, jr, :],
                                                  op=ALU.mult), r=[B_coef[jr], B_gate[jr]], w=[B_coef[jr]])

        def V_slot(j, slot):
            jr = j % RJ
            gi = gcnt["v"] % NG
            gcnt["v"] += 1
            di = gcnt["d"] % 2
            gcnt["d"] += 1
            P.dma("pool", lambda e: e.indirect_dma_start(
                out=GV[gi][:, :], out_offset=None, in_=vbf,
                in_offset=bass.IndirectOffsetOnAxis(ap=eidx[:, jr, slot:slot + 1], axis=0)),
                  d_GV[gi], r=[B_eidx[jr], B_vbf], w=[B_GV[gi]])
            P.op("act", lambda e: e.activation(
                out=diag[:, di, :], in_=ident_f[:], func=AF.Copy, scale=coef[:, jr, slot:slot + 1]),
                 r=[B_coef[jr], B_const], w=[B_diag[di]])
            for n in range(4):
                P.op("pe", lambda e, n=n: e.matmul(
                    psA[n][:, :], lhsT=diag[:, di, :], rhs=GV[gi][:, n * 512:(n + 1) * 512],
                    start=(slot == 0), stop=(slot == 127)),
                     r=[B_diag[di], B_GV[gi]], w=[B_psA[n]])

        def V_start(j):
            ti = j // NS
            s = j % NS
            y0 = tiles[ti][4]
            P.dma("act", lambda e: e.dma_start(out=xfin[:], in_=yout[y0 + 128 * s:y0 + 128 * (s + 1), :]),
                  d_rl, r=[B_ydram[j % RJ]], w=[B_xfin])

        def V_finish(j):
            ti = j // NS
            s = j % NS
            seg, it, ntiles, r0, y0 = tiles[ti]
            for n in range(4):
                P.op("dve", lambda e, n=n: e.tensor_tensor(
                    out=tmpo[:], in0=psA[n][:, :], in1=gt2b[:, n * 512:(n + 1) * 512], op=ALU.mult),
                     r=[B_psA[n], B_bc[id(gt2b)]], w=[B_tmpo])
                P.op("dve", lambda e, n=n: e.tensor_tensor(
                    out=xfin[:, n * 512:(n + 1) * 512], in0=xfin[:, n * 512:(n + 1) * 512], in1=tmpo[:],
                    op=ALU.add), r=[B_tmpo, B_xfin], w=[B_xfin])
            P.op("act", lambda e: e.activation(
                out=sqa[:], in_=xfin[:], func=AF.Square, accum_out=stt[:, 8:9]),
                 r=[B_xfin], w=[B_sqa, B_stt2])
            rstd_from_ss(8, B_stt2)
            P.op("dve", lambda e: e.scalar_tensor_tensor(
                out=xfin[:], in0=xfin[:], scalar=stt[:, 8:9], in1=gfb[:],
                op0=ALU.mult, op1=ALU.mult), r=[B_xfin, B_stt2, B_misc], w=[B_xfin])
            P.dma("act", lambda e: e.dma_start(out=yout[y0 + 128 * s:y0 + 128 * (s + 1), :], in_=xfin[:]),
                  d_y, r=[B_xfin], w=[B_ydram[j % RJ]])
            if j + 1 < NJ and (j + 1) % NS == 0 and tiles[(j + 1) // NS][1] == 0:
                bcast_tiles(tiles[(j + 1) // NS][0], ((gt2b, 5),), psA[0:2], B_psA[0:2])

        for (c_, fn) in M_units(0):
            fn()

        class MSched:
            def __init__(self, fifo, nhalf):
                self.items = fifo
                n = len(fifo)
                self.deps = [None] * n
                self.when = [None] * n
                self.first = 0
                writer, readers = {}, {}
                for i, (kind, eng, fn, dsem, r, w) in enumerate(fifo):
                    d = set()
                    rr, ww = _flat(r), _flat(w)
                    for b_ in rr:
                        if id(b_) in writer:
                            d.add(writer[id(b_)])
                    for b_ in ww:
                        if id(b_) in writer:
                            d.add(writer[id(b_)])
                        d.update(readers.get(id(b_), ()))
                    d.discard(i)
                    self.deps[i] = d
                    for b_ in rr:
                        readers.setdefault(id(b_), set()).add(i)
                    for b_ in ww:
                        writer[id(b_)] = i
                        readers[id(b_)] = set()
                tot = {e: 0.0 for e in Prog.ENGS}
                for it_ in fifo:
                    tot[it_[1]] += Prog.COST[it_[1]]
                tgt = max(nhalf * 0.72, 1.0)
                self.budget = {e: 1.25 * tot[e] / tgt + Prog.COST[e] for e in Prog.ENGS}
                self.t = 0

            def empty(self):
                return self.first >= len(self.items)

            def step(self, flush=False):
                used = {e: 0.0 for e in Prog.ENGS}
                n = len(self.items)
                i = self.first
                scanned = 0
                while i < n and scanned < 700:
                    if self.when[i] is None:
                        scanned += 1
                        kind, eng, fn, dsem, r, w = self.items[i]
                        ok = flush or used[eng] + Prog.COST[eng] <= self.budget[eng]
                        if ok:
                            for d in self.deps[i]:
                                wd = self.when[d]
                                if wd is None:
                                    ok = False
                                    break
                                if not flush and self.items[d][1] != eng and wd >= self.t:
                                    ok = False
                                    break
                        if ok:
                            self.when[i] = self.t
                            used[eng] += Prog.COST[eng]
                            if kind == "op":
                                P._add(Op(eng, fn), r, w)
                            else:
                                o = Op(eng, fn)
                                o.dsem = dsem
                                o.needed = True
                                o.inc = 16
                                P._add(o, r, w)
                    i += 1
                while self.first < n and self.when[self.first] is not None:
                    self.first += 1
                self.t += 1

        bcast_tiles(tiles[0][0], ((gt2b, 5),), psA[0:2], B_psA[0:2])
        sched = None
        for p in range(NJ + 1):
            ju = p if p < NJ else None
            jv = p - 1 if p >= 1 else None
            if p % NS == 0:
                tnext = p // NS + 1
                assert sched is None or sched.empty()
                sched = None
                if tnext < NT:
                    fifo = []
                    P.defer = fifo
                    for (c_, fn) in M_units(tnext):
                        fn()
                    P.defer = None
                    sched = MSched(fifo, NS * 128 * 2)
            if jv is not None:
                V_start(jv)
            for slot in range(128):
                if ju is not None:
                    U_slot(ju, slot)
                if sched is not None:
                    sched.step()
                if jv is not None:
                    V_slot(jv, slot)
                if sched is not None:
                    sched.step()
            if sched is not None and p % NS == NS - 1:
                while not sched.empty():
                    sched.step(flush=True)
            if ju is not None:
                U_finish(ju)
            if jv is not None:
                V_finish(jv)

        lasts = {}
        for en in ("sp", "act"):
            for o in P.ops[en]:
                if o.dsem is not None and (o.dsem is d_y or o.dsem in d_st):
                    lasts[id(o.dsem)] = o
        fin = Op("sp", lambda e: e.nop())
        fin.deps = list(lasts.values())
        P.ops["sp"].append(fin)

        with nc.Block() as block2:
            P.resolve(engsem)

            @block2.sync
            def _(e):
                P.emit("sp", e)

            @block2.scalar
            def _(e):
                P.emit("act", e)

            @block2.vector
            def _(e):
                P.emit("dve", e)

            @block2.tensor
            def _(e):
                P.emit("pe", e)

            @block2.gpsimd
            def _(e):
                P.emit("pool", e)
    return nc


def _block_order():
    order = []
    for c in range(8):
        order += [c, 8 + c]
    for c in range(8):
        order += [32 + c, 24 + c, 16 + c]
    return order


def prep_shared(w_ada, b_ada, g_norm1, w_in, conv_w, conv_b, ln_g, ln_b, short_w, w_out, g_norm2, w_query,
                sub_keys, expert_u, expert_v, g_final):
    f = np.float32
    sh = {}
    wa = np.asarray(w_ada[0], dtype=f)
    sh["wada_blocks"] = np.ascontiguousarray(wa.reshape(KC, 128, 96, 128).transpose(2, 1, 0, 3))
    sh["badaT"] = np.ascontiguousarray(np.asarray(b_ada[0], dtype=f).reshape(96, 128).T)
    sh["g1T"] = np.ascontiguousarray(np.asarray(g_norm1[0], dtype=f).reshape(KC, 128).T)
    sh["g2T"] = np.ascontiguousarray(np.asarray(g_norm2[0], dtype=f).reshape(KC, 128).T)
    sh["gfin_b"] = np.ascontiguousarray(np.broadcast_to(np.asarray(g_final)[None, :], (128, D)), dtype=f)
    wi = np.asarray(w_in[0], dtype=f)
    wi_blocks = wi.reshape(KC, 128, 40, 128).transpose(2, 1, 0, 3)
    wi_blocks = wi_blocks[_block_order()]
    wo_blocks = np.asarray(w_out[0], dtype=f).reshape(KC, 128, 16, 128).transpose(2, 1, 0, 3)
    wq_blocks = np.asarray(w_query[0], dtype=f).reshape(KC, 128, 16, 128).transpose(2, 1, 0, 3)
    sh["w_blocks"] = np.ascontiguousarray(np.concatenate([wi_blocks, wo_blocks, wq_blocks], axis=0))
    sh["convw"] = np.ascontiguousarray(np.asarray(conv_w[0], dtype=f).reshape(31, 8, 128).transpose(2, 1, 0))
    sh["convb"] = np.ascontiguousarray(np.asarray(conv_b[0], dtype=f).reshape(8, 128).T)
    sh["lng"] = np.ascontiguousarray(np.asarray(ln_g[0], dtype=f).reshape(8, 128).T)
    sh["lnb"] = np.ascontiguousarray(np.asarray(ln_b[0], dtype=f).reshape(8, 128).T)
    sh["shortw"] = np.ascontiguousarray(np.asarray(short_w[0], dtype=f).reshape(3, 8, 128).transpose(2, 1, 0))
    sk = np.asarray(sub_keys[0], dtype=f)
    sh["keysT"] = np.ascontiguousarray(sk.reshape(16, 128, 128).transpose(2, 0, 1))
    sh["expert_u"] = np.ascontiguousarray(expert_u[0], dtype=f)
    sh["expert_v"] = np.ascontiguousarray(expert_v[0], dtype=f)
    sh["ident"] = np.eye(128, dtype=f)
    sh["iota16"] = np.ascontiguousarray(np.broadcast_to(np.arange(16, dtype=f)[None, :], (128, 16)))
    return sh


def prep_core(xp_seq, xs_seq, s0, L1, cp, cs):
    f = np.float32
    L0 = xp_seq.shape[0]
    S = xs_seq.shape[0]
    xin = np.zeros((L0 + L1 + 4 * HALO, D), dtype=f)
    xin[HALO:HALO + L0] = xp_seq
    base = L0 + 2 * HALO
    lo = max(0, s0 - HALO)
    hi = min(S, s0 + L1 + HALO)
    xin[base + (lo - (s0 - HALO)):base + (hi - (s0 - HALO))] = xs_seq[lo:hi]
    emask = np.zeros((128, 4), dtype=f)
    emask[:, 2] = 1.0 if s0 > 0 else 0.0
    emask[:, 3] = 1.0 if s0 + L1 < S else 0.0
    cc = np.stack([np.asarray(cp, dtype=f), np.asarray(cs, dtype=f)], axis=-1)
    cT = np.ascontiguousarray(cc.reshape(KC, 128, 2).transpose(1, 0, 2))
    return {"xin": xin, "emask": emask, "cT": cT}


_NC_CACHE = {}


def kernel(x_prompt, x_sample, c_prompt, c_sample, w_ada, b_ada, g_norm1, w_in, conv_w, conv_b,
           ln_g, ln_b, short_w, w_out, g_norm2, w_query, sub_keys, expert_u, expert_v, g_final):
    x_prompt = np.asarray(x_prompt)
    x_sample = np.asarray(x_sample)
    c_prompt = np.asarray(c_prompt)
    c_sample = np.asarray(c_sample)
    ncores = 8
    B0, L0, _ = x_prompt.shape
    B1, S1, _ = x_sample.shape
    per = ncores // B1
    L1 = S1 // per
    sh = prep_shared(*(np.asarray(a) for a in (w_ada, b_ada, g_norm1, w_in, conv_w, conv_b, ln_g, ln_b, short_w,
                                               w_out, g_norm2, w_query, sub_keys, expert_u, expert_v, g_final)))
    in_maps = []
    for c in range(ncores):
        b1 = c // per
        q = c % per
        m = dict(sh)
        m.update(prep_core(x_prompt[c], x_sample[b1], q * L1, L1, c_prompt[c], c_sample[b1]))
        in_maps.append(m)
    key = (L0, L1)
    if key not in _NC_CACHE:
        _NC_CACHE[key] = build_program(L0, L1)
    nc = _NC_CACHE[key]
    res = run_bass_kernel_spmd(nc, in_maps, core_ids=list(range(ncores)))
    y_prompt = np.empty((B0, L0, D), dtype=np.float32)
    y_sample = np.empty((B1, S1, D), dtype=np.float32)
    for c in range(ncores):
        y = np.asarray(res.results[c]["yout"])
        y_prompt[c] = y[:L0]
        y_sample[c // per, (c % per) * L1:(c % per + 1) * L1] = y[L0:]
    return (y_prompt, y_sample)
```

```python
import contextlib
import numpy as np
import concourse.bass as bass
import concourse.mybir as mybir
from concourse.bass_utils import run_bass_kernel_spmd

F32 = mybir.dt.float32
BF16 = mybir.dt.bfloat16
I32 = mybir.dt.int32
U32 = mybir.dt.uint32
ALU = mybir.AluOpType
AF = mybir.ActivationFunctionType
AX = mybir.AxisListType

D = 2048
KC = 16
DIN = 5120
NEXP = 16384
T = 256
NS = T // 128
HALO = 15
N = T + 2 * HALO
RMS_EPS = 1e-6
LN_EPS = 1e-5
NBLK = 72


class Buf:
    __slots__ = ("name", "writer", "readers")

    def __init__(self, name):
        self.name = name
        self.writer = None
        self.readers = {}


class DSem:
    __slots__ = ("sem", "count")

    def __init__(self, sem):
        self.sem = sem
        self.count = 0


class Op:
    __slots__ = ("eng", "fn", "deps", "needed", "sem", "val", "inc", "dsem")

    def __init__(self, eng, fn):
        self.eng = eng
        self.fn = fn
        self.deps = []
        self.needed = False
        self.sem = None
        self.val = 0
        self.inc = 1
        self.dsem = None


def _flat(x):
    out = []
    for b in x:
        if isinstance(b, (list, tuple)):
            out.extend(_flat(b))
        elif b is not None:
            out.append(b)
    return out


def _writers(b):
    w = b.writer
    if w is None:
        return []
    return w if isinstance(w, list) else [w]


class Prog:
    ENGS = ("sp", "act", "dve", "pe", "pool")

    def __init__(self):
        self.ops = {e: [] for e in self.ENGS}
        self.defer = None

    COST = {"dve": 0.6, "pe": 0.27, "act": 0.45, "sp": 0.0, "pool": 0.0}

    def drain(self, fifo, budget):
        used = {e: 0.0 for e in self.ENGS}
        while fifo:
            eng0 = fifo[0][1]
            if used[eng0] > 0 and used[eng0] + self.COST[eng0] > budget[eng0]:
                break
            kind, eng, fn, dsem, r, w = fifo.pop(0)
            used[eng] += self.COST[eng]
            if kind == "op":
                self._add(Op(eng, fn), r, w)
            else:
                o = Op(eng, fn)
                o.dsem = dsem
                o.needed = True
                o.inc = 16
                self._add(o, r, w)

    def _add(self, op, reads, writes):
        reads = _flat(reads)
        writes = _flat(writes)
        deps = {}
        for b in reads:
            for wr in _writers(b):
                deps[id(wr)] = wr
        for b in writes:
            for wr in _writers(b):
                deps[id(wr)] = wr
            for r in b.readers.values():
                deps[id(r)] = r
        for d in deps.values():
            if d is op:
                continue
            if op.eng == "pe" and d.eng == "pe" and d.dsem is None:
                continue
            op.deps.append(d)
            d.needed = True
        for b in reads:
            key = ("d", id(op.dsem)) if op.dsem is not None else op.eng
            b.readers[key] = op
        for b in writes:
            b.writer = op
            b.readers = {}
        self.ops[op.eng].append(op)
        return op

    def op(self, eng, fn, r=(), w=()):
        if self.defer is not None:
            self.defer.append(("op", eng, fn, None, r, w))
            return None
        return self._add(Op(eng, fn), r, w)

    def dma(self, eng, fn, dsem, r=(), w=()):
        if self.defer is not None:
            self.defer.append(("dma", eng, fn, dsem, r, w))
            return None
        o = Op(eng, fn)
        o.dsem = dsem
        o.needed = True
        o.inc = 16
        return self._add(o, r, w)

    def resolve(self, engsem):
        for e in self.ENGS:
            cnt = 0
            for o in self.ops[e]:
                if o.dsem is not None:
                    o.dsem.count += 16
                    o.sem = o.dsem.sem
                    o.val = o.dsem.count
                elif o.needed:
                    cnt += 1
                    o.sem = engsem[e]
                    o.val = cnt

    def emit(self, eng_name, eng):
        waited = {}
        for o in self.ops[eng_name]:
            need = {}
            for d in o.deps:
                k = id(d.sem)
                if k not in need or need[k][1] < d.val:
                    need[k] = (d.sem, d.val)
            for k, (sem, val) in need.items():
                if waited.get(k, 0) >= val:
                    continue
                eng.wait_ge(sem, val)
                waited[k] = val
            ins = o.fn(eng)
            if o.needed:
                ins.then_inc(o.sem, o.inc)

    def clear(self):
        self.ops = {e: [] for e in self.ENGS}


def build_program(L0, L1, debug=False):
    assert L0 % T == 0 and L1 % T == 0
    nc = bass.Bass("TRN2", target_bir_lowering=False)
    RIN = L0 + L1 + 4 * HALO
    LT = L0 + L1

    def din(name, shape, dt=F32):
        return nc.dram_tensor(name, list(shape), dt, kind="ExternalInput").ap()

    xin = din("xin", [RIN, D])
    cT = din("cT", [128, KC, 2])
    wada_blocks = din("wada_blocks", [96, 128, KC, 128])
    badaT_d = din("badaT", [128, 96])
    g1T_d = din("g1T", [128, KC])
    g2T_d = din("g2T", [128, KC])
    gfin_b = din("gfin_b", [128, D])
    w_blocks = din("w_blocks", [NBLK, 128, KC, 128])
    convw_d = din("convw", [128, 8, 31])
    convb_d = din("convb", [128, 8])
    lng_d = din("lng", [128, 8])
    lnb_d = din("lnb", [128, 8])
    shortw_d = din("shortw", [128, 8, 3])
    keysT_d = din("keysT", [128, 16, 128])
    eu = din("expert_u", [NEXP, D])
    ev = din("expert_v", [NEXP, D])
    ident_d = din("ident", [128, 128])
    iota16_d = din("iota16", [128, 16])
    em_d = din("emask", [128, 4])
    yout = nc.dram_tensor("yout", [LT, D], F32, kind="ExternalOutput").ap()
    wbf = nc.dram_tensor("wbf_scratch", [NBLK, 128, KC * 128], BF16).ap()
    ubf = nc.dram_tensor("ubf_scratch", [NEXP, D], BF16).ap()
    vbf = nc.dram_tensor("vbf_scratch", [NEXP, D], BF16).ap()
    dbg = {}
    if debug:
        for nm, shp, dt in (("dbg_h2", [128, D], F32), ("dbg_eidx", [128, 128], I32),
                            ("dbg_gate", [128, 128], F32), ("dbg_act", [128, 128], F32),
                            ("dbg_x1", [128, D], F32), ("dbg_S", [128, 2048], F32),
                            ("dbg_us", [128, 16 * T], F32), ("dbg_peer", [128, D], F32)):
            dbg[nm] = nc.dram_tensor(nm, shp, dt, kind="ExternalOutput").ap()

    es = contextlib.ExitStack()
    with es:
        def sb(name, shape, dt=F32):
            return es.enter_context(nc.sbuf_tensor(name + "_sb", list(shape), dt))

        def ps(name, shape, dt=F32):
            return es.enter_context(nc.psum_tensor(name + "_ps", list(shape), dt))

        def newsem(name):
            return es.enter_context(nc.semaphore(name))

        engsem = {e: newsem("s_" + e) for e in Prog.ENGS}
        _dsn = [0]

        def dsem():
            _dsn[0] += 1
            return DSem(newsem("d%d" % _dsn[0]))

        P = Prog()

        ident_f = sb("ident_f", [128, 128]);
        ident_b = sb("ident_b", [128, 128], BF16)
        ones_f = sb("ones_f", [128, 128])
        iota16 = sb("iota16", [128, 16])
        em = sb("em", [128, 4])
        featv = sb("featv", [128, KC, 12])
        featmod = sb("featmod", [128, 96, 2])
        badaT = sb("badaT_s", [128, 96])
        g1T = sb("g1T_s", [128, KC])
        g2T = sb("g2T_s", [128, KC])
        cTs = sb("cTs", [128, KC, 2])
        sT = sb("sT", [128, KC, 2])
        g2b = sb("g2b", [128, D]);
        b2b = sb("b2b", [128, D]);
        gt2b = sb("gt2b", [128, D]);
        gfb = sb("gfb", [128, D])
        convw = sb("convw_s", [128, 8, 31]);
        convb = sb("convb_s", [128, 8])
        lng = sb("lng_s", [128, 8]);
        lnb = sb("lnb_s", [128, 8]);
        shortw = sb("shortw_s", [128, 8, 3])
        keysT = sb("keysT_s", [128, 16, 128])

        B_const = Buf("const")
        B_featv = Buf("featv")

        psA = [ps("psA%d" % i, [128, 512]) for i in range(4)]
        B_psA = [Buf("psA%d" % i) for i in range(4)]
        psZ = [ps("psZ%d" % i, [128, 512]) for i in range(2)]
        B_psZ = [Buf("psZ%d" % i) for i in range(2)]
        psZ_, B_psZ_ = psZ, B_psZ
        psX = ps("psX", [128, 512])
        B_psX = Buf("psX")
        psS = ps("psS", [128, 512])
        B_psS = Buf("psS")
        psXr = [psX, psS]
        B_psXr = [B_psX, B_psS]
        psTvr = [psXr[i][:, 0:256].bitcast(BF16).rearrange("p (j t) -> p j t", j=4) for i in range(2)]
        xrc = [0]

        xt = sb("xt", [128, NS, D])
        B_xt = [Buf("xt%d" % s) for s in range(NS)]
        xfin = sb("xfin", [128, D])
        B_xfin = Buf("xfin")
        h2b = sb("h2b", [128, 2, NS, D], BF16)
        B_h2b = [[Buf("h2b_%d_%d" % (a, s)) for s in range(NS)] for a in range(2)]
        sqa = sb("sqa", [128, D], BF16)
        B_sqa = Buf("sqa")
        sqd = sb("sqd", [128, D], BF16)
        B_sqd = Buf("sqd")
        stt = sb("stt", [128, 16])
        B_stt = Buf("stt")
        B_stt2 = Buf("stt2")
        hT = sb("hT", [128, KC, N], BF16)
        B_hT = Buf("hT")
        h2T = hT
        B_h2T = B_hT
        wst = sb("wst", [128, 2, KC * 128])
        B_wst = [Buf("wst0"), Buf("wst1")]
        d_wst = [dsem(), dsem()]
        NWB = 3
        wb = sb("wb", [128, NWB, KC, 128], BF16)
        B_wb = [Buf("wb%d" % i) for i in range(NWB)]
        d_wb = [dsem() for _ in range(NWB)]
        B_wbf = Buf("wbf")
        B_ubf = Buf("ubf")
        B_vbf = Buf("vbf")
        sg = sb("sg", [128, N]);
        B_sg = Buf("sg")
        u2 = sb("u", [128, 2, N], BF16);
        B_u2 = [Buf("u0"), Buf("u1")]
        NDC = 4
        dgc = sb("dgc", [128, NDC, 128], BF16)
        B_dgc = [Buf("dgc%d" % i) for i in range(NDC)]
        dgn = [0]
        xs_sb = sb("xs_sb", [128, N]);
        B_xs = Buf("xs")
        vv = sb("vv", [128, N]);
        B_vv = Buf("vv")
        acc = sb("acc", [128, T]);
        B_acc = Buf("acc")
        poolA = sb("poolA", [128, 4096])
        B_pA = [Buf("pA%d" % i) for i in range(16)]
        ucall = poolA[:, 0:2048].rearrange("p (c t) -> p c t", c=8)
        cat = poolA[:, 2048:2560].rearrange("p (a t) -> p a t", a=2)
        B_cat = [B_pA[8], B_pA[9]]
        lnm = poolA[:, 2560:2816];
        B_lnm = B_pA[10]
        lnr = poolA[:, 2816:3072];
        B_lnr = B_pA[11]
        tmp1 = poolA[:, 3072:3328];
        B_tmp1 = B_pA[12]
        tmp2 = poolA[:, 3328:3584];
        B_tmp2 = B_pA[13]
        Sall = poolA[:].rearrange("p (s q k) -> p s q k", s=2, q=16)

        def B_S(s, q):
            return B_pA[(s * 16 + q) // 2]

        usT = sb("usT", [128, KC, T], BF16)
        B_usT = [Buf("usT%d" % i) for i in range(KC)]
        mTs = sb("mTs", [128, 2, T]);
        B_mTs = [Buf("mTs0"), Buf("mTs1")]
        work = sb("work", [128, 256]);
        B_work = Buf("work")
        tv = sb("tv", [128, 16, 16]);
        B_tv = Buf("tv")
        ti_ = sb("ti", [128, 16, 16], U32);
        B_ti = Buf("ti")
        tif = sb("tif", [128, 16, 16]);
        B_tif = Buf("tif")
        xh = wst[:, 0, :]
        B_xh = B_wst[0]
        cand = wst[:, 0, :].rearrange("p (h c) -> p h c", h=8)
        B_cand = B_wst[0]
        oh = wst[:, 1, :].rearrange("p (h a b) -> p h a b", h=8, a=16)
        B_oh = B_wst[1]
        h2f = wst[:, 1, :]
        B_h2f = B_wst[1]
        cv = sb("cv", [128, 8, 16]);
        B_cv = Buf("cv")
        cpos = sb("cpos", [128, 8, 16], U32);
        B_cpos = Buf("cpos")
        cab = sb("cab", [128, 2, 128], U32);
        B_cab = Buf("cab")
        cabf = sb("cabf", [128, 2, 128]);
        B_cabf = Buf("cabf")
        isel = sb("isel", [128, 2, 128]);
        B_isel = Buf("isel")
        eidxf = sb("eidxf", [128, 128]);
        B_eidxf = Buf("eidxf")
        RJ = 4
        eidx = sb("eidx", [128, RJ, 128], I32);
        B_eidx = [Buf("eidx%d" % i) for i in range(RJ)]
        gate = sb("gate", [128, RJ, 128]);
        B_gate = [Buf("gate%d" % i) for i in range(RJ)]
        gsum = sb("gsum", [128, 8]);
        B_gsum = Buf("gsum")
        actv = sb("actv", [128, RJ, 128]);
        B_actv = [Buf("actv%d" % i) for i in range(RJ)]
        coef = sb("coef", [128, RJ, 128]);
        B_coef = [Buf("coef%d" % i) for i in range(RJ)]
        NG = 3
        GU = [sb("GU%d" % i, [128, D], BF16) for i in range(NG)]
        B_GU = [Buf("GU%d" % i) for i in range(NG)]
        d_GU = [dsem() for _ in range(NG)]
        GV = [sb("GV%d" % i, [128, D], BF16) for i in range(NG)]
        B_GV = [Buf("GV%d" % i) for i in range(NG)]
        d_GV = [dsem() for _ in range(NG)]
        diag = sb("diag", [128, 2, 128], BF16)
        B_diag = [Buf("diag0"), Buf("diag1")]
        tmpo = sb("tmpo", [128, 512]);
        B_tmpo = Buf("tmpo")
        d_x = [dsem() for _ in range(NS)]
        d_xh = dsem()
        d_st = [dsem() for _ in range(NS)]
        d_rl = dsem()
        d_y = dsem()
        B_ydram = [Buf("ydram%d" % i) for i in range(RJ)]

        def fv(seg, kk, kc):
            return featv[:, kc, seg * 6 + kk:seg * 6 + kk + 1]

        def rstd_from_ss(col, Bst, nparts=128):
            P.op("act", lambda e: e.activation(out=stt[0:nparts, col:col + 1], in_=stt[0:nparts, col:col + 1],
                                               func=AF.Sqrt, bias=RMS_EPS, scale=1.0 / D),
                 r=[Bst], w=[Bst])
            P.op("dve", lambda e: e.reciprocal(out=stt[0:nparts, col:col + 1], in_=stt[0:nparts, col:col + 1]),
                 r=[Bst], w=[Bst])

        def bcast_tiles(seg, which, banks=None, Bbanks=None):
            psZ, B_psZ = (banks, Bbanks) if banks is not None else (psZ_, B_psZ_)
            if banks is None:
                scr = [(mTs[:, 0, 0:128], B_mTs[0]), (mTs[:, 1, 0:128], B_mTs[1])]
            else:
                scr = [(tmpo[:, 0:128], B_tmpo), (tmpo[:, 128:256], B_tmpo)]
            for (dst, kk) in which:
                for kc in range(KC):
                    sv, Bsv = scr[kc % 2]
                    P.op("dve", lambda e, sv=sv, kc=kc, kk=kk, seg=seg: e.tensor_scalar(
                        out=sv, in0=ident_f[:], scalar1=fv(seg, kk, kc), scalar2=None,
                        op0=ALU.mult), r=[B_const, B_featv], w=[Bsv])
                    n = (kc // 4) % 2
                    P.op("pe", lambda e, sv=sv, kc=kc, n=n: e.matmul(
                        psZ[n][:, (kc % 4) * 128:(kc % 4 + 1) * 128], lhsT=ones_f[:], rhs=sv,
                        start=True, stop=True), r=[Bsv, B_const], w=[B_psZ[n]])
                    if kc % 4 == 3:
                        c4 = kc // 4
                        P.op("dve", lambda e, dst=dst, n=n, c4=c4: e.tensor_copy(
                            out=dst[:, c4 * 512:(c4 + 1) * 512], in_=psZ[n][:, :]), r=[B_psZ[n]], w=[B_bc[id(dst)]])

        B_bc = {id(b2b): Buf("b2b"), id(g2b): Buf("g2b"), id(gt2b): Buf("gt2b")}

        d_misc = dsem()
        B_misc = Buf("misc")
        loads = [(ident_f[:], ident_d), (iota16[:], iota16_d), (em[:], em_d),
                 (gfb[:], gfin_b), (convw[:], convw_d), (convb[:], convb_d), (lng[:], lng_d),
                 (lnb[:], lnb_d), (shortw[:], shortw_d), (keysT[:], keysT_d), (cTs[:], cT),
                 (badaT[:], badaT_d), (g1T[:], g1T_d), (g2T[:], g2T_d)]
        for (dst, src) in loads:
            P.dma("sp", lambda e, dst=dst, src=src: e.dma_start(out=dst, in_=src), d_misc, w=[B_misc])
        P.op("dve", lambda e: e.tensor_copy(out=ident_b[:], in_=ident_f[:]), r=[B_misc], w=[B_const])
        P.op("dve", lambda e: e.memset(ones_f[:], 1.0), w=[B_const])
        B_sT = Buf("sT")
        P.op("act", lambda e: e.activation(out=sT[:], in_=cTs[:], func=AF.Silu), r=[B_misc], w=[B_sT])
        for oc in range(96):
            i = oc % 2
            P.dma("sp", lambda e, i=i, oc=oc: e.dma_start(
                out=wst[:, i, :], in_=wada_blocks[oc].rearrange("p k c -> p (k c)")), d_wst[i], w=[B_wst[i]])
            for kc in range(KC):
                P.op("pe", lambda e, i=i, oc=oc, kc=kc: e.matmul(
                    psS[:, oc * 2:(oc + 1) * 2], lhsT=wst[:, i, kc * 128:(kc + 1) * 128], rhs=sT[:, kc, :],
                    start=(kc == 0), stop=(kc == KC - 1)), r=[B_wst[i], B_sT], w=[B_psS])
        B_fm = Buf("featmod")
        P.op("dve", lambda e: e.tensor_tensor(
            out=featmod[:], in0=psS[:, 0:192].rearrange("p (o r) -> p o r", r=2),
            in1=badaT[:].unsqueeze(2).to_broadcast([128, 96, 2]), op=ALU.add), r=[B_psS, B_misc], w=[B_fm])
        for seg_ in range(2):
            for kk in range(6):
                src = featmod[:, kk * 16:(kk + 1) * 16, seg_]
                dst = featv[:, :, seg_ * 6 + kk]
                if kk in (1, 4):
                    gT = g1T if kk == 1 else g2T
                    P.op("dve", lambda e, src=src, dst=dst, gT=gT: e.scalar_tensor_tensor(
                        out=dst, in0=src, scalar=1.0, in1=gT[:], op0=ALU.add, op1=ALU.mult),
                         r=[B_fm, B_misc], w=[B_featv])
                else:
                    P.op("dve", lambda e, src=src, dst=dst: e.tensor_copy(out=dst, in_=src), r=[B_fm], w=[B_featv])

        stage_bf = [(GU[i], B_GU[i], d_GU[i]) for i in range(NG)] + [(GV[i], B_GV[i], d_GV[i]) for i in range(NG)]
        cnt_ = [0]
        last_store = {}

        def convert(src_ap, dst_ap, Bdst):
            k = cnt_[0]
            cnt_[0] += 1
            i = k % 2
            bt, Bbt, dbt = stage_bf[k % len(stage_bf)]
            P.dma("sp", lambda e, i=i, src_ap=src_ap: e.dma_start(out=wst[:, i, :], in_=src_ap), d_wst[i],
                  w=[B_wst[i]])
            if k % 2 == 0:
                P.op("act", lambda e, i=i, bt=bt: e.activation(out=bt[:], in_=wst[:, i, :], func=AF.Copy),
                     r=[B_wst[i]], w=[Bbt])
            else:
                P.op("dve", lambda e, i=i, bt=bt: e.tensor_copy(out=bt[:], in_=wst[:, i, :]), r=[B_wst[i]], w=[Bbt])
            o = P.dma("pool", lambda e, bt=bt, dst_ap=dst_ap: e.dma_start(out=dst_ap, in_=bt[:]), dbt,
                      r=[Bbt])
            last_store.setdefault(id(Bdst), {})[id(dbt)] = o

        for blk in range(NBLK):
            convert(w_blocks[blk].rearrange("p k c -> p (k c)"), wbf[blk], B_wbf)
        for r_ in range(NEXP // 128):
            convert(eu[r_ * 128:(r_ + 1) * 128, :], ubf[r_ * 128:(r_ + 1) * 128, :], B_ubf)
            convert(ev[r_ * 128:(r_ + 1) * 128, :], vbf[r_ * 128:(r_ + 1) * 128, :], B_vbf)
        for Bd in (B_wbf, B_ubf, B_vbf):
            Bd.writer = list(last_store[id(Bd)].values())
        print("[kernel] sbuf bytes remaining per partition:", nc.sbuf_bytes_remaining)

        tiles = []
        for (seg, xbase, ybase, ntiles) in ((0, HALO, 0, L0 // T), (1, L0 + 3 * HALO, L0, L1 // T)):
            for it in range(ntiles):
                tiles.append((seg, it, ntiles, xbase + it * T, ybase + it * T))
        NT = len(tiles)
        NJ = NT * NS

        wload = [0]

        def prefetch_to(g):
            lim = min(g, NT * NBLK - 1)
            while wload[0] <= lim:
                k = wload[0]
                wload[0] += 1
                i = k % NWB
                blk = k % NBLK
                P.dma("sp", lambda e, i=i, blk=blk: e.dma_start(
                    out=wb[:, i].rearrange("p k c -> p (k c)"), in_=wbf[blk]), d_wb[i], r=[B_wbf], w=[B_wb[i]])

        zr = [0]

        def zmm(g, ncols, rhsT, Brhs):
            prefetch_to(g + 2)
            i = g % NWB
            wv, Bw = wb[:, i], B_wb[i]
            bi = zr[0] % 2
            zr[0] += 1
            for kc in range(KC):
                P.op("pe", lambda e, bi=bi, kc=kc, wv=wv, ncols=ncols, rhsT=rhsT: e.matmul(
                    psZ[bi][:, 0:ncols], lhsT=wv[:, kc, :], rhs=rhsT[:, kc, 0:ncols],
                    start=(kc == 0), stop=(kc == KC - 1)), r=[Bw, Brhs], w=[B_psZ[bi]])
            return psZ[bi], B_psZ[bi]

        def transposes_bf(src, npart, dsts, Bsrc, Bdst, scale_bias=None):
            for g4 in range(4):
                xi = xrc[0] % 2
                xrc[0] += 1
                psTv = psTvr[xi]
                B_psX = B_psXr[xi]
                for j in range(4):
                    kc = g4 * 4 + j
                    P.op("pe", lambda e, kc=kc, j=j, psTv=psTv: e.transpose(
                        psTv[:, j, 0:npart], src[0:npart, kc * 128:(kc + 1) * 128], ident_b[0:npart, 0:npart]),
                         r=[Bsrc, B_const], w=[B_psX])
                for j in range(4):
                    kc = g4 * 4 + j
                    for (dfn, p0, w_) in dsts:
                        dst = dfn(kc)
                        if scale_bias is not None:
                            sc, bi_ = scale_bias(kc)
                            if j % 2 == 0:
                                P.op("dve", lambda e, dst=dst, j=j, p0=p0, w_=w_, sc=sc, bi_=bi_, psTv=psTv: e.tensor_scalar(
                                    out=dst, in0=psTv[:, j, p0:p0 + w_], scalar1=sc, scalar2=bi_,
                                    op0=ALU.mult, op1=ALU.add), r=[B_psX, B_featv], w=[Bdst])
                            else:
                                P.op("act", lambda e, dst=dst, j=j, p0=p0, w_=w_, sc=sc, bi_=bi_, psTv=psTv: e.activation(
                                    out=dst, in_=psTv[:, j, p0:p0 + w_], func=AF.Identity, scale=sc, bias=bi_),
                                     r=[B_psX, B_featv], w=[Bdst])
                        else:
                            if j % 2 == 0:
                                P.op("dve", lambda e, dst=dst, j=j, p0=p0, w_=w_, psTv=psTv: e.tensor_copy(
                                    out=dst, in_=psTv[:, j, p0:p0 + w_]), r=[B_psX], w=[Bdst])
                            else:
                                P.op("act", lambda e, dst=dst, j=j, p0=p0, w_=w_, psTv=psTv: e.activation(
                                    out=dst, in_=psTv[:, j, p0:p0 + w_], func=AF.Copy), r=[B_psX], w=[Bdst])

        def M_units(ti):
            seg, it, ntiles, r0, y0 = tiles[ti]
            edgeL = (it == 0)
            edgeR = (it == ntiles - 1)
            hp_ = ti % 2
            g0 = ti * NBLK
            units = []

            def add(cost, fn):
                units.append((cost, fn))

            if it == 0:
                add(6, lambda: bcast_tiles(seg, ((b2b, 3), (g2b, 4))))

            def st_1a():
                for s in range(NS):
                    P.dma("sp", lambda e, s=s: e.dma_start(
                        out=xt[:, s, :], in_=xin[r0 + 128 * s:r0 + 128 * (s + 1), :]), d_x[s], w=[B_xt[s]])
                P.dma("sp", lambda e: e.dma_start(out=xh[0:HALO, :], in_=xin[r0 - HALO:r0, :]), d_xh, w=[B_xh])
                P.dma("sp", lambda e: e.dma_start(out=xh[HALO:2 * HALO, :], in_=xin[r0 + T:r0 + T + HALO, :]),
                      d_xh, w=[B_xh])
                prefetch_to(g0 + 1)

            add(1, st_1a)

            def st_1b(pt):
                if pt < NS:
                    src, Bs, npart = xt[:, pt, :], B_xt[pt], 128
                else:
                    src, Bs, npart = xh, B_xh, 2 * HALO
                xn, B_xn = h2b[:, hp_, pt % 2, :], B_h2b[hp_][pt % 2]
                P.op("act", lambda e: e.activation(
                    out=sqa[0:npart, :], in_=src[0:npart, :], func=AF.Square,
                    accum_out=stt[0:npart, pt:pt + 1]), r=[Bs], w=[B_sqa, B_stt])
                rstd_from_ss(pt, B_stt, npart)
                P.op("act", lambda e: e.activation(
                    out=xn[0:npart, :], in_=src[0:npart, :], func=AF.Copy,
                    scale=stt[0:npart, pt:pt + 1]), r=[Bs, B_stt], w=[B_xn])
                if pt < NS:
                    dsts = [(lambda kc: hT[:, kc, HALO + 128 * pt:HALO + 128 * (pt + 1)], 0, 128)]
                else:
                    dsts = [(lambda kc: hT[:, kc, 0:HALO], 0, HALO),
                            (lambda kc: hT[:, kc, HALO + T:N], HALO, HALO)]
                transposes_bf(xn, npart, dsts, B_xn, B_hT, scale_bias=lambda kc: (fv(seg, 1, kc), fv(seg, 0, kc)))

            for pt in range(NS + 1):
                add(8, lambda pt=pt: st_1b(pt))

            def conf_chunk(c):
                za, Bza = zmm(g0 + 2 * c, N, hT, B_hT)
                zg, Bzg = zmm(g0 + 2 * c + 1, N, hT, B_hT)
                ui = c % 2
                u, B_u = u2[:, ui, :], B_u2[ui]
                P.op("act", lambda e: e.activation(out=sg[:], in_=zg[:, 0:N], func=AF.Sigmoid), r=[Bzg], w=[B_sg])
                P.op("dve", lambda e: e.tensor_tensor(out=u, in0=za[:, 0:N], in1=sg[:], op=ALU.mult),
                     r=[Bza, B_sg], w=[B_u])
                if edgeL:
                    P.op("dve", lambda e: e.tensor_scalar(
                        out=u[:, 0:HALO], in0=u[:, 0:HALO], scalar1=em[:, 2 * seg:2 * seg + 1], scalar2=None,
                        op0=ALU.mult), r=[B_u, B_const], w=[B_u])
                if edgeR:
                    P.op("dve", lambda e: e.tensor_scalar(
                        out=u[:, HALO + T:N], in0=u[:, HALO + T:N], scalar1=em[:, 2 * seg + 1:2 * seg + 2],
                        scalar2=None, op0=ALU.mult), r=[B_u, B_const], w=[B_u])
                cb = zr[0] % 2
                zr[0] += 1
                for k in range(31):
                    di = dgn[0] % NDC
                    dgn[0] += 1
                    P.op("act", lambda e, k=k, di=di: e.activation(
                        out=dgc[:, di, :], in_=ident_f[:], func=AF.Copy, scale=convw[:, c, k:k + 1]),
                         r=[B_const], w=[B_dgc[di]])
                    P.op("pe", lambda e, k=k, di=di: e.matmul(
                        psZ[cb][:, 0:T], lhsT=dgc[:, di, :], rhs=u[:, k:k + T], start=(k == 0), stop=(k == 30)),
                         r=[B_dgc[di], B_u], w=[B_psZ[cb]])
                P.op("act", lambda e: e.activation(out=cat[:, 0, :], in_=psZ[cb][:, 0:T], func=AF.Identity,
                                                   bias=convb[:, c:c + 1], scale=1.0),
                     r=[B_psZ[cb], B_const], w=[B_cat[0]])
                P.op("act", lambda e: e.activation(out=cat[:, 1, :], in_=psZ[cb][:, 0:T], func=AF.Square,
                                                   bias=convb[:, c:c + 1], scale=1.0),
                     r=[B_psZ[cb], B_const], w=[B_cat[1]])
                P.op("dve", lambda e: e.tensor_copy(out=ucall[:, c, :], in_=cat[:, 0, :]),
                     r=[B_cat[0]], w=[B_pA[c]])
                P.op("pe", lambda e: e.matmul(
                    psS[:, :], lhsT=ones_f[:], rhs=cat[:].rearrange("p a t -> p (a t)"),
                    start=(c == 0), stop=(c == 7)), r=[B_cat, B_const], w=[B_psS])

            for c in range(8):
                add(25, lambda c=c: conf_chunk(c))

            def ln_stats():
                P.op("dve", lambda e: e.tensor_scalar(out=lnm, in0=psS[:, 0:T], scalar1=1.0 / 1024, scalar2=None,
                                                      op0=ALU.mult), r=[B_psS], w=[B_lnm])
                P.op("dve", lambda e: e.tensor_tensor(out=tmp1, in0=lnm, in1=lnm, op=ALU.mult),
                     r=[B_lnm], w=[B_tmp1])
                P.op("dve", lambda e: e.scalar_tensor_tensor(out=tmp2, in0=psS[:, T:2 * T], scalar=1.0 / 1024,
                                                             in1=tmp1, op0=ALU.mult, op1=ALU.subtract),
                     r=[B_psS, B_tmp1], w=[B_tmp2])
                P.op("act", lambda e: e.activation(out=tmp2, in_=tmp2, func=AF.Sqrt, bias=LN_EPS, scale=1.0),
                     r=[B_tmp2], w=[B_tmp2])
                P.op("dve", lambda e: e.reciprocal(out=lnr, in_=tmp2), r=[B_tmp2], w=[B_lnr])

            add(4, ln_stats)

            def ln_apply(c):
                P.op("dve", lambda e: e.tensor_tensor(out=tmp1, in0=ucall[:, c, :], in1=lnm, op=ALU.subtract),
                     r=[B_pA[c], B_lnm], w=[B_tmp1])
                P.op("dve", lambda e: e.tensor_tensor(out=tmp1, in0=tmp1, in1=lnr, op=ALU.mult),
                     r=[B_tmp1, B_lnr], w=[B_tmp1])
                P.op("act", lambda e: e.activation(out=usT[:, c, :], in_=tmp1, func=AF.Silu,
                                                   scale=lng[:, c:c + 1], bias=lnb[:, c:c + 1]),
                     r=[B_tmp1, B_const], w=[B_usT[c]])

            for c in range(8):
                add(2, lambda c=c: ln_apply(c))

            def short_chunk(c):
                gb = g0 + 16 + 3 * c
                zx, Bzx = zmm(gb, N, hT, B_hT)
                zc, Bzc = zmm(gb + 1, N, hT, B_hT)
                P.op("act", lambda e: e.activation(out=xs_sb[:], in_=zx[:, 0:N], func=AF.Copy), r=[Bzx], w=[B_xs])
                P.op("dve", lambda e: e.tensor_tensor(out=vv[:], in0=zc[:, 0:N], in1=xs_sb[:], op=ALU.mult),
                     r=[Bzc, B_xs], w=[B_vv])
                zb, Bzb = zmm(gb + 2, N, hT, B_hT)
                if edgeL:
                    P.op("dve", lambda e: e.tensor_scalar(
                        out=vv[:, 0:HALO], in0=vv[:, 0:HALO], scalar1=em[:, 2 * seg:2 * seg + 1], scalar2=None,
                        op0=ALU.mult), r=[B_vv, B_const], w=[B_vv])
                if edgeR:
                    P.op("dve", lambda e: e.tensor_scalar(
                        out=vv[:, HALO + T:N], in0=vv[:, HALO + T:N], scalar1=em[:, 2 * seg + 1:2 * seg + 2],
                        scalar2=None, op0=ALU.mult), r=[B_vv, B_const], w=[B_vv])
                P.op("dve", lambda e: e.tensor_scalar(
                    out=acc[:], in0=vv[:, HALO - 1:HALO - 1 + T], scalar1=shortw[:, c, 0:1], scalar2=None,
                    op0=ALU.mult), r=[B_vv, B_const], w=[B_acc])
                for k in (1, 2):
                    P.op("dve", lambda e, k=k: e.scalar_tensor_tensor(
                        out=acc[:], in0=vv[:, HALO - 1 + k:HALO - 1 + k + T], scalar=shortw[:, c, k:k + 1],
                        in1=acc[:], op0=ALU.mult, op1=ALU.add), r=[B_vv, B_acc, B_const], w=[B_acc])
                P.op("dve", lambda e: e.tensor_tensor(
                    out=usT[:, 8 + c, :], in0=zb[:, HALO:HALO + T], in1=acc[:], op=ALU.mult),
                     r=[Bzb, B_acc], w=[B_usT[8 + c]])

            for c in range(8):
                add(12, lambda c=c: short_chunk(c))

            def wout_chunk(dc):
                mp, Bmp = zmm(g0 + 40 + dc, T, usT, B_usT)
                mi = dc % 2
                psX, B_psX = psXr[dc % 2], B_psXr[dc % 2]
                P.op("act", lambda e: e.activation(
                    out=mTs[:, mi, :], in_=mp[:, 0:T], func=AF.Copy, scale=fv(seg, 2, dc)),
                     r=[Bmp, B_featv], w=[B_mTs[mi]])
                for s in range(NS):
                    P.op("pe", lambda e, s=s: e.transpose(
                        psX[:, s * 128:(s + 1) * 128], mTs[:, mi, s * 128:(s + 1) * 128], ident_f[:]),
                         r=[B_mTs[mi], B_const], w=[B_psX])
                for s in range(NS):
                    P.op("dve", lambda e, s=s: e.tensor_tensor(
                        out=xt[:, s, dc * 128:(dc + 1) * 128], in0=xt[:, s, dc * 128:(dc + 1) * 128],
                        in1=psX[:, s * 128:(s + 1) * 128], op=ALU.add),
                         r=[B_xt[s], B_psX], w=[B_xt[s]])

            for dc in range(KC):
                add(4, lambda dc=dc: wout_chunk(dc))

            def st_3(s):
                j = ti * NS + s
                P.dma("sp", lambda e: e.dma_start(out=yout[y0 + 128 * s:y0 + 128 * (s + 1), :], in_=xt[:, s, :]),
                      d_st[s], r=[B_xt[s]], w=[B_ydram[j % RJ]])
                P.op("act", lambda e: e.activation(
                    out=sqa[:], in_=xt[:, s, :], func=AF.Square, accum_out=stt[:, 4 + s:5 + s]),
                     r=[B_xt[s]], w=[B_sqa, B_stt])
                rstd_from_ss(4 + s, B_stt)
                P.op("dve", lambda e: e.scalar_tensor_tensor(
                    out=h2f, in0=xt[:, s, :], scalar=stt[:, 4 + s:5 + s], in1=g2b[:],
                    op0=ALU.mult, op1=ALU.mult), r=[B_xt[s], B_stt, B_bc[id(g2b)]], w=[B_h2f])
                P.op("dve", lambda e: e.tensor_tensor(
                    out=h2b[:, hp_, s, :], in0=h2f, in1=b2b[:], op=ALU.add),
                     r=[B_h2f, B_bc[id(b2b)]], w=[B_h2b[hp_][s]])
                dsts = [(lambda kc: h2T[:, kc, 128 * s:128 * (s + 1)], 0, 128)]
                transposes_bf(h2b[:, hp_, s, :], 128, dsts, B_h2b[hp_][s], B_h2T)

            for s in range(NS):
                add(12, lambda s=s: st_3(s))

            def q_chunk(qc):
                qp, Bqp = zmm(g0 + 56 + qc, T, h2T, B_h2T)
                mi = qc % 2
                psX, B_psX = psXr[qc % 2], B_psXr[qc % 2]
                P.op("act", lambda e: e.activation(out=mTs[:, mi, :], in_=qp[:, 0:T], func=AF.Copy),
                     r=[Bqp], w=[B_mTs[mi]])
                for s in range(NS):
                    P.op("pe", lambda e, s=s: e.matmul(
                        psX[:, s * 128:(s + 1) * 128], lhsT=mTs[:, mi, s * 128:(s + 1) * 128],
                        rhs=keysT[:, qc, :], start=True, stop=True),
                         r=[B_mTs[mi], B_const], w=[B_psX])
                if qc % 2 == 0:
                    P.op("dve", lambda e: e.tensor_copy(
                        out=Sall[:, :, qc, :], in_=psX[:, 0:256].rearrange("p (s k) -> p s k", s=2)),
                         r=[B_psX], w=[B_S(0, qc), B_S(1, qc)])
                else:
                    P.op("act", lambda e: e.activation(
                        out=Sall[:, :, qc, :], in_=psX[:, 0:256].rearrange("p (s k) -> p s k", s=2), func=AF.Copy),
                         r=[B_psX], w=[B_S(0, qc), B_S(1, qc)])

            for qc in range(16):
                add(4, lambda qc=qc: q_chunk(qc))

            def retr_topk(s, hps):
                B_Ss = [B_S(s, q) for q in range(16)]
                for hp in hps:
                    Sv = Sall[:, s, hp, :]
                    P.op("dve", lambda e, hp=hp, Sv=Sv: e.max(out=tv[:, hp, 0:8], in_=Sv), r=[B_Ss], w=[B_tv])
                    P.op("dve", lambda e, hp=hp, Sv=Sv: e.match_replace(
                        out=work[:, 0:128], in_to_replace=tv[:, hp, 0:8], in_values=Sv, imm_value=-1e30),
                         r=[B_Ss, B_tv], w=[B_work])
                    P.op("dve", lambda e, hp=hp: e.max(out=tv[:, hp, 8:16], in_=work[:, 0:128]),
                         r=[B_work], w=[B_tv])
                    P.op("dve", lambda e, hp=hp, Sv=Sv: e.max_index(
                        out=ti_[:, hp, 0:8], in_max=tv[:, hp, 0:8], in_values=Sv), r=[B_Ss, B_tv], w=[B_ti])
                    P.op("dve", lambda e, hp=hp, Sv=Sv: e.max_index(
                        out=ti_[:, hp, 8:16], in_max=tv[:, hp, 8:16], in_values=Sv), r=[B_Ss, B_tv], w=[B_ti])

            def retr_cand(s):
                P.op("dve", lambda e: e.tensor_copy(out=tif[:], in_=ti_[:]), r=[B_ti], w=[B_tif])
                tv4 = tv[:].rearrange("p (h t) k -> p h t k", t=2)
                P.op("dve", lambda e: e.tensor_tensor(
                    out=cand.rearrange("p h (a b) -> p h a b", a=16),
                    in0=tv4[:, :, 0, :].unsqueeze(3).to_broadcast([128, 8, 16, 16]),
                    in1=tv4[:, :, 1, :].unsqueeze(2).to_broadcast([128, 8, 16, 16]), op=ALU.add),
                     r=[B_tv], w=[B_cand])
                for h in range(8):
                    P.op("dve", lambda e, h=h: e.max(out=cv[:, h, 0:8], in_=cand[:, h, :]), r=[B_cand], w=[B_cv])
                    P.op("dve", lambda e, h=h: e.match_replace(
                        out=work[:, :], in_to_replace=cv[:, h, 0:8], in_values=cand[:, h, :], imm_value=-1e30),
                         r=[B_cand, B_cv], w=[B_work])
                    P.op("dve", lambda e, h=h: e.max(out=cv[:, h, 8:16], in_=work[:, :]), r=[B_work], w=[B_cv])
                    P.op("dve", lambda e, h=h: e.max_index(
                        out=cpos[:, h, 0:8], in_max=cv[:, h, 0:8], in_values=cand[:, h, :]),
                         r=[B_cand, B_cv], w=[B_cpos])
                    P.op("dve", lambda e, h=h: e.max_index(
                        out=cpos[:, h, 8:16], in_max=cv[:, h, 8:16], in_values=cand[:, h, :]),
                         r=[B_cand, B_cv], w=[B_cpos])

            def retr_idx(s):
                j = ti * NS + s
                jr = j % RJ
                cposf = cpos[:].rearrange("p h k -> p (h k)")
                P.op("dve", lambda e: e.tensor_single_scalar(out=cab[:, 0, :], in_=cposf, scalar=4,
                                                             op=ALU.logical_shift_right), r=[B_cpos], w=[B_cab])
                P.op("dve", lambda e: e.tensor_single_scalar(out=cab[:, 1, :], in_=cposf, scalar=15,
                                                             op=ALU.bitwise_and), r=[B_cpos], w=[B_cab])
                P.op("dve", lambda e: e.tensor_copy(out=cabf[:], in_=cab[:]), r=[B_cab], w=[B_cabf])
                tif4 = tif[:].rearrange("p (h t) k -> p h t k", t=2)
                for t_ in range(2):
                    P.op("dve", lambda e, t_=t_: e.tensor_tensor(
                        out=oh,
                        in0=cabf[:, t_, :].rearrange("p (h k) -> p h k", h=8).unsqueeze(3).to_broadcast(
                            [128, 8, 16, 16]),
                        in1=iota16[:].unsqueeze(1).unsqueeze(1).to_broadcast([128, 8, 16, 16]),
                        op=ALU.is_equal), r=[B_cabf, B_const], w=[B_oh])
                    P.op("dve", lambda e, t_=t_: e.tensor_tensor(
                        out=oh, in0=oh, in1=tif4[:, :, t_, :].unsqueeze(2).to_broadcast([128, 8, 16, 16]),
                        op=ALU.mult), r=[B_oh, B_tif], w=[B_oh])
                    P.op("dve", lambda e, t_=t_: e.tensor_reduce(
                        out=isel[:, t_, :].rearrange("p (h k) -> p h k", h=8), in_=oh, axis=AX.X, op=ALU.add),
                         r=[B_oh], w=[B_isel])
                P.op("dve", lambda e: e.scalar_tensor_tensor(
                    out=eidxf[:], in0=isel[:, 0, :], scalar=128.0, in1=isel[:, 1, :], op0=ALU.mult, op1=ALU.add),
                     r=[B_isel], w=[B_eidxf])
                P.op("dve", lambda e: e.tensor_copy(out=eidx[:, jr, :], in_=eidxf[:]), r=[B_eidxf], w=[B_eidx[jr]])
                gv_ = gate[:, jr, :].rearrange("p (h k) -> p h k", h=8)
                P.op("dve", lambda e: e.tensor_tensor(
                    out=gv_, in0=cv[:], in1=cv[:, :, 0:1].to_broadcast([128, 8, 16]), op=ALU.subtract),
                     r=[B_cv], w=[B_gate[jr]])
                P.op("act", lambda e: e.activation(out=gv_, in_=gv_, func=AF.Exp), r=[B_gate[jr]], w=[B_gate[jr]])
                P.op("dve", lambda e: e.tensor_reduce(out=gsum[:], in_=gv_, axis=AX.X, op=ALU.add),
                     r=[B_gate[jr]], w=[B_gsum])
                P.op("dve", lambda e: e.reciprocal(out=gsum[:], in_=gsum[:]), r=[B_gsum], w=[B_gsum])
                P.op("dve", lambda e: e.tensor_tensor(
                    out=gv_, in0=gv_, in1=gsum[:].unsqueeze(2).to_broadcast([128, 8, 16]), op=ALU.mult),
                     r=[B_gate[jr], B_gsum], w=[B_gate[jr]])

            for s in range(NS):
                add(12, lambda s=s: retr_topk(s, range(0, 8)))
                add(12, lambda s=s: retr_topk(s, range(8, 16)))
                add(20, lambda s=s: retr_cand(s))
                add(20, lambda s=s: retr_idx(s))
            return units

        gcnt = {"u": 0, "v": 0, "d": 0}

        def U_slot(j, slot):
            jr = j % RJ
            hp_, s = (j // NS) % 2, j % NS
            gi = gcnt["u"] % NG
            gcnt["u"] += 1
            P.dma("pool", lambda e: e.indirect_dma_start(
                out=GU[gi][:, :], out_offset=None, in_=ubf,
                in_offset=bass.IndirectOffsetOnAxis(ap=eidx[:, jr, slot:slot + 1], axis=0)),
                  d_GU[gi], r=[B_eidx[jr], B_ubf], w=[B_GU[gi]])
            P.op("dve", lambda e: e.scalar_tensor_tensor(
                out=sqd[:], in0=GU[gi][:], scalar=1.0, in1=h2b[:, hp_, s, :], op0=ALU.mult, op1=ALU.mult,
                accum_out=actv[:, jr, slot:slot + 1]), r=[B_GU[gi], B_h2b[hp_][s]], w=[B_sqd, B_actv[jr]])

        def U_finish(j):
            jr = j % RJ
            P.op("act", lambda e: e.activation(out=coef[:, jr, :], in_=actv[:, jr, :], func=AF.Gelu),
                 r=[B_actv[jr]], w=[B_coef[jr]])
            P.op("dve", lambda e: e.tensor_tensor(out=coef[:, jr, :], in0=coef[:, jr, :], in1=gate[:, jr, :],
                                                  op=ALU.mult), r=[B_coef[jr], B_gate[jr]], w=[B_coef[jr]])

        def V_slot(j, slot):
            jr = j % RJ
            gi = gcnt["v"] % NG
            gcnt["v"] += 1
            di = gcnt["d"] % 2
            gcnt["d"] += 1
            P.dma("pool", lambda e: e.indirect_dma_start(
                out=GV[gi][:, :], out_offset=None, in_=vbf,
                in_offset=bass.IndirectOffsetOnAxis(ap=eidx[:, jr, slot:slot + 1], axis=0)),
                  d_GV[gi], r=[B_eidx[jr], B_vbf], w=[B_GV[gi]])
            P.op("act", lambda e: e.activation(
                out=diag[:, di, :], in_=ident_f[:], func=AF.Copy, scale=coef[:, jr, slot:slot + 1]),
                 r=[B_coef[jr], B_const], w=[B_diag[di]])
            for n in range(4):
                P.op("pe", lambda e, n=n: e.matmul(
                    psA[n][:, :], lhsT=diag[:, di, :], rhs=GV[gi][:, n * 512:(n + 1) * 512],
                    start=(slot == 0), stop=(slot == 127)),
                     r=[B_diag[di], B_GV[gi]], w=[B_psA[n]])

        def V_start(j):
            ti = j // NS
            s = j % NS
            y0 = tiles[ti][4]
            P.dma("act", lambda e: e.dma_start(out=xfin[:], in_=yout[y0 + 128 * s:y0 + 128 * (s + 1), :]),
                  d_rl, r=[B_ydram[j % RJ]], w=[B_xfin])

        def V_finish(j):
            ti = j // NS
            s = j % NS
            seg, it, ntiles, r0, y0 = tiles[ti]
            for n in range(4):
                P.op("dve", lambda e, n=n: e.tensor_tensor(
                    out=tmpo[:], in0=psA[n][:, :], in1=gt2b[:, n * 512:(n + 1) * 512], op=ALU.mult),
                     r=[B_psA[n], B_bc[id(gt2b)]], w=[B_tmpo])
                P.op("dve", lambda e, n=n: e.tensor_tensor(
                    out=xfin[:, n * 512:(n + 1) * 512], in0=xfin[:, n * 512:(n + 1) * 512], in1=tmpo[:],
                    op=ALU.add), r=[B_tmpo, B_xfin], w=[B_xfin])
            P.op("act", lambda e: e.activation(
                out=sqa[:], in_=xfin[:], func=AF.Square, accum_out=stt[:, 8:9]),
                 r=[B_xfin], w=[B_sqa, B_stt2])
            rstd_from_ss(8, B_stt2)
            P.op("dve", lambda e: e.scalar_tensor_tensor(
                out=xfin[:], in0=xfin[:], scalar=stt[:, 8:9], in1=gfb[:],
                op0=ALU.mult, op1=ALU.mult), r=[B_xfin, B_stt2, B_misc], w=[B_xfin])
            P.dma("act", lambda e: e.dma_start(out=yout[y0 + 128 * s:y0 + 128 * (s + 1), :], in_=xfin[:]),
                  d_y, r=[B_xfin], w=[B_ydram[j % RJ]])
            if j + 1 < NJ and (j + 1) % NS == 0 and tiles[(j + 1) // NS][1] == 0:
                bcast_tiles(tiles[(j + 1) // NS][0], ((gt2b, 5),), psA[0:2], B_psA[0:2])

        for (c_, fn) in M_units(0):
            fn()

        class MSched:
            def __init__(self, fifo, nhalf):
                self.items = fifo
                n = len(fifo)
                self.deps = [None] * n
                self.when = [None] * n
                self.first = 0
                writer, readers = {}, {}
                for i, (kind, eng, fn, dsem, r, w) in enumerate(fifo):
                    d = set()
                    rr, ww = _flat(r), _flat(w)
                    for b_ in rr:
                        if id(b_) in writer:
                            d.add(writer[id(b_)])
                    for b_ in ww:
                        if id(b_) in writer:
                            d.add(writer[id(b_)])
                        d.update(readers.get(id(b_), ()))
                    d.discard(i)
                    self.deps[i] = d
                    for b_ in rr:
                        readers.setdefault(id(b_), set()).add(i)
                    for b_ in ww:
                        writer[id(b_)] = i
                        readers[id(b_)] = set()
                tot = {e: 0.0 for e in Prog.ENGS}
                for it_ in fifo:
                    tot[it_[1]] += Prog.COST[it_[1]]
                tgt = max(nhalf * 0.55, 1.0)
                self.budget = {e: 2.0 * tot[e] / tgt for e in Prog.ENGS}
                self.t = 0

            def empty(self):
                return self.first >= len(self.items)

            def step(self, flush=False):
                used = {e: 0.0 for e in Prog.ENGS}
                n = len(self.items)
                i = self.first
                scanned = 0
                while i < n and scanned < 700:
                    if self.when[i] is None:
                        scanned += 1
                        kind, eng, fn, dsem, r, w = self.items[i]
                        ok = flush or used[eng] == 0.0 or used[eng] + Prog.COST[eng] <= self.budget[eng]
                        if ok:
                            for d in self.deps[i]:
                                wd = self.when[d]
                                if wd is None:
                                    ok = False
                                    break
                                if not flush and self.items[d][1] != eng and wd >= self.t:
                                    ok = False
                                    break
                        if ok:
                            self.when[i] = self.t
                            used[eng] += Prog.COST[eng]
                            if kind == "op":
                                P._add(Op(eng, fn), r, w)
                            else:
                                o = Op(eng, fn)
                                o.dsem = dsem
                                o.needed = True
                                o.inc = 16
                                P._add(o, r, w)
                    i += 1
                while self.first < n and self.when[self.first] is not None:
                    self.first += 1
                self.t += 1

        bcast_tiles(tiles[0][0], ((gt2b, 5),), psA[0:2], B_psA[0:2])
        SPS = 4
        sched = None
        for p in range(NJ + 1):
            ju = p if p < NJ else None
            jv = p - 1 if p >= 1 else None
            if p % NS == 0:
                tnext = p // NS + 1
                assert sched is None or sched.empty()
                sched = None
                if tnext < NT:
                    fifo = []
                    P.defer = fifo
                    for (c_, fn) in M_units(tnext):
                        fn()
                    P.defer = None
                    sched = MSched(fifo, NS * 128 * SPS)
            if jv is not None:
                V_start(jv)
            for slot in range(128):
                if ju is not None:
                    U_slot(ju, slot)
                if sched is not None:
                    for _ in range(SPS // 2):
                        sched.step()
                if jv is not None:
                    V_slot(jv, slot)
                if sched is not None:
                    for _ in range(SPS // 2):
                        sched.step()
            if sched is not None and p % NS == NS - 1:
                while not sched.empty():
                    sched.step(flush=True)
            if ju is not None:
                U_finish(ju)
            if jv is not None:
                V_finish(jv)

        lasts = {}
        for en in ("sp", "act"):
            for o in P.ops[en]:
                if o.dsem is not None and (o.dsem is d_y or o.dsem in d_st):
                    lasts[id(o.dsem)] = o
        fin = Op("sp", lambda e: e.nop())
        fin.deps = list(lasts.values())
        P.ops["sp"].append(fin)

        with nc.Block() as block2:
            P.resolve(engsem)

            @block2.sync
            def _(e):
                P.emit("sp", e)

            @block2.scalar
            def _(e):
                P.emit("act", e)

            @block2.vector
            def _(e):
                P.emit("dve", e)

            @block2.tensor
            def _(e):
                P.emit("pe", e)

            @block2.gpsimd
            def _(e):
                P.emit("pool", e)
    return nc


def _block_order():
    order = []
    for c in range(8):
        order += [c, 8 + c]
    for c in range(8):
        order += [32 + c, 24 + c, 16 + c]
    return order


def prep_shared(w_ada, b_ada, g_norm1, w_in, conv_w, conv_b, ln_g, ln_b, short_w, w_out, g_norm2, w_query,
                sub_keys, expert_u, expert_v, g_final):
    f = np.float32
    sh = {}
    wa = np.asarray(w_ada[0], dtype=f)
    sh["wada_blocks"] = np.ascontiguousarray(wa.reshape(KC, 128, 96, 128).transpose(2, 1, 0, 3))
    sh["badaT"] = np.ascontiguousarray(np.asarray(b_ada[0], dtype=f).reshape(96, 128).T)
    sh["g1T"] = np.ascontiguousarray(np.asarray(g_norm1[0], dtype=f).reshape(KC, 128).T)
    sh["g2T"] = np.ascontiguousarray(np.asarray(g_norm2[0], dtype=f).reshape(KC, 128).T)
    sh["gfin_b"] = np.ascontiguousarray(np.broadcast_to(np.asarray(g_final)[None, :], (128, D)), dtype=f)
    wi = np.asarray(w_in[0], dtype=f)
    wi_blocks = wi.reshape(KC, 128, 40, 128).transpose(2, 1, 0, 3)
    wi_blocks = wi_blocks[_block_order()]
    wo_blocks = np.asarray(w_out[0], dtype=f).reshape(KC, 128, 16, 128).transpose(2, 1, 0, 3)
    wq_blocks = np.asarray(w_query[0], dtype=f).reshape(KC, 128, 16, 128).transpose(2, 1, 0, 3)
    sh["w_blocks"] = np.ascontiguousarray(np.concatenate([wi_blocks, wo_blocks, wq_blocks], axis=0))
    sh["convw"] = np.ascontiguousarray(np.asarray(conv_w[0], dtype=f).reshape(31, 8, 128).transpose(2, 1, 0))
    sh["convb"] = np.ascontiguousarray(np.asarray(conv_b[0], dtype=f).reshape(8, 128).T)
    sh["lng"] = np.ascontiguousarray(np.asarray(ln_g[0], dtype=f).reshape(8, 128).T)
    sh["lnb"] = np.ascontiguousarray(np.asarray(ln_b[0], dtype=f).reshape(8, 128).T)
    sh["shortw"] = np.ascontiguousarray(np.asarray(short_w[0], dtype=f).reshape(3, 8, 128).transpose(2, 1, 0))
    sk = np.asarray(sub_keys[0], dtype=f)
    sh["keysT"] = np.ascontiguousarray(sk.reshape(16, 128, 128).transpose(2, 0, 1))
    sh["expert_u"] = np.ascontiguousarray(expert_u[0], dtype=f)
    sh["expert_v"] = np.ascontiguousarray(expert_v[0], dtype=f)
    sh["ident"] = np.eye(128, dtype=f)
    sh["iota16"] = np.ascontiguousarray(np.broadcast_to(np.arange(16, dtype=f)[None, :], (128, 16)))
    return sh


def prep_core(xp_seq, xs_seq, s0, L1, cp, cs):
    f = np.float32
    L0 = xp_seq.shape[0]
    S = xs_seq.shape[0]
    xin = np.zeros((L0 + L1 + 4 * HALO, D), dtype=f)
    xin[HALO:HALO + L0] = xp_seq
    base = L0 + 2 * HALO
    lo = max(0, s0 - HALO)
    hi = min(S, s0 + L1 + HALO)
    xin[base + (lo - (s0 - HALO)):base + (hi - (s0 - HALO))] = xs_seq[lo:hi]
    emask = np.zeros((128, 4), dtype=f)
    emask[:, 2] = 1.0 if s0 > 0 else 0.0
    emask[:, 3] = 1.0 if s0 + L1 < S else 0.0
    cc = np.stack([np.asarray(cp, dtype=f), np.asarray(cs, dtype=f)], axis=-1)
    cT = np.ascontiguousarray(cc.reshape(KC, 128, 2).transpose(1, 0, 2))
    return {"xin": xin, "emask": emask, "cT": cT}


_NC_CACHE = {}


def kernel(x_prompt, x_sample, c_prompt, c_sample, w_ada, b_ada, g_norm1, w_in, conv_w, conv_b,
           ln_g, ln_b, short_w, w_out, g_norm2, w_query, sub_keys, expert_u, expert_v, g_final):
    x_prompt = np.asarray(x_prompt)
    x_sample = np.asarray(x_sample)
    c_prompt = np.asarray(c_prompt)
    c_sample = np.asarray(c_sample)
    ncores = 8
    B0, L0, _ = x_prompt.shape
    B1, S1, _ = x_sample.shape
    per = ncores // B1
    L1 = S1 // per
    sh = prep_shared(*(np.asarray(a) for a in (w_ada, b_ada, g_norm1, w_in, conv_w, conv_b, ln_g, ln_b, short_w,
                                               w_out, g_norm2, w_query, sub_keys, expert_u, expert_v, g_final)))
    in_maps = []
    for c in range(ncores):
        b1 = c // per
        q = c % per
        m = dict(sh)
        m.update(prep_core(x_prompt[c], x_sample[b1], q * L1, L1, c_prompt[c], c_sample[b1]))
        in_maps.append(m)
    key = (L0, L1)
    if key not in _NC_CACHE:
        _NC_CACHE[key] = build_program(L0, L1)
    nc = _NC_CACHE[key]
    res = run_bass_kernel_spmd(nc, in_maps, core_ids=list(range(ncores)))
    y_prompt = np.empty((B0, L0, D), dtype=np.float32)
    y_sample = np.empty((B1, S1, D), dtype=np.float32)
    for c in range(ncores):
        y = np.asarray(res.results[c]["yout"])
        y_prompt[c] = y[:L0]
        y_sample[c // per, (c % per) * L1:(c % per + 1) * L1] = y[L0:]
    return (y_prompt, y_sample)
```

```python
import contextlib
import numpy as np
import concourse.bass as bass
import concourse.mybir as mybir
from concourse.bass_utils import run_bass_kernel_spmd

F32 = mybir.dt.float32
BF16 = mybir.dt.bfloat16
I32 = mybir.dt.int32
U32 = mybir.dt.uint32
ALU = mybir.AluOpType
AF = mybir.ActivationFunctionType
AX = mybir.AxisListType

D = 2048
KC = 16
DIN = 5120
NEXP = 16384
T = 256
NS = T // 128
HALO = 15
N = T + 2 * HALO
RMS_EPS = 1e-6
LN_EPS = 1e-5
NBLK = 72


class Buf:
    __slots__ = ("name", "writer", "readers")

    def __init__(self, name):
        self.name = name
        self.writer = None
        self.readers = {}


class DSem:
    __slots__ = ("sem", "count")

    def __init__(self, sem):
        self.sem = sem
        self.count = 0


class Op:
    __slots__ = ("eng", "fn", "deps", "needed", "sem", "val", "inc", "dsem")

    def __init__(self, eng, fn):
        self.eng = eng
        self.fn = fn
        self.deps = []
        self.needed = False
        self.sem = None
        self.val = 0
        self.inc = 1
        self.dsem = None


def _flat(x):
    out = []
    for b in x:
        if isinstance(b, (list, tuple)):
            out.extend(_flat(b))
        elif b is not None:
            out.append(b)
    return out


def _writers(b):
    w = b.writer
    if w is None:
        return []
    return w if isinstance(w, list) else [w]


class Prog:
    ENGS = ("sp", "act", "dve", "pe", "pool")

    def __init__(self):
        self.ops = {e: [] for e in self.ENGS}
        self.defer = None

    COST = {"dve": 0.6, "pe": 0.27, "act": 0.45, "sp": 0.0, "pool": 0.0}

    def drain(self, fifo, budget):
        used = {e: 0.0 for e in self.ENGS}
        while fifo:
            eng0 = fifo[0][1]
            if used[eng0] > 0 and used[eng0] + self.COST[eng0] > budget[eng0]:
                break
            kind, eng, fn, dsem, r, w = fifo.pop(0)
            used[eng] += self.COST[eng]
            if kind == "op":
                self._add(Op(eng, fn), r, w)
            else:
                o = Op(eng, fn)
                o.dsem = dsem
                o.needed = True
                o.inc = 16
                self._add(o, r, w)

    def _add(self, op, reads, writes):
        reads = _flat(reads)
        writes = _flat(writes)
        deps = {}
        for b in reads:
            for wr in _writers(b):
                deps[id(wr)] = wr
        for b in writes:
            for wr in _writers(b):
                deps[id(wr)] = wr
            for r in b.readers.values():
                deps[id(r)] = r
        for d in deps.values():
            if d is op:
                continue
            if op.eng == "pe" and d.eng == "pe" and d.dsem is None:
                continue
            op.deps.append(d)
            d.needed = True
        for b in reads:
            key = ("d", id(op.dsem)) if op.dsem is not None else op.eng
            b.readers[key] = op
        for b in writes:
            b.writer = op
            b.readers = {}
        self.ops[op.eng].append(op)
        return op

    def op(self, eng, fn, r=(), w=()):
        if self.defer is not None:
            self.defer.append(("op", eng, fn, None, r, w))
            return None
        return self._add(Op(eng, fn), r, w)

    def dma(self, eng, fn, dsem, r=(), w=()):
        if self.defer is not None:
            self.defer.append(("dma", eng, fn, dsem, r, w))
            return None
        o = Op(eng, fn)
        o.dsem = dsem
        o.needed = True
        o.inc = 16
        return self._add(o, r, w)

    def resolve(self, engsem):
        for e in self.ENGS:
            cnt = 0
            for o in self.ops[e]:
                if o.dsem is not None:
                    o.dsem.count += 16
                    o.sem = o.dsem.sem
                    o.val = o.dsem.count
                elif o.needed:
                    cnt += 1
                    o.sem = engsem[e]
                    o.val = cnt

    def emit(self, eng_name, eng):
        waited = {}
        for o in self.ops[eng_name]:
            need = {}
            for d in o.deps:
                k = id(d.sem)
                if k not in need or need[k][1] < d.val:
                    need[k] = (d.sem, d.val)
            for k, (sem, val) in need.items():
                if waited.get(k, 0) >= val:
                    continue
                eng.wait_ge(sem, val)
                waited[k] = val
            ins = o.fn(eng)
            if o.needed:
                ins.then_inc(o.sem, o.inc)

    def clear(self):
        self.ops = {e: [] for e in self.ENGS}


def build_program(L0, L1, debug=False):
    assert L0 % T == 0 and L1 % T == 0
    nc = bass.Bass("TRN2", target_bir_lowering=False)
    RIN = L0 + L1 + 4 * HALO
    LT = L0 + L1

    def din(name, shape, dt=F32):
        return nc.dram_tensor(name, list(shape), dt, kind="ExternalInput").ap()

    xin = din("xin", [RIN, D])
    cT = din("cT", [128, KC, 2])
    wada_blocks = din("wada_blocks", [96, 128, KC, 128])
    badaT_d = din("badaT", [128, 96])
    g1T_d = din("g1T", [128, KC])
    g2T_d = din("g2T", [128, KC])
    gfin_b = din("gfin_b", [128, D])
    w_blocks = din("w_blocks", [NBLK, 128, KC, 128])
    convw_d = din("convw", [128, 8, 31])
    convb_d = din("convb", [128, 8])
    lng_d = din("lng", [128, 8])
    lnb_d = din("lnb", [128, 8])
    shortw_d = din("shortw", [128, 8, 3])
    keysT_d = din("keysT", [128, 16, 128])
    eu = din("expert_u", [NEXP, D])
    ev = din("expert_v", [NEXP, D])
    ident_d = din("ident", [128, 128])
    iota16_d = din("iota16", [128, 16])
    em_d = din("emask", [128, 4])
    yout = nc.dram_tensor("yout", [LT, D], F32, kind="ExternalOutput").ap()
    wbf = nc.dram_tensor("wbf_scratch", [NBLK, 128, KC * 128], BF16).ap()
    ubf = nc.dram_tensor("ubf_scratch", [NEXP, D], BF16).ap()
    vbf = nc.dram_tensor("vbf_scratch", [NEXP, D], BF16).ap()
    dbg = {}
    if debug:
        for nm, shp, dt in (("dbg_h2", [128, D], F32), ("dbg_eidx", [128, 128], I32),
                            ("dbg_gate", [128, 128], F32), ("dbg_act", [128, 128], F32),
                            ("dbg_x1", [128, D], F32), ("dbg_S", [128, 2048], F32),
                            ("dbg_us", [128, 16 * T], F32), ("dbg_peer", [128, D], F32)):
            dbg[nm] = nc.dram_tensor(nm, shp, dt, kind="ExternalOutput").ap()

    es = contextlib.ExitStack()
    with es:
        def sb(name, shape, dt=F32):
            return es.enter_context(nc.sbuf_tensor(name + "_sb", list(shape), dt))

        def ps(name, shape, dt=F32):
            return es.enter_context(nc.psum_tensor(name + "_ps", list(shape), dt))

        def newsem(name):
            return es.enter_context(nc.semaphore(name))

        engsem = {e: newsem("s_" + e) for e in Prog.ENGS}
        _dsn = [0]

        def dsem():
            _dsn[0] += 1
            return DSem(newsem("d%d" % _dsn[0]))

        P = Prog()

        ident_f = sb("ident_f", [128, 128]);
        ident_b = sb("ident_b", [128, 128], BF16)
        ones_f = sb("ones_f", [128, 128])
        iota16 = sb("iota16", [128, 16])
        em = sb("em", [128, 4])
        featv = sb("featv", [128, KC, 12])
        featmod = sb("featmod", [128, 96, 2])
        badaT = sb("badaT_s", [128, 96])
        g1T = sb("g1T_s", [128, KC])
        g2T = sb("g2T_s", [128, KC])
        cTs = sb("cTs", [128, KC, 2])
        sT = sb("sT", [128, KC, 2])
        g2b = sb("g2b", [128, D]);
        b2b = sb("b2b", [128, D]);
        gt2b = sb("gt2b", [128, D]);
        gfb = sb("gfb", [128, D])
        convw = sb("convw_s", [128, 8, 31]);
        convb = sb("convb_s", [128, 8])
        lng = sb("lng_s", [128, 8]);
        lnb = sb("lnb_s", [128, 8]);
        shortw = sb("shortw_s", [128, 8, 3])
        keysT = sb("keysT_s", [128, 16, 128])

        B_const = Buf("const")
        B_featv = Buf("featv")

        psA = [ps("psA%d" % i, [128, 512]) for i in range(4)]
        B_psA = [Buf("psA%d" % i) for i in range(4)]
        psZ = [ps("psZ%d" % i, [128, 512]) for i in range(2)]
        B_psZ = [Buf("psZ%d" % i) for i in range(2)]
        psZ_, B_psZ_ = psZ, B_psZ
        psX = ps("psX", [128, 512])
        B_psX = Buf("psX")
        psS = ps("psS", [128, 512])
        B_psS = Buf("psS")
        psXr = [psX, psS]
        B_psXr = [B_psX, B_psS]
        psTvr = [psXr[i][:, 0:256].bitcast(BF16).rearrange("p (j t) -> p j t", j=4) for i in range(2)]
        xrc = [0]

        xt = sb("xt", [128, NS, D])
        B_xt = [Buf("xt%d" % s) for s in range(NS)]
        xfin = sb("xfin", [128, D])
        B_xfin = Buf("xfin")
        h2b = sb("h2b", [128, 2, NS, D], BF16)
        B_h2b = [[Buf("h2b_%d_%d" % (a, s)) for s in range(NS)] for a in range(2)]
        sqa = sb("sqa", [128, D], BF16)
        B_sqa = Buf("sqa")
        sqd = sb("sqd", [128, D], BF16)
        B_sqd = Buf("sqd")
        stt = sb("stt", [128, 16])
        B_stt = Buf("stt")
        B_stt2 = Buf("stt2")
        hT = sb("hT", [128, KC, N], BF16)
        B_hT = Buf("hT")
        h2T = hT
        B_h2T = B_hT
        wst = sb("wst", [128, 2, KC * 128])
        B_wst = [Buf("wst0"), Buf("wst1")]
        d_wst = [dsem(), dsem()]
        NWB = 3
        wb = sb("wb", [128, NWB, KC, 128], BF16)
        B_wb = [Buf("wb%d" % i) for i in range(NWB)]
        d_wb = [dsem() for _ in range(NWB)]
        B_wbf = Buf("wbf")
        B_ubf = Buf("ubf")
        B_vbf = Buf("vbf")
        sg = sb("sg", [128, N]);
        B_sg = Buf("sg")
        u2 = sb("u", [128, 2, N], BF16);
        B_u2 = [Buf("u0"), Buf("u1")]
        NDC = 4
        dgc = sb("dgc", [128, NDC, 128], BF16)
        B_dgc = [Buf("dgc%d" % i) for i in range(NDC)]
        dgn = [0]
        xs_sb = sb("xs_sb", [128, N]);
        B_xs = Buf("xs")
        vv = sb("vv", [128, N]);
        B_vv = Buf("vv")
        acc = sb("acc", [128, T]);
        B_acc = Buf("acc")
        poolA = sb("poolA", [128, 4096])
        B_pA = [Buf("pA%d" % i) for i in range(16)]
        ucall = poolA[:, 0:2048].rearrange("p (c t) -> p c t", c=8)
        cat = poolA[:, 2048:2560].rearrange("p (a t) -> p a t", a=2)
        B_cat = [B_pA[8], B_pA[9]]
        lnm = poolA[:, 2560:2816];
        B_lnm = B_pA[10]
        lnr = poolA[:, 2816:3072];
        B_lnr = B_pA[11]
        tmp1 = poolA[:, 3072:3328];
        B_tmp1 = B_pA[12]
        tmp2 = poolA[:, 3328:3584];
        B_tmp2 = B_pA[13]
        Sall = poolA[:].rearrange("p (s q k) -> p s q k", s=2, q=16)

        def B_S(s, q):
            return B_pA[(s * 16 + q) // 2]

        usT = sb("usT", [128, KC, T], BF16)
        B_usT = [Buf("usT%d" % i) for i in range(KC)]
        mTs = sb("mTs", [128, 2, T]);
        B_mTs = [Buf("mTs0"), Buf("mTs1")]
        work = sb("work", [128, 256]);
        B_work = Buf("work")
        tv = sb("tv", [128, 16, 16]);
        B_tv = Buf("tv")
        ti_ = sb("ti", [128, 16, 16], U32);
        B_ti = Buf("ti")
        tif = sb("tif", [128, 16, 16]);
        B_tif = Buf("tif")
        xh = wst[:, 0, :]
        B_xh = B_wst[0]
        cand = wst[:, 0, :].rearrange("p (h c) -> p h c", h=8)
        B_cand = B_wst[0]
        oh = wst[:, 1, :].rearrange("p (h a b) -> p h a b", h=8, a=16)
        B_oh = B_wst[1]
        h2f = wst[:, 1, :]
        B_h2f = B_wst[1]
        cv = sb("cv", [128, 8, 16]);
        B_cv = Buf("cv")
        cpos = sb("cpos", [128, 8, 16], U32);
        B_cpos = Buf("cpos")
        cab = sb("cab", [128, 2, 128], U32);
        B_cab = Buf("cab")
        cabf = sb("cabf", [128, 2, 128]);
        B_cabf = Buf("cabf")
        isel = sb("isel", [128, 2, 128]);
        B_isel = Buf("isel")
        eidxf = sb("eidxf", [128, 128]);
        B_eidxf = Buf("eidxf")
        RJ = 4
        eidx = sb("eidx", [128, RJ, 128], I32);
        B_eidx = [Buf("eidx%d" % i) for i in range(RJ)]
        gate = sb("gate", [128, RJ, 128]);
        B_gate = [Buf("gate%d" % i) for i in range(RJ)]
        gsum = sb("gsum", [128, 8]);
        B_gsum = Buf("gsum")
        actv = sb("actv", [128, RJ, 128]);
        B_actv = [Buf("actv%d" % i) for i in range(RJ)]
        coef = sb("coef", [128, RJ, 128]);
        B_coef = [Buf("coef%d" % i) for i in range(RJ)]
        NG = 3
        GU = [sb("GU%d" % i, [128, D], BF16) for i in range(NG)]
        B_GU = [Buf("GU%d" % i) for i in range(NG)]
        d_GU = [dsem() for _ in range(NG)]
        GV = [sb("GV%d" % i, [128, D], BF16) for i in range(NG)]
        B_GV = [Buf("GV%d" % i) for i in range(NG)]
        d_GV = [dsem() for _ in range(NG)]
        diag = sb("diag", [128, 2, 128], BF16)
        B_diag = [Buf("diag0"), Buf("diag1")]
        tmpo = sb("tmpo", [128, 512]);
        B_tmpo = Buf("tmpo")
        d_x = [dsem() for _ in range(NS)]
        d_xh = dsem()
        d_st = [dsem() for _ in range(NS)]
        d_rl = dsem()
        d_y = dsem()
        B_ydram = [Buf("ydram%d" % i) for i in range(RJ)]

        def fv(seg, kk, kc):
            return featv[:, kc, seg * 6 + kk:seg * 6 + kk + 1]

        def rstd_from_ss(col, Bst, nparts=128):
            P.op("act", lambda e: e.activation(out=stt[0:nparts, col:col + 1], in_=stt[0:nparts, col:col + 1],
                                               func=AF.Sqrt, bias=RMS_EPS, scale=1.0 / D),
                 r=[Bst], w=[Bst])
            P.op("dve", lambda e: e.reciprocal(out=stt[0:nparts, col:col + 1], in_=stt[0:nparts, col:col + 1]),
                 r=[Bst], w=[Bst])

        def bcast_tiles(seg, which, banks=None, Bbanks=None):
            psZ, B_psZ = (banks, Bbanks) if banks is not None else (psZ_, B_psZ_)
            if banks is None:
                scr = [(mTs[:, 0, 0:128], B_mTs[0]), (mTs[:, 1, 0:128], B_mTs[1])]
            else:
                scr = [(tmpo[:, 0:128], B_tmpo), (tmpo[:, 128:256], B_tmpo)]
            for (dst, kk) in which:
                for kc in range(KC):
                    sv, Bsv = scr[kc % 2]
                    P.op("dve", lambda e, sv=sv, kc=kc, kk=kk, seg=seg: e.tensor_scalar(
                        out=sv, in0=ident_f[:], scalar1=fv(seg, kk, kc), scalar2=None,
                        op0=ALU.mult), r=[B_const, B_featv], w=[Bsv])
                    n = (kc // 4) % 2
                    P.op("pe", lambda e, sv=sv, kc=kc, n=n: e.matmul(
                        psZ[n][:, (kc % 4) * 128:(kc % 4 + 1) * 128], lhsT=ones_f[:], rhs=sv,
                        start=True, stop=True), r=[Bsv, B_const], w=[B_psZ[n]])
                    if kc % 4 == 3:
                        c4 = kc // 4
                        P.op("dve", lambda e, dst=dst, n=n, c4=c4: e.tensor_copy(
                            out=dst[:, c4 * 512:(c4 + 1) * 512], in_=psZ[n][:, :]), r=[B_psZ[n]], w=[B_bc[id(dst)]])

        B_bc = {id(b2b): Buf("b2b"), id(g2b): Buf("g2b"), id(gt2b): Buf("gt2b")}

        d_misc = dsem()
        B_misc = Buf("misc")
        loads = [(ident_f[:], ident_d), (iota16[:], iota16_d), (em[:], em_d),
                 (gfb[:], gfin_b), (convw[:], convw_d), (convb[:], convb_d), (lng[:], lng_d),
                 (lnb[:], lnb_d), (shortw[:], shortw_d), (keysT[:], keysT_d), (cTs[:], cT),
                 (badaT[:], badaT_d), (g1T[:], g1T_d), (g2T[:], g2T_d)]
        for (dst, src) in loads:
            P.dma("sp", lambda e, dst=dst, src=src: e.dma_start(out=dst, in_=src), d_misc, w=[B_misc])
        P.op("dve", lambda e: e.tensor_copy(out=ident_b[:], in_=ident_f[:]), r=[B_misc], w=[B_const])
        P.op("dve", lambda e: e.memset(ones_f[:], 1.0), w=[B_const])
        B_sT = Buf("sT")
        P.op("act", lambda e: e.activation(out=sT[:], in_=cTs[:], func=AF.Silu), r=[B_misc], w=[B_sT])
        for oc in range(96):
            i = oc % 2
            P.dma("sp", lambda e, i=i, oc=oc: e.dma_start(
                out=wst[:, i, :], in_=wada_blocks[oc].rearrange("p k c -> p (k c)")), d_wst[i], w=[B_wst[i]])
            for kc in range(KC):
                P.op("pe", lambda e, i=i, oc=oc, kc=kc: e.matmul(
                    psS[:, oc * 2:(oc + 1) * 2], lhsT=wst[:, i, kc * 128:(kc + 1) * 128], rhs=sT[:, kc, :],
                    start=(kc == 0), stop=(kc == KC - 1)), r=[B_wst[i], B_sT], w=[B_psS])
        B_fm = Buf("featmod")
        P.op("dve", lambda e: e.tensor_tensor(
            out=featmod[:], in0=psS[:, 0:192].rearrange("p (o r) -> p o r", r=2),
            in1=badaT[:].unsqueeze(2).to_broadcast([128, 96, 2]), op=ALU.add), r=[B_psS, B_misc], w=[B_fm])
        for seg_ in range(2):
            for kk in range(6):
                src = featmod[:, kk * 16:(kk + 1) * 16, seg_]
                dst = featv[:, :, seg_ * 6 + kk]
                if kk in (1, 4):
                    gT = g1T if kk == 1 else g2T
                    P.op("dve", lambda e, src=src, dst=dst, gT=gT: e.scalar_tensor_tensor(
                        out=dst, in0=src, scalar=1.0, in1=gT[:], op0=ALU.add, op1=ALU.mult),
                         r=[B_fm, B_misc], w=[B_featv])
                else:
                    P.op("dve", lambda e, src=src, dst=dst: e.tensor_copy(out=dst, in_=src), r=[B_fm], w=[B_featv])

        stage_bf = [(GU[i], B_GU[i], d_GU[i]) for i in range(NG)] + [(GV[i], B_GV[i], d_GV[i]) for i in range(NG)]
        cnt_ = [0]
        last_store = {}

        def convert(src_ap, dst_ap, Bdst):
            k = cnt_[0]
            cnt_[0] += 1
            i = k % 2
            bt, Bbt, dbt = stage_bf[k % len(stage_bf)]
            P.dma("sp", lambda e, i=i, src_ap=src_ap: e.dma_start(out=wst[:, i, :], in_=src_ap), d_wst[i],
                  w=[B_wst[i]])
            if k % 2 == 0:
                P.op("act", lambda e, i=i, bt=bt: e.activation(out=bt[:], in_=wst[:, i, :], func=AF.Copy),
                     r=[B_wst[i]], w=[Bbt])
            else:
                P.op("dve", lambda e, i=i, bt=bt: e.tensor_copy(out=bt[:], in_=wst[:, i, :]), r=[B_wst[i]], w=[Bbt])
            o = P.dma("pool", lambda e, bt=bt, dst_ap=dst_ap: e.dma_start(out=dst_ap, in_=bt[:]), dbt,
                      r=[Bbt])
            last_store.setdefault(id(Bdst), {})[id(dbt)] = o

        for blk in range(NBLK):
            convert(w_blocks[blk].rearrange("p k c -> p (k c)"), wbf[blk], B_wbf)
        for r_ in range(NEXP // 128):
            convert(eu[r_ * 128:(r_ + 1) * 128, :], ubf[r_ * 128:(r_ + 1) * 128, :], B_ubf)
            convert(ev[r_ * 128:(r_ + 1) * 128, :], vbf[r_ * 128:(r_ + 1) * 128, :], B_vbf)
        for Bd in (B_wbf, B_ubf, B_vbf):
            Bd.writer = list(last_store[id(Bd)].values())
        print("[kernel] sbuf bytes remaining per partition:", nc.sbuf_bytes_remaining)

        tiles = []
        for (seg, xbase, ybase, ntiles) in ((0, HALO, 0, L0 // T), (1, L0 + 3 * HALO, L0, L1 // T)):
            for it in range(ntiles):
                tiles.append((seg, it, ntiles, xbase + it * T, ybase + it * T))
        NT = len(tiles)
        NJ = NT * NS

        wload = [0]

        def prefetch_to(g):
            lim = min(g, NT * NBLK - 1)
            while wload[0] <= lim:
                k = wload[0]
                wload[0] += 1
                i = k % NWB
                blk = k % NBLK
                P.dma("sp", lambda e, i=i, blk=blk: e.dma_start(
                    out=wb[:, i].rearrange("p k c -> p (k c)"), in_=wbf[blk]), d_wb[i], r=[B_wbf], w=[B_wb[i]])

        zr = [0]

        def zmm(g, ncols, rhsT, Brhs):
            prefetch_to(g + 2)
            i = g % NWB
            wv, Bw = wb[:, i], B_wb[i]
            bi = zr[0] % 2
            zr[0] += 1
            for kc in range(KC):
                P.op("pe", lambda e, bi=bi, kc=kc, wv=wv, ncols=ncols, rhsT=rhsT: e.matmul(
                    psZ[bi][:, 0:ncols], lhsT=wv[:, kc, :], rhs=rhsT[:, kc, 0:ncols],
                    start=(kc == 0), stop=(kc == KC - 1)), r=[Bw, Brhs], w=[B_psZ[bi]])
            return psZ[bi], B_psZ[bi]

        def transposes_bf(src, npart, dsts, Bsrc, Bdst, scale_bias=None):
            for g4 in range(4):
                xi = xrc[0] % 2
                xrc[0] += 1
                psTv = psTvr[xi]
                B_psX = B_psXr[xi]
                for j in range(4):
                    kc = g4 * 4 + j
                    P.op("pe", lambda e, kc=kc, j=j, psTv=psTv: e.transpose(
                        psTv[:, j, 0:npart], src[0:npart, kc * 128:(kc + 1) * 128], ident_b[0:npart, 0:npart]),
                         r=[Bsrc, B_const], w=[B_psX])
                for j in range(4):
                    kc = g4 * 4 + j
                    for (dfn, p0, w_) in dsts:
                        dst = dfn(kc)
                        if scale_bias is not None:
                            sc, bi_ = scale_bias(kc)
                            if j % 2 == 0:
                                P.op("dve", lambda e, dst=dst, j=j, p0=p0, w_=w_, sc=sc, bi_=bi_, psTv=psTv: e.tensor_scalar(
                                    out=dst, in0=psTv[:, j, p0:p0 + w_], scalar1=sc, scalar2=bi_,
                                    op0=ALU.mult, op1=ALU.add), r=[B_psX, B_featv], w=[Bdst])
                            else:
                                P.op("act", lambda e, dst=dst, j=j, p0=p0, w_=w_, sc=sc, bi_=bi_, psTv=psTv: e.activation(
                                    out=dst, in_=psTv[:, j, p0:p0 + w_], func=AF.Identity, scale=sc, bias=bi_),
                                     r=[B_psX, B_featv], w=[Bdst])
                        else:
                            if j % 2 == 0:
                                P.op("dve", lambda e, dst=dst, j=j, p0=p0, w_=w_, psTv=psTv: e.tensor_copy(
                                    out=dst, in_=psTv[:, j, p0:p0 + w_]), r=[B_psX], w=[Bdst])
                            else:
                                P.op("act", lambda e, dst=dst, j=j, p0=p0, w_=w_, psTv=psTv: e.activation(
                                    out=dst, in_=psTv[:, j, p0:p0 + w_], func=AF.Copy), r=[B_psX], w=[Bdst])

        def M_units(ti):
            seg, it, ntiles, r0, y0 = tiles[ti]
            edgeL = (it == 0)
            edgeR = (it == ntiles - 1)
            hp_ = ti % 2
            g0 = ti * NBLK
            units = []

            def add(cost, fn):
                units.append((cost, fn))

            if it == 0:
                add(6, lambda: bcast_tiles(seg, ((b2b, 3), (g2b, 4))))

            def st_1a():
                for s in range(NS):
                    P.dma("sp", lambda e, s=s: e.dma_start(
                        out=xt[:, s, :], in_=xin[r0 + 128 * s:r0 + 128 * (s + 1), :]), d_x[s], w=[B_xt[s]])
                P.dma("sp", lambda e: e.dma_start(out=xh[0:HALO, :], in_=xin[r0 - HALO:r0, :]), d_xh, w=[B_xh])
                P.dma("sp", lambda e: e.dma_start(out=xh[HALO:2 * HALO, :], in_=xin[r0 + T:r0 + T + HALO, :]),
                      d_xh, w=[B_xh])
                prefetch_to(g0 + 1)

            add(1, st_1a)

            def st_1b(pt):
                if pt < NS:
                    src, Bs, npart = xt[:, pt, :], B_xt[pt], 128
                else:
                    src, Bs, npart = xh, B_xh, 2 * HALO
                xn, B_xn = h2b[:, hp_, pt % 2, :], B_h2b[hp_][pt % 2]
                P.op("act", lambda e: e.activation(
                    out=sqa[0:npart, :], in_=src[0:npart, :], func=AF.Square,
                    accum_out=stt[0:npart, pt:pt + 1]), r=[Bs], w=[B_sqa, B_stt])
                rstd_from_ss(pt, B_stt, npart)
                P.op("act", lambda e: e.activation(
                    out=xn[0:npart, :], in_=src[0:npart, :], func=AF.Copy,
                    scale=stt[0:npart, pt:pt + 1]), r=[Bs, B_stt], w=[B_xn])
                if pt < NS:
                    dsts = [(lambda kc: hT[:, kc, HALO + 128 * pt:HALO + 128 * (pt + 1)], 0, 128)]
                else:
                    dsts = [(lambda kc: hT[:, kc, 0:HALO], 0, HALO),
                            (lambda kc: hT[:, kc, HALO + T:N], HALO, HALO)]
                transposes_bf(xn, npart, dsts, B_xn, B_hT, scale_bias=lambda kc: (fv(seg, 1, kc), fv(seg, 0, kc)))

            for pt in range(NS + 1):
                add(8, lambda pt=pt: st_1b(pt))

            def conf_chunk(c):
                za, Bza = zmm(g0 + 2 * c, N, hT, B_hT)
                zg, Bzg = zmm(g0 + 2 * c + 1, N, hT, B_hT)
                ui = c % 2
                u, B_u = u2[:, ui, :], B_u2[ui]
                P.op("act", lambda e: e.activation(out=sg[:], in_=zg[:, 0:N], func=AF.Sigmoid), r=[Bzg], w=[B_sg])
                P.op("dve", lambda e: e.tensor_tensor(out=u, in0=za[:, 0:N], in1=sg[:], op=ALU.mult),
                     r=[Bza, B_sg], w=[B_u])
                if edgeL:
                    P.op("dve", lambda e: e.tensor_scalar(
                        out=u[:, 0:HALO], in0=u[:, 0:HALO], scalar1=em[:, 2 * seg:2 * seg + 1], scalar2=None,
                        op0=ALU.mult), r=[B_u, B_const], w=[B_u])
                if edgeR:
                    P.op("dve", lambda e: e.tensor_scalar(
                        out=u[:, HALO + T:N], in0=u[:, HALO + T:N], scalar1=em[:, 2 * seg + 1:2 * seg + 2],
                        scalar2=None, op0=ALU.mult), r=[B_u, B_const], w=[B_u])
                cb = zr[0] % 2
                zr[0] += 1
                for k in range(31):
                    di = dgn[0] % NDC
                    dgn[0] += 1
                    P.op("act", lambda e, k=k, di=di: e.activation(
                        out=dgc[:, di, :], in_=ident_f[:], func=AF.Copy, scale=convw[:, c, k:k + 1]),
                         r=[B_const], w=[B_dgc[di]])
                    P.op("pe", lambda e, k=k, di=di: e.matmul(
                        psZ[cb][:, 0:T], lhsT=dgc[:, di, :], rhs=u[:, k:k + T], start=(k == 0), stop=(k == 30)),
                         r=[B_dgc[di], B_u], w=[B_psZ[cb]])
                P.op("act", lambda e: e.activation(out=cat[:, 0, :], in_=psZ[cb][:, 0:T], func=AF.Identity,
                                                   bias=convb[:, c:c + 1], scale=1.0),
                     r=[B_psZ[cb], B_const], w=[B_cat[0]])
                P.op("act", lambda e: e.activation(out=cat[:, 1, :], in_=psZ[cb][:, 0:T], func=AF.Square,
                                                   bias=convb[:, c:c + 1], scale=1.0),
                     r=[B_psZ[cb], B_const], w=[B_cat[1]])
                P.op("dve", lambda e: e.tensor_copy(out=ucall[:, c, :], in_=cat[:, 0, :]),
                     r=[B_cat[0]], w=[B_pA[c]])
                P.op("pe", lambda e: e.matmul(
                    psS[:, :], lhsT=ones_f[:], rhs=cat[:].rearrange("p a t -> p (a t)"),
                    start=(c == 0), stop=(c == 7)), r=[B_cat, B_const], w=[B_psS])

            for c in range(8):
                add(25, lambda c=c: conf_chunk(c))

            def ln_stats():
                P.op("dve", lambda e: e.tensor_scalar(out=lnm, in0=psS[:, 0:T], scalar1=1.0 / 1024, scalar2=None,
                                                      op0=ALU.mult), r=[B_psS], w=[B_lnm])
                P.op("dve", lambda e: e.tensor_tensor(out=tmp1, in0=lnm, in1=lnm, op=ALU.mult),
                     r=[B_lnm], w=[B_tmp1])
                P.op("dve", lambda e: e.scalar_tensor_tensor(out=tmp2, in0=psS[:, T:2 * T], scalar=1.0 / 1024,
                                                             in1=tmp1, op0=ALU.mult, op1=ALU.subtract),
                     r=[B_psS, B_tmp1], w=[B_tmp2])
                P.op("act", lambda e: e.activation(out=tmp2, in_=tmp2, func=AF.Sqrt, bias=LN_EPS, scale=1.0),
                     r=[B_tmp2], w=[B_tmp2])
                P.op("dve", lambda e: e.reciprocal(out=lnr, in_=tmp2), r=[B_tmp2], w=[B_lnr])

            add(4, ln_stats)

            def ln_apply(c):
                P.op("dve", lambda e: e.tensor_tensor(out=tmp1, in0=ucall[:, c, :], in1=lnm, op=ALU.subtract),
                     r=[B_pA[c], B_lnm], w=[B_tmp1])
                P.op("dve", lambda e: e.tensor_tensor(out=tmp1, in0=tmp1, in1=lnr, op=ALU.mult),
                     r=[B_tmp1, B_lnr], w=[B_tmp1])
                P.op("act", lambda e: e.activation(out=usT[:, c, :], in_=tmp1, func=AF.Silu,
                                                   scale=lng[:, c:c + 1], bias=lnb[:, c:c + 1]),
                     r=[B_tmp1, B_const], w=[B_usT[c]])

            for c in range(8):
                add(2, lambda c=c: ln_apply(c))

            def short_chunk(c):
                gb = g0 + 16 + 3 * c
                zx, Bzx = zmm(gb, N, hT, B_hT)
                zc, Bzc = zmm(gb + 1, N, hT, B_hT)
                P.op("act", lambda e: e.activation(out=xs_sb[:], in_=zx[:, 0:N], func=AF.Copy), r=[Bzx], w=[B_xs])
                P.op("dve", lambda e: e.tensor_tensor(out=vv[:], in0=zc[:, 0:N], in1=xs_sb[:], op=ALU.mult),
                     r=[Bzc, B_xs], w=[B_vv])
                zb, Bzb = zmm(gb + 2, N, hT, B_hT)
                if edgeL:
                    P.op("dve", lambda e: e.tensor_scalar(
                        out=vv[:, 0:HALO], in0=vv[:, 0:HALO], scalar1=em[:, 2 * seg:2 * seg + 1], scalar2=None,
                        op0=ALU.mult), r=[B_vv, B_const], w=[B_vv])
                if edgeR:
                    P.op("dve", lambda e: e.tensor_scalar(
                        out=vv[:, HALO + T:N], in0=vv[:, HALO + T:N], scalar1=em[:, 2 * seg + 1:2 * seg + 2],
                        scalar2=None, op0=ALU.mult), r=[B_vv, B_const], w=[B_vv])
                P.op("dve", lambda e: e.tensor_scalar(
                    out=acc[:], in0=vv[:, HALO - 1:HALO - 1 + T], scalar1=shortw[:, c, 0:1], scalar2=None,
                    op0=ALU.mult), r=[B_vv, B_const], w=[B_acc])
                for k in (1, 2):
                    P.op("dve", lambda e, k=k: e.scalar_tensor_tensor(
                        out=acc[:], in0=vv[:, HALO - 1 + k:HALO - 1 + k + T], scalar=shortw[:, c, k:k + 1],
                        in1=acc[:], op0=ALU.mult, op1=ALU.add), r=[B_vv, B_acc, B_const], w=[B_acc])
                P.op("dve", lambda e: e.tensor_tensor(
                    out=usT[:, 8 + c, :], in0=zb[:, HALO:HALO + T], in1=acc[:], op=ALU.mult),
                     r=[Bzb, B_acc], w=[B_usT[8 + c]])

            for c in range(8):
                add(12, lambda c=c: short_chunk(c))

            def wout_chunk(dc):
                mp, Bmp = zmm(g0 + 40 + dc, T, usT, B_usT)
                mi = dc % 2
                psX, B_psX = psXr[dc % 2], B_psXr[dc % 2]
                P.op("act", lambda e: e.activation(
                    out=mTs[:, mi, :], in_=mp[:, 0:T], func=AF.Copy, scale=fv(seg, 2, dc)),
                     r=[Bmp, B_featv], w=[B_mTs[mi]])
                for s in range(NS):
                    P.op("pe", lambda e, s=s: e.transpose(
                        psX[:, s * 128:(s + 1) * 128], mTs[:, mi, s * 128:(s + 1) * 128], ident_f[:]),
                         r=[B_mTs[mi], B_const], w=[B_psX])
                for s in range(NS):
                    P.op("dve", lambda e, s=s: e.tensor_tensor(
                        out=xt[:, s, dc * 128:(dc + 1) * 128], in0=xt[:, s, dc * 128:(dc + 1) * 128],
                        in1=psX[:, s * 128:(s + 1) * 128], op=ALU.add),
                         r=[B_xt[s], B_psX], w=[B_xt[s]])

            for dc in range(KC):
                add(4, lambda dc=dc: wout_chunk(dc))

            def st_3(s):
                j = ti * NS + s
                P.dma("sp", lambda e: e.dma_start(out=yout[y0 + 128 * s:y0 + 128 * (s + 1), :], in_=xt[:, s, :]),
                      d_st[s], r=[B_xt[s]], w=[B_ydram[j % RJ]])
                P.op("act", lambda e: e.activation(
                    out=sqa[:], in_=xt[:, s, :], func=AF.Square, accum_out=stt[:, 4 + s:5 + s]),
                     r=[B_xt[s]], w=[B_sqa, B_stt])
                rstd_from_ss(4 + s, B_stt)
                P.op("dve", lambda e: e.scalar_tensor_tensor(
                    out=h2f, in0=xt[:, s, :], scalar=stt[:, 4 + s:5 + s], in1=g2b[:],
                    op0=ALU.mult, op1=ALU.mult), r=[B_xt[s], B_stt, B_bc[id(g2b)]], w=[B_h2f])
                P.op("dve", lambda e: e.tensor_tensor(
                    out=h2b[:, hp_, s, :], in0=h2f, in1=b2b[:], op=ALU.add),
                     r=[B_h2f, B_bc[id(b2b)]], w=[B_h2b[hp_][s]])
                dsts = [(lambda kc: h2T[:, kc, 128 * s:128 * (s + 1)], 0, 128)]
                transposes_bf(h2b[:, hp_, s, :], 128, dsts, B_h2b[hp_][s], B_h2T)

            for s in range(NS):
                add(12, lambda s=s: st_3(s))

            def q_chunk(qc):
                qp, Bqp = zmm(g0 + 56 + qc, T, h2T, B_h2T)
                mi = qc % 2
                psX, B_psX = psXr[qc % 2], B_psXr[qc % 2]
                P.op("act", lambda e: e.activation(out=mTs[:, mi, :], in_=qp[:, 0:T], func=AF.Copy),
                     r=[Bqp], w=[B_mTs[mi]])
                for s in range(NS):
                    P.op("pe", lambda e, s=s: e.matmul(
                        psX[:, s * 128:(s + 1) * 128], lhsT=mTs[:, mi, s * 128:(s + 1) * 128],
                        rhs=keysT[:, qc, :], start=True, stop=True),
                         r=[B_mTs[mi], B_const], w=[B_psX])
                if qc % 2 == 0:
                    P.op("dve", lambda e: e.tensor_copy(
                        out=Sall[:, :, qc, :], in_=psX[:, 0:256].rearrange("p (s k) -> p s k", s=2)),
                         r=[B_psX], w=[B_S(0, qc), B_S(1, qc)])
                else:
                    P.op("act", lambda e: e.activation(
                        out=Sall[:, :, qc, :], in_=psX[:, 0:256].rearrange("p (s k) -> p s k", s=2), func=AF.Copy),
                         r=[B_psX], w=[B_S(0, qc), B_S(1, qc)])

            for qc in range(16):
                add(4, lambda qc=qc: q_chunk(qc))

            def retr_topk(s, hps):
                B_Ss = [B_S(s, q) for q in range(16)]
                for hp in hps:
                    Sv = Sall[:, s, hp, :]
                    P.op("dve", lambda e, hp=hp, Sv=Sv: e.max(out=tv[:, hp, 0:8], in_=Sv), r=[B_Ss], w=[B_tv])
                    P.op("dve", lambda e, hp=hp, Sv=Sv: e.match_replace(
                        out=work[:, 0:128], in_to_replace=tv[:, hp, 0:8], in_values=Sv, imm_value=-1e30),
                         r=[B_Ss, B_tv], w=[B_work])
                    P.op("dve", lambda e, hp=hp: e.max(out=tv[:, hp, 8:16], in_=work[:, 0:128]),
                         r=[B_work], w=[B_tv])
                    P.op("dve", lambda e, hp=hp, Sv=Sv: e.max_index(
                        out=ti_[:, hp, 0:8], in_max=tv[:, hp, 0:8], in_values=Sv), r=[B_Ss, B_tv], w=[B_ti])
                    P.op("dve", lambda e, hp=hp, Sv=Sv: e.max_index(
                        out=ti_[:, hp, 8:16], in_max=tv[:, hp, 8:16], in_values=Sv), r=[B_Ss, B_tv], w=[B_ti])

            def retr_cand(s):
                P.op("dve", lambda e: e.tensor_copy(out=tif[:], in_=ti_[:]), r=[B_ti], w=[B_tif])
                tv4 = tv[:].rearrange("p (h t) k -> p h t k", t=2)
                P.op("dve", lambda e: e.tensor_tensor(
                    out=cand.rearrange("p h (a b) -> p h a b", a=16),
                    in0=tv4[:, :, 0, :].unsqueeze(3).to_broadcast([128, 8, 16, 16]),
                    in1=tv4[:, :, 1, :].unsqueeze(2).to_broadcast([128, 8, 16, 16]), op=ALU.add),
                     r=[B_tv], w=[B_cand])
                for h in range(8):
                    P.op("dve", lambda e, h=h: e.max(out=cv[:, h, 0:8], in_=cand[:, h, :]), r=[B_cand], w=[B_cv])
                    P.op("dve", lambda e, h=h: e.match_replace(
                        out=work[:, :], in_to_replace=cv[:, h, 0:8], in_values=cand[:, h, :], imm_value=-1e30),
                         r=[B_cand, B_cv], w=[B_work])
                    P.op("dve", lambda e, h=h: e.max(out=cv[:, h, 8:16], in_=work[:, :]), r=[B_work], w=[B_cv])
                    P.op("dve", lambda e, h=h: e.max_index(
                        out=cpos[:, h, 0:8], in_max=cv[:, h, 0:8], in_values=cand[:, h, :]),
                         r=[B_cand, B_cv], w=[B_cpos])
                    P.op("dve", lambda e, h=h: e.max_index(
                        out=cpos[:, h, 8:16], in_max=cv[:, h, 8:16], in_values=cand[:, h, :]),
                         r=[B_cand, B_cv], w=[B_cpos])

            def retr_idx(s):
                j = ti * NS + s
                jr = j % RJ
                cposf = cpos[:].rearrange("p h k -> p (h k)")
                P.op("dve", lambda e: e.tensor_single_scalar(out=cab[:, 0, :], in_=cposf, scalar=4,
                                                             op=ALU.logical_shift_right), r=[B_cpos], w=[B_cab])
                P.op("dve", lambda e: e.tensor_single_scalar(out=cab[:, 1, :], in_=cposf, scalar=15,
                                                             op=ALU.bitwise_and), r=[B_cpos], w=[B_cab])
                P.op("dve", lambda e: e.tensor_copy(out=cabf[:], in_=cab[:]), r=[B_cab], w=[B_cabf])
                tif4 = tif[:].rearrange("p (h t) k -> p h t k", t=2)
                for t_ in range(2):
                    P.op("dve", lambda e, t_=t_: e.tensor_tensor(
                        out=oh,
                        in0=cabf[:, t_, :].rearrange("p (h k) -> p h k", h=8).unsqueeze(3).to_broadcast(
                            [128, 8, 16, 16]),
                        in1=iota16[:].unsqueeze(1).unsqueeze(1).to_broadcast([128, 8, 16, 16]),
                        op=ALU.is_equal), r=[B_cabf, B_const], w=[B_oh])
                    P.op("dve", lambda e, t_=t_: e.tensor_tensor(
                        out=oh, in0=oh, in1=tif4[:, :, t_, :].unsqueeze(2).to_broadcast([128, 8, 16, 16]),
                        op=ALU.mult), r=[B_oh, B_tif], w=[B_oh])
                    P.op("dve", lambda e, t_=t_: e.tensor_reduce(
                        out=isel[:, t_, :].rearrange("p (h k) -> p h k", h=8), in_=oh, axis=AX.X, op=ALU.add),
                         r=[B_oh], w=[B_isel])
                P.op("dve", lambda e: e.scalar_tensor_tensor(
                    out=eidxf[:], in0=isel[:, 0, :], scalar=128.0, in1=isel[:, 1, :], op0=ALU.mult, op1=ALU.add),
                     r=[B_isel], w=[B_eidxf])
                P.op("dve", lambda e: e.tensor_copy(out=eidx[:, jr, :], in_=eidxf[:]), r=[B_eidxf], w=[B_eidx[jr]])
                gv_ = gate[:, jr, :].rearrange("p (h k) -> p h k", h=8)
                P.op("dve", lambda e: e.tensor_tensor(
                    out=gv_, in0=cv[:], in1=cv[:, :, 0:1].to_broadcast([128, 8, 16]), op=ALU.subtract),
                     r=[B_cv], w=[B_gate[jr]])
                P.op("act", lambda e: e.activation(out=gv_, in_=gv_, func=AF.Exp), r=[B_gate[jr]], w=[B_gate[jr]])
                P.op("dve", lambda e: e.tensor_reduce(out=gsum[:], in_=gv_, axis=AX.X, op=ALU.add),
                     r=[B_gate[jr]], w=[B_gsum])
                P.op("dve", lambda e: e.reciprocal(out=gsum[:], in_=gsum[:]), r=[B_gsum], w=[B_gsum])
                P.op("dve", lambda e: e.tensor_tensor(
                    out=gv_, in0=gv_, in1=gsum[:].unsqueeze(2).to_broadcast([128, 8, 16]), op=ALU.mult),
                     r=[B_gate[jr], B_gsum], w=[B_gate[jr]])

            for s in range(NS):
                add(12, lambda s=s: retr_topk(s, range(0, 8)))
                add(12, lambda s=s: retr_topk(s, range(8, 16)))
                add(20, lambda s=s: retr_cand(s))
                add(20, lambda s=s: retr_idx(s))
            return units

        gcnt = {"u": 0, "v": 0, "d": 0}

        def U_slot(j, slot):
            jr = j % RJ
            hp_, s = (j // NS) % 2, j % NS
            gi = gcnt["u"] % NG
            gcnt["u"] += 1
            P.dma("pool", lambda e: e.indirect_dma_start(
                out=GU[gi][:, :], out_offset=None, in_=ubf,
                in_offset=bass.IndirectOffsetOnAxis(ap=eidx[:, jr, slot:slot + 1], axis=0)),
                  d_GU[gi], r=[B_eidx[jr], B_ubf], w=[B_GU[gi]])
            P.op("dve", lambda e: e.scalar_tensor_tensor(
                out=sqd[:], in0=GU[gi][:], scalar=1.0, in1=h2b[:, hp_, s, :], op0=ALU.mult, op1=ALU.mult,
                accum_out=actv[:, jr, slot:slot + 1]), r=[B_GU[gi], B_h2b[hp_][s]], w=[B_sqd, B_actv[jr]])

        def U_finish(j):
            jr = j % RJ
            P.op("act", lambda e: e.activation(out=coef[:, jr, :], in_=actv[:, jr, :], func=AF.Gelu),
                 r=[B_actv[jr]], w=[B_coef[jr]])
            P.op("dve", lambda e: e.tensor_tensor(out=coef[:, jr, :], in0=coef[:, jr, :], in1=gate[:, jr, :],
                                                  op=ALU.mult), r=[B_coef[jr], B_gate[jr]], w=[B_coef[jr]])

        def V_slot(j, slot):
            jr = j % RJ
            gi = gcnt["v"] % NG
            gcnt["v"] += 1
            di = gcnt["d"] % 2
            gcnt["d"] += 1
            P.dma("pool", lambda e: e.indirect_dma_start(
                out=GV[gi][:, :], out_offset=None, in_=vbf,
                in_offset=bass.IndirectOffsetOnAxis(ap=eidx[:, jr, slot:slot + 1], axis=0)),
                  d_GV[gi], r=[B_eidx[jr], B_vbf], w=[B_GV[gi]])
            P.op("act", lambda e: e.activation(
                out=diag[:, di, :], in_=ident_f[:], func=AF.Copy, scale=coef[:, jr, slot:slot + 1]),
                 r=[B_coef[jr], B_const], w=[B_diag[di]])
            for n in range(4):
                P.op("pe", lambda e, n=n: e.matmul(
                    psA[n][:, :], lhsT=diag[:, di, :], rhs=GV[gi][:, n * 512:(n + 1) * 512],
                    start=(slot == 0), stop=(slot == 127)),
                     r=[B_diag[di], B_GV[gi]], w=[B_psA[n]])

        def V_start(j):
            ti = j // NS
            s = j % NS
            y0 = tiles[ti][4]
            P.dma("act", lambda e: e.dma_start(out=xfin[:], in_=yout[y0 + 128 * s:y0 + 128 * (s + 1), :]),
                  d_rl, r=[B_ydram[j % RJ]], w=[B_xfin])

        def V_finish(j):
            ti = j // NS
            s = j % NS
            seg, it, ntiles, r0, y0 = tiles[ti]
            for n in range(4):
                P.op("dve", lambda e, n=n: e.tensor_tensor(
                    out=tmpo[:], in0=psA[n][:, :], in1=gt2b[:, n * 512:(n + 1) * 512], op=ALU.mult),
                     r=[B_psA[n], B_bc[id(gt2b)]], w=[B_tmpo])
                P.op("dve", lambda e, n=n: e.tensor_tensor(
                    out=xfin[:, n * 512:(n + 1) * 512], in0=xfin[:, n * 512:(n + 1) * 512], in1=tmpo[:],
                    op=ALU.add), r=[B_tmpo, B_xfin], w=[B_xfin])
            P.op("act", lambda e: e.activation(
                out=sqa[:], in_=xfin[:], func=AF.Square, accum_out=stt[:, 8:9]),
                 r=[B_xfin], w=[B_sqa, B_stt2])
            rstd_from_ss(8, B_stt2)
            P.op("dve", lambda e: e.scalar_tensor_tensor(
                out=xfin[:], in0=xfin[:], scalar=stt[:, 8:9], in1=gfb[:],
                op0=ALU.mult, op1=ALU.mult), r=[B_xfin, B_stt2, B_misc], w=[B_xfin])
            P.dma("act", lambda e: e.dma_start(out=yout[y0 + 128 * s:y0 + 128 * (s + 1), :], in_=xfin[:]),
                  d_y, r=[B_xfin], w=[B_ydram[j % RJ]])
            if j + 1 < NJ and (j + 1) % NS == 0 and tiles[(j + 1) // NS][1] == 0:
                bcast_tiles(tiles[(j + 1) // NS][0], ((gt2b, 5),), psA[0:2], B_psA[0:2])

        for (c_, fn) in M_units(0):
            fn()

        class MSched:
            def __init__(self, fifo, nhalf):
                self.items = fifo
                n = len(fifo)
                self.deps = [None] * n
                self.when = [None] * n
                self.first = 0
                writer, readers = {}, {}
                for i, (kind, eng, fn, dsem, r, w) in enumerate(fifo):
                    d = set()
                    rr, ww = _flat(r), _flat(w)
                    for b_ in rr:
                        if id(b_) in writer:
                            d.add(writer[id(b_)])
                    for b_ in ww:
                        if id(b_) in writer:
                            d.add(writer[id(b_)])
                        d.update(readers.get(id(b_), ()))
                    d.discard(i)
                    self.deps[i] = d
                    for b_ in rr:
                        readers.setdefault(id(b_), set()).add(i)
                    for b_ in ww:
                        writer[id(b_)] = i
                        readers[id(b_)] = set()
                tot = {e: 0.0 for e in Prog.ENGS}
                for it_ in fifo:
                    tot[it_[1]] += Prog.COST[it_[1]]
                tgt = max(nhalf * 0.55, 1.0)
                caps = {"pe": 1.1, "dve": 1.25, "act": 0.95}
                self.budget = {e: min(2.0 * tot[e] / tgt, caps.get(e, 9.9)) for e in Prog.ENGS}
                self.t = 0

            def empty(self):
                return self.first >= len(self.items)

            def step(self, flush=False):
                used = {e: 0.0 for e in Prog.ENGS}
                n = len(self.items)
                i = self.first
                scanned = 0
                while i < n and scanned < 700:
                    if self.when[i] is None:
                        scanned += 1
                        kind, eng, fn, dsem, r, w = self.items[i]
                        ok = flush or used[eng] == 0.0 or used[eng] + Prog.COST[eng] <= self.budget[eng]
                        if ok:
                            for d in self.deps[i]:
                                wd = self.when[d]
                                if wd is None:
                                    ok = False
                                    break
                                if not flush and self.items[d][1] != eng and wd >= self.t:
                                    ok = False
                                    break
                        if ok:
                            self.when[i] = self.t
                            used[eng] += Prog.COST[eng]
                            if kind == "op":
                                P._add(Op(eng, fn), r, w)
                            else:
                                o = Op(eng, fn)
                                o.dsem = dsem
                                o.needed = True
                                o.inc = 16
                                P._add(o, r, w)
                    i += 1
                while self.first < n and self.when[self.first] is not None:
                    self.first += 1
                self.t += 1

        bcast_tiles(tiles[0][0], ((gt2b, 5),), psA[0:2], B_psA[0:2])
        SPS = 4
        sched = None
        for p in range(NJ + 1):
            ju = p if p < NJ else None
            jv = p - 1 if p >= 1 else None
            if p % NS == 0:
                tnext = p // NS + 1
                assert sched is None or sched.empty()
                sched = None
                if tnext < NT:
                    fifo = []
                    P.defer = fifo
                    for (c_, fn) in M_units(tnext):
                        fn()
                    P.defer = None
                    sched = MSched(fifo, NS * 128 * SPS)
            if jv is not None:
                V_start(jv)
            for slot in range(128):
                if ju is not None:
                    U_slot(ju, slot)
                if sched is not None:
                    for _ in range(SPS // 2):
                        sched.step()
                if jv is not None:
                    V_slot(jv, slot)
                if sched is not None:
                    for _ in range(SPS // 2):
                        sched.step()
            if sched is not None and p % NS == NS - 1:
                while not sched.empty():
                    sched.step(flush=True)
            if ju is not None:
                U_finish(ju)
            if jv is not None:
                V_finish(jv)

        lasts = {}
        for en in ("sp", "act"):
            for o in P.ops[en]:
                if o.dsem is not None and (o.dsem is d_y or o.dsem in d_st):
                    lasts[id(o.dsem)] = o
        fin = Op("sp", lambda e: e.nop())
        fin.deps = list(lasts.values())
        P.ops["sp"].append(fin)

        with nc.Block() as block2:
            P.resolve(engsem)

            @block2.sync
            def _(e):
                P.emit("sp", e)

            @block2.scalar
            def _(e):
                P.emit("act", e)

            @block2.vector
            def _(e):
                P.emit("dve", e)

            @block2.tensor
            def _(e):
                P.emit("pe", e)

            @block2.gpsimd
            def _(e):
                P.emit("pool", e)
    return nc


def _block_order():
    order = []
    for c in range(8):
        order += [c, 8 + c]
    for c in range(8):
        order += [32 + c, 24 + c, 16 + c]
    return order


def prep_shared(w_ada, b_ada, g_norm1, w_in, conv_w, conv_b, ln_g, ln_b, short_w, w_out, g_norm2, w_query,
                sub_keys, expert_u, expert_v, g_final):
    f = np.float32
    sh = {}
    wa = np.asarray(w_ada[0], dtype=f)
    sh["wada_blocks"] = np.ascontiguousarray(wa.reshape(KC, 128, 96, 128).transpose(2, 1, 0, 3))
    sh["badaT"] = np.ascontiguousarray(np.asarray(b_ada[0], dtype=f).reshape(96, 128).T)
    sh["g1T"] = np.ascontiguousarray(np.asarray(g_norm1[0], dtype=f).reshape(KC, 128).T)
    sh["g2T"] = np.ascontiguousarray(np.asarray(g_norm2[0], dtype=f).reshape(KC, 128).T)
    sh["gfin_b"] = np.ascontiguousarray(np.broadcast_to(np.asarray(g_final)[None, :], (128, D)), dtype=f)
    wi = np.asarray(w_in[0], dtype=f)
    wi_blocks = wi.reshape(KC, 128, 40, 128).transpose(2, 1, 0, 3)
    wi_blocks = wi_blocks[_block_order()]
    wo_blocks = np.asarray(w_out[0], dtype=f).reshape(KC, 128, 16, 128).transpose(2, 1, 0, 3)
    wq_blocks = np.asarray(w_query[0], dtype=f).reshape(KC, 128, 16, 128).transpose(2, 1, 0, 3)
    sh["w_blocks"] = np.ascontiguousarray(np.concatenate([wi_blocks, wo_blocks, wq_blocks], axis=0))
    sh["convw"] = np.ascontiguousarray(np.asarray(conv_w[0], dtype=f).reshape(31, 8, 128).transpose(2, 1, 0))
    sh["convb"] = np.ascontiguousarray(np.asarray(conv_b[0], dtype=f).reshape(8, 128).T)
    sh["lng"] = np.ascontiguousarray(np.asarray(ln_g[0], dtype=f).reshape(8, 128).T)
    sh["lnb"] = np.ascontiguousarray(np.asarray(ln_b[0], dtype=f).reshape(8, 128).T)
    sh["shortw"] = np.ascontiguousarray(np.asarray(short_w[0], dtype=f).reshape(3, 8, 128).transpose(2, 1, 0))
    sk = np.asarray(sub_keys[0], dtype=f)
    sh["keysT"] = np.ascontiguousarray(sk.reshape(16, 128, 128).transpose(2, 0, 1))
    sh["expert_u"] = np.ascontiguousarray(expert_u[0], dtype=f)
    sh["expert_v"] = np.ascontiguousarray(expert_v[0], dtype=f)
    sh["ident"] = np.eye(128, dtype=f)
    sh["iota16"] = np.ascontiguousarray(np.broadcast_to(np.arange(16, dtype=f)[None, :], (128, 16)))
    return sh


def prep_core(xp_seq, xs_seq, s0, L1, cp, cs):
    f = np.float32
    L0 = xp_seq.shape[0]
    S = xs_seq.shape[0]
    xin = np.zeros((L0 + L1 + 4 * HALO, D), dtype=f)
    xin[HALO:HALO + L0] = xp_seq
    base = L0 + 2 * HALO
    lo = max(0, s0 - HALO)
    hi = min(S, s0 + L1 + HALO)
    xin[base + (lo - (s0 - HALO)):base + (hi - (s0 - HALO))] = xs_seq[lo:hi]
    emask = np.zeros((128, 4), dtype=f)
    emask[:, 2] = 1.0 if s0 > 0 else 0.0
    emask[:, 3] = 1.0 if s0 + L1 < S else 0.0
    cc = np.stack([np.asarray(cp, dtype=f), np.asarray(cs, dtype=f)], axis=-1)
    cT = np.ascontiguousarray(cc.reshape(KC, 128, 2).transpose(1, 0, 2))
    return {"xin": xin, "emask": emask, "cT": cT}


_NC_CACHE = {}


def kernel(x_prompt, x_sample, c_prompt, c_sample, w_ada, b_ada, g_norm1, w_in, conv_w, conv_b,
           ln_g, ln_b, short_w, w_out, g_norm2, w_query, sub_keys, expert_u, expert_v, g_final):
    x_prompt = np.asarray(x_prompt)
    x_sample = np.asarray(x_sample)
    c_prompt = np.asarray(c_prompt)
    c_sample = np.asarray(c_sample)
    ncores = 8
    B0, L0, _ = x_prompt.shape
    B1, S1, _ = x_sample.shape
    per = ncores // B1
    L1 = S1 // per
    sh = prep_shared(*(np.asarray(a) for a in (w_ada, b_ada, g_norm1, w_in, conv_w, conv_b, ln_g, ln_b, short_w,
                                               w_out, g_norm2, w_query, sub_keys, expert_u, expert_v, g_final)))
    in_maps = []
    for c in range(ncores):
        b1 = c // per
        q = c % per
        m = dict(sh)
        m.update(prep_core(x_prompt[c], x_sample[b1], q * L1, L1, c_prompt[c], c_sample[b1]))
        in_maps.append(m)
    key = (L0, L1)
    if key not in _NC_CACHE:
        _NC_CACHE[key] = build_program(L0, L1)
    nc = _NC_CACHE[key]
    res = run_bass_kernel_spmd(nc, in_maps, core_ids=list(range(ncores)))
    y_prompt = np.empty((B0, L0, D), dtype=np.float32)
    y_sample = np.empty((B1, S1, D), dtype=np.float32)
    for c in range(ncores):
        y = np.asarray(res.results[c]["yout"])
        y_prompt[c] = y[:L0]
        y_sample[c // per, (c % per) * L1:(c % per + 1) * L1] = y[L0:]
    return (y_prompt, y_sample)
```
